# Optimizing a Trainium2 kernel written in Bass

```python
import math
import jax, jax.numpy as jnp
from jax import lax
import numpy as np

D_MODEL = 1024
BATCH = 8
SEQ = 2048
DEPTH = 1
DEC_BATCH = 128
DEC_SEQ = 4
PAST_LEN = 2048
PAGE_SIZE = 128

D_MIX = D_MODEL
H_ATT = 4
ATT_DQ = 64
ATT_KDIM = 2 * ATT_DQ
ATT_DV = D_MIX // 2 // H_ATT
H_ML = 4
ML_DH = D_MIX // 2 // H_ML
FFN_DIM = 2816
NUM_BUCKETS = 32
MAX_DISTANCE = 128
Q_BLOCK = 128
MLSTM_CHUNK = 64
EPS = 1e-6
D_IN = H_ATT * (2 * ATT_KDIM + ATT_DV) + 4 * H_ML * ML_DH + 2 * H_ML

kernel_name = 'hymba_diffattn_mlstm_macaron_step'


def rmsnorm(x, g):
    xf = x.astype(jnp.float32)
    y = xf * lax.rsqrt(jnp.mean(xf * xf, axis=-1, keepdims=True) + EPS)
    return (y * g.astype(jnp.float32)).astype(x.dtype)


def half_ffn(x, g_pre, g_post, w_gate, w_up, w_down):
    h = rmsnorm(x, g_pre)
    u = jax.nn.silu(h @ w_gate) * (h @ w_up)
    return x + 0.5 * rmsnorm(u @ w_down, g_post)


def t5_bucket(rel):
    n = jnp.maximum(rel, 0)
    max_exact = NUM_BUCKETS // 2
    large = max_exact + (jnp.log(jnp.maximum(n, max_exact).astype(jnp.float32) / max_exact)
                         / math.log(MAX_DISTANCE / max_exact) * (NUM_BUCKETS - max_exact)).astype(jnp.int32)
    large = jnp.minimum(large, NUM_BUCKETS - 1)
    return jnp.where(n < max_exact, n, large)


def diff_attention(q, k, v, q_pos, k_pos, rel_bias, lam):
    logits = jnp.einsum('bqhcd,bkhcd->bchqk', q, k, preferred_element_type=jnp.float32) * (ATT_DQ ** -0.5)
    bias = rel_bias[t5_bucket(q_pos[:, None] - k_pos[None, :])].astype(jnp.float32)
    bias = jnp.transpose(bias, (2, 0, 1))
    mask = k_pos[None, :] <= q_pos[:, None]
    logits = jnp.where(mask, logits + bias, -jnp.inf)
    p = jax.nn.softmax(logits, axis=-1)
    a = p[:, 0] - lam * p[:, 1]
    return jnp.einsum('bhqk,bkhd->bqhd', a.astype(v.dtype), v)


def mlstm_chunkwise(q, k, v, ig, fg, C0, n0, m0):
    B, S, H, DK = q.shape
    DV = v.shape[-1]
    L = math.gcd(S, MLSTM_CHUNK)
    nc = S // L
    f32 = jnp.float32

    def chunks(t):
        return t.astype(f32).reshape(B, nc, L, H, -1).transpose(1, 0, 3, 2, 4)

    qc = chunks(q)
    kc = chunks(k) * (DK ** -0.5)
    vc = chunks(v)
    ic = chunks(ig[..., None])[..., 0]
    lfc = jax.nn.log_sigmoid(chunks(fg[..., None])[..., 0])
    causal = jnp.tril(jnp.ones((L, L), dtype=bool))

    def step(carry, xs):
        C, n, m = carry
        qt, kt, vt, it, lf = xs
        b = jnp.cumsum(lf, axis=-1)
        log_d = jnp.where(causal, b[..., :, None] - b[..., None, :] + it[..., None, :], -jnp.inf)
        inter = b + m[..., None]
        m_t = jnp.maximum(inter, jnp.max(log_d, axis=-1))
        d = jnp.exp(log_d - m_t[..., None])
        w_inter = jnp.exp(inter - m_t)
        s = jnp.einsum('bhtd,bhsd->bhts', qt, kt) * d
        num = jnp.einsum('bhts,bhsd->bhtd', s, vt) + w_inter[..., None] * jnp.einsum('bhtd,bhde->bhte', qt, C)
        den = jnp.sum(s, axis=-1) + w_inter * jnp.einsum('bhtd,bhd->bht', qt, n)
        h = num / jnp.maximum(jnp.abs(den), jnp.exp(-m_t))[..., None]
        b_last = b[..., -1]
        log_w = b_last[..., None] - b + it
        m_new = jnp.maximum(b_last + m, jnp.max(log_w, axis=-1))
        ws = jnp.exp(log_w - m_new[..., None])
        fw = jnp.exp(b_last + m - m_new)
        C_new = fw[..., None, None] * C + jnp.einsum('bhs,bhsd,bhse->bhde', ws, kt, vt)
        n_new = fw[..., None] * n + jnp.einsum('bhs,bhsd->bhd', ws, kt)
        return (C_new, n_new, m_new), h

    (C, n, m), h = lax.scan(step, (C0.astype(f32), n0.astype(f32), m0.astype(f32)), (qc, kc, vc, ic, lfc))
    h = h.transpose(1, 0, 3, 2, 4).reshape(B, S, H, DV)
    return h, C, n, m


def token_mix(h, past_k, past_v, C0, n0, m0, rel_bias, lam_init,
              w_in, b_gates, lam_q, lam_k, attn_norm, mlstm_norm, w_out):
    B, S, _ = h.shape
    proj = h @ w_in
    sizes = [H_ATT * ATT_KDIM, H_ATT * ATT_KDIM, H_ATT * ATT_DV,
             H_ML * ML_DH, H_ML * ML_DH, H_ML * ML_DH, H_ML * ML_DH]
    idx = np.cumsum(sizes).tolist()
    q_a, k_a, v_a, q_m, k_m, v_m, o_m, gates = jnp.split(proj, idx, axis=-1)

    q_a = q_a.reshape(B, S, H_ATT, 2, ATT_DQ)
    k_a = k_a.reshape(B, S, H_ATT, 2, ATT_DQ)
    v_a = v_a.reshape(B, S, H_ATT, ATT_DV)
    k_rows = k_a.reshape(B, S, H_ATT, ATT_KDIM)
    if past_k is None:
        P = 0
        k_all, v_all = k_a, v_a
    else:
        P = past_k.shape[1]
        k_all = jnp.concatenate([past_k.reshape(B, P, H_ATT, 2, ATT_DQ).astype(k_a.dtype), k_a], axis=1)
        v_all = jnp.concatenate([past_v.astype(v_a.dtype), v_a], axis=1)
    k_pos = jnp.arange(P + S)
    q_pos = P + jnp.arange(S)
    lam = jnp.exp(jnp.sum(lam_q[0] * lam_k[0])) - jnp.exp(jnp.sum(lam_q[1] * lam_k[1])) + lam_init
    if S > Q_BLOCK and S % Q_BLOCK == 0:
        nb = S // Q_BLOCK
        qb = jnp.swapaxes(q_a.reshape(B, nb, Q_BLOCK, H_ATT, 2, ATT_DQ), 0, 1)
        pb = q_pos.reshape(nb, Q_BLOCK)
        ob = lax.map(lambda a: diff_attention(a[0], k_all, v_all, a[1], k_pos, rel_bias, lam), (qb, pb))
        attn = jnp.swapaxes(ob, 0, 1).reshape(B, S, H_ATT, ATT_DV)
    else:
        attn = diff_attention(q_a, k_all, v_all, q_pos, k_pos, rel_bias, lam)
    attn = (rmsnorm(attn, attn_norm) * (1.0 - lam_init)).reshape(B, S, H_ATT * ATT_DV)

    gates = gates.reshape(B, S, 2, H_ML) + b_gates
    h_m, C, n, m = mlstm_chunkwise(q_m.reshape(B, S, H_ML, ML_DH), k_m.reshape(B, S, H_ML, ML_DH),
                                   v_m.reshape(B, S, H_ML, ML_DH), gates[:, :, 0], gates[:, :, 1], C0, n0, m0)
    h_m = rmsnorm(h_m, mlstm_norm).astype(h.dtype).reshape(B, S, H_ML * ML_DH) * jax.nn.sigmoid(o_m)

    mixed = jnp.concatenate([attn.astype(h.dtype), h_m], axis=-1)
    return mixed @ w_out, k_rows, v_a, C, n, m


def setup_inputs(seed: int = 0) -> dict:
    key = jax.random.key(seed)
    ks = jax.random.split(key, 24)
    n_pages = PAST_LEN // PAGE_SIZE
    n_used = DEC_BATCH * n_pages
    n_phys = n_used + n_used // 4

    def nrm(k, shape, s):
        return jax.random.normal(k, shape, jnp.float32) * s

    x_prompt = nrm(ks[0], (BATCH, SEQ, D_MODEL), 1.0)
    x_sample = nrm(ks[1], (DEC_BATCH, DEC_SEQ, D_MODEL), 1.0)
    cache_k = nrm(ks[2], (DEPTH, n_phys, PAGE_SIZE, H_ATT, ATT_KDIM), 1.0)
    cache_v = nrm(ks[3], (DEPTH, n_phys, PAGE_SIZE, H_ATT, ATT_DV), 1.0)
    state_C = nrm(ks[4], (DEPTH, DEC_BATCH, H_ML, ML_DH, ML_DH), 0.1)
    state_n = nrm(ks[5], (DEPTH, DEC_BATCH, H_ML, ML_DH), 0.1)
    state_m = nrm(ks[6], (DEPTH, DEC_BATCH, H_ML), 1.0)
    page_table = jax.random.permutation(ks[7], n_phys)[:n_used].reshape(DEC_BATCH, n_pages).astype(jnp.int32)
    rel_bias = nrm(ks[8], (NUM_BUCKETS, H_ATT), 0.5)
    norm_gains = 1.0 + nrm(ks[9], (DEPTH, 6, D_MODEL), 0.05)
    ffn_w_gate = nrm(ks[10], (DEPTH, 2, D_MODEL, FFN_DIM), D_MODEL ** -0.5)
    ffn_w_up = nrm(ks[11], (DEPTH, 2, D_MODEL, FFN_DIM), D_MODEL ** -0.5)
    ffn_w_down = nrm(ks[12], (DEPTH, 2, FFN_DIM, D_MODEL), FFN_DIM ** -0.5)
    w_in = nrm(ks[13], (DEPTH, D_MODEL, D_IN), D_MODEL ** -0.5)
    b_i = nrm(ks[14], (DEPTH, 1, H_ML), 0.1)
    b_f = jnp.linspace(3.0, 6.0, H_ML, dtype=jnp.float32)[None, None, :] + nrm(ks[15], (DEPTH, 1, H_ML), 0.1)
    b_gates = jnp.concatenate([b_i, b_f], axis=1)
    lam_q = nrm(ks[16], (DEPTH, 2, ATT_DQ), 0.1)
    lam_k = nrm(ks[17], (DEPTH, 2, ATT_DQ), 0.1)
    attn_norm = 1.0 + nrm(ks[18], (DEPTH, ATT_DV), 0.05)
    mlstm_norm = 1.0 + nrm(ks[19], (DEPTH, ML_DH), 0.05)
    w_out = nrm(ks[20], (DEPTH, D_MIX, D_MODEL), D_MIX ** -0.5)
    return {'x_prompt': x_prompt, 'x_sample': x_sample, 'cache_k': cache_k, 'cache_v': cache_v,
            'state_C': state_C, 'state_n': state_n, 'state_m': state_m, 'page_table': page_table,
            'rel_bias': rel_bias, 'norm_gains': norm_gains, 'ffn_w_gate': ffn_w_gate, 'ffn_w_up': ffn_w_up,
            'ffn_w_down': ffn_w_down, 'w_in': w_in, 'b_gates': b_gates, 'lam_q': lam_q, 'lam_k': lam_k,
            'attn_norm': attn_norm, 'mlstm_norm': mlstm_norm, 'w_out': w_out}


def reference(x_prompt, x_sample, cache_k, cache_v, state_C, state_n, state_m, page_table,
              rel_bias, norm_gains, ffn_w_gate, ffn_w_up, ffn_w_down, w_in, b_gates, lam_q, lam_k,
              attn_norm, mlstm_norm, w_out):
    xp, xs = x_prompt, x_sample
    B, DB = xp.shape[0], xs.shape[0]
    kp_l, vp_l, Cp_l, np_l, mp_l = [], [], [], [], []
    ks_l, vs_l, Cs_l, ns_l, ms_l = [], [], [], [], []
    for l in range(DEPTH):
        g = norm_gains[l]
        lam_init = 0.8 - 0.6 * math.exp(-0.3 * l)
        mix_w = (w_in[l], b_gates[l], lam_q[l], lam_k[l], attn_norm[l], mlstm_norm[l], w_out[l])
        xp = half_ffn(xp, g[0], g[1], ffn_w_gate[l, 0], ffn_w_up[l, 0], ffn_w_down[l, 0])
        xs = half_ffn(xs, g[0], g[1], ffn_w_gate[l, 0], ffn_w_up[l, 0], ffn_w_down[l, 0])
        zC = jnp.zeros((B, H_ML, ML_DH, ML_DH), jnp.float32)
        zn = jnp.zeros((B, H_ML, ML_DH), jnp.float32)
        zm = jnp.zeros((B, H_ML), jnp.float32)
        yp, kp, vp, Cp, npr, mp = token_mix(rmsnorm(xp, g[2]), None, None, zC, zn, zm, rel_bias, lam_init, *mix_w)
        past_k = cache_k[l][page_table].reshape(DB, -1, H_ATT, ATT_KDIM)
        past_v = cache_v[l][page_table].reshape(DB, -1, H_ATT, ATT_DV)
        ys, ksn, vsn, Cs, nsn, msn = token_mix(rmsnorm(xs, g[2]), past_k, past_v, state_C[l], state_n[l], state_m[l],
                                               rel_bias, lam_init, *mix_w)
        xp = xp + rmsnorm(yp, g[3])
        xs = xs + rmsnorm(ys, g[3])
        xp = half_ffn(xp, g[4], g[5], ffn_w_gate[l, 1], ffn_w_up[l, 1], ffn_w_down[l, 1])
        xs = half_ffn(xs, g[4], g[5], ffn_w_gate[l, 1], ffn_w_up[l, 1], ffn_w_down[l, 1])
        kp_l.append(kp); vp_l.append(vp); Cp_l.append(Cp); np_l.append(npr); mp_l.append(mp)
        ks_l.append(ksn); vs_l.append(vsn); Cs_l.append(Cs); ns_l.append(nsn); ms_l.append(msn)
    return (xp, xs,
            jnp.stack(kp_l), jnp.stack(vp_l), jnp.stack(Cp_l), jnp.stack(np_l), jnp.stack(mp_l),
            jnp.stack(ks_l), jnp.stack(vs_l), jnp.stack(Cs_l), jnp.stack(ns_l), jnp.stack(ms_l))
```

```python
import numpy as np
from contextlib import ExitStack
import concourse.bass as bass
import concourse.mybir as mybir
from concourse.bass_utils import run_bass_kernel_spmd

F32 = mybir.dt.float32
BF16 = mybir.dt.bfloat16
I32 = mybir.dt.int32
ALU = mybir.AluOpType
AF = mybir.ActivationFunctionType
AX = mybir.AxisListType

NCORES = 8
TP, TS = 2048, 64
T = TP + TS
NT = 17
D, FF, KD, KF = 1024, 2816, 8, 22
DIN = 3592
EPS = 1e-6
NDS = 40
LAM_INIT = 0.2


def rows_of(tt):
    return 128 if tt < 16 else 64


def blk_cols(blk):
    return (blk * 512, 512) if blk < 4 else (2048, 64)


class Sched:
    def __init__(self, nc, es):
        self.nc = nc
        self.E = {'pe': nc.tensor, 'act': nc.scalar, 'dve': nc.vector, 'pool': nc.gpsimd, 'sp': nc.sync}
        self.sem = {k: es.enter_context(nc.semaphore(f"sem_{k}")) for k in self.E}
        self.cnt = {k: 0 for k in self.E}
        self.pending = {k: False for k in self.E}
        self.dsem = [es.enter_context(nc.semaphore(f"dsem{i}")) for i in range(NDS)]
        self.dcnt = [0] * NDS
        self.dnext = 0
        self.waited = {}
        self.lastw = {}
        self.readers = {}
        self.out_dmas = []
        self.alias_deps = {}
        for h in list(self.sem.values()) + self.dsem:
            nc.gpsimd.sem_clear(h)
        nc.all_engine_barrier()

    def _semof(self, prod):
        return self.sem[prod] if isinstance(prod, str) else self.dsem[prod[1]]

    def _wait(self, e, prod, val):
        key = (e, prod)
        if self.waited.get(key, 0) >= val:
            return
        self.waited[key] = val
        self.E[e].wait_ge(self._semof(prod), val)

    def _deps(self, r, w):
        deps = {}

        def add(p, v):
            if deps.get(p, 0) < v:
                deps[p] = v
        for b in r:
            lw = self.lastw.get(b)
            if lw is not None:
                add(*lw)
            elif b not in self.readers:
                for p, v in self.alias_deps.items():
                    add(p, v)
        for b in w:
            lw = self.lastw.get(b)
            if lw is not None:
                add(*lw)
            else:
                for p, v in self.alias_deps.items():
                    add(p, v)
            for p, v in self.readers.get(b, {}).items():
                add(p, v)
        return deps

    def freed(self):
        d = {}
        for e in self.E:
            v = self.cnt[e] + (1 if self.pending[e] else 0)
            if v > 0:
                d[e] = v
        for j in range(NDS):
            if self.dcnt[j] > 0:
                d[('d', j)] = self.dcnt[j]
        self.alias_deps = d
        self.lastw = {}
        self.readers = {}

    def _record(self, me, r, w):
        for b in w:
            self.lastw[b] = me
            self.readers[b] = {}
        for b in r:
            d = self.readers.setdefault(b, {})
            if d.get(me[0], 0) < me[1]:
                d[me[0]] = me[1]

    def op(self, e, fn, r=(), w=(), inc=True):
        deps = self._deps(r, w)
        for p, v in deps.items():
            if p == e and e == 'pe':
                continue
            self._wait(e, p, v)
        ins = fn(self.E[e])
        if inc:
            self.cnt[e] += 1
            ins.then_inc(self.sem[e], 1)
            me = (e, self.cnt[e])
            self.pending[e] = False
        else:
            me = (e, self.cnt[e] + 1)
            self.pending[e] = True
        self._record(me, r, w)

    def dma(self, q, fn, r=(), w=(), is_out=False):
        deps = self._deps(r, w)
        j = self.dnext
        self.dnext = (self.dnext + 1) % NDS
        if self.dcnt[j] > 0:
            self._wait(q, ('d', j), self.dcnt[j])
        for p, v in deps.items():
            self._wait(q, p, v)
        ins = fn(self.E[q])
        self.dcnt[j] += 16
        ins.then_inc(self.dsem[j], 16)
        me = (('d', j), self.dcnt[j])
        self._record(me, r, w)
        if is_out:
            self.out_dmas.append(me)

    def finish(self):
        for j in range(NDS):
            if self.dcnt[j] > 0:
                self._wait('sp', ('d', j), self.dcnt[j])
        for e in ('pe', 'act', 'dve', 'pool'):
            assert not self.pending[e], e
            if self.cnt[e] > 0:
                self._wait('sp', e, self.cnt[e])
        self.nc.all_engine_barrier()
        for h in list(self.sem.values()) + self.dsem:
            self.nc.gpsimd.sem_clear(h)
        self.nc.all_engine_barrier()


DEBUG = set()
SKIP = set()


def build_nc(stage=99):
    nc = bass.Bass("TRN2", target_bir_lowering=False)

    def din(name, shape, dt=F32):
        return nc.dram_tensor(name, list(shape), dt, kind="ExternalInput").ap()

    def dout(name, shape, dt=F32):
        return nc.dram_tensor(name, list(shape), dt, kind="ExternalOutput").ap()

    def dscr(name, shape, dt=F32):
        return nc.dram_tensor(name, list(shape), dt, kind="Internal").ap()

    xin = din("xin", [T, D])
    gains = din("gains", [6, D])
    wgate = din("wgate", [2, D, FF])
    wup = din("wup", [2, D, FF])
    wdown = din("wdown", [2, FF, D])
    identb_d = din("identb", [128, 128], BF16)

    w_in = din("w_in", [D, DIN])
    rel_bias = din("rel_bias", [32, 4])
    b_gates = din("b_gates", [2, 4])
    lam_q = din("lam_q", [1, 128])
    lam_k = din("lam_k", [1, 128])
    attn_norm = din("attn_norm", [1, 128])
    mlstm_norm = din("mlstm_norm", [1, 128])
    w_out = din("w_out", [D, D])
    onehot_d = din("onehot", [33, 384])
    identf_d = din("identf", [128, 128])
    tri_d = din("tri", [128, 128])
    sel_d = din("sel", [4, 4, 128])
    trim_d = din("trim", [128, 128])
    sm_in = din("sm", [16, 4])
    ck_in = din("cache_k", [2560 * 128, 512])
    cv_in = din("cache_v", [2560 * 128, 512])
    pt_in = din("pt", [1, 256], I32)
    rmod_d = din("rmod", [4, 64])
    sel01_d = din("sel01", [32, 32])
    bm16_d = din("bm16", [16, 512])
    sC_in = din("sC", [64, 128, 128])
    sn_in = din("sn", [64, 128])
    bmask_d = din("bmask", [64, 64])
    rowsel_d = din("rowsel", [64, 16])
    cs_out = dout("cs_out", [64, 128, 128])
    ns_out = dout("ns_out", [64, 128])
    ms_out = dout("ms_out", [16, 4])
    cp_out = dout("cp_out", [4, 128, 128])
    np_out = dout("np_out", [4, 128])
    mp_out = dout("mp_out", [1, 4])
    vec_h = nc.dram_tensor("vec_s", [4, 384], F32, kind="Internal")
    vec_s = vec_h.ap()
    y_out = dout("y_out", [T, D])
    k_out = dout("k_out", [T, 512])
    v_out = dout("v_out", [T, 512])
    x1_s = dscr("x1_s", [T, D])
    x2_s = dout("x2_s", [T, D]) if "x2" in DEBUG else dscr("x2_s", [T, D])

    with ExitStack() as es:
        S = Sched(nc, es)

        def sb(name, shape, dt):
            return es.enter_context(nc.sbuf_tensor(name, list(shape), dt))

        psum = es.enter_context(nc.psum_tensor("psum", [128, 8, 512], F32))

        def dbg(name, ap, shape, dt, rbufs):
            if name not in DEBUG:
                return
            o = nc.dram_tensor("dbg_" + name, list(shape), dt, kind="ExternalOutput").ap()
            S.dma('sp', lambda e: e.dma_start(out=o, in_=ap), r=rbufs, w=[('dbg', name)])

        def psb(b, n=512):
            return psum[:, b, 0:n]

        def psb16(b):
            return psum[:, b, :].bitcast(BF16)

        identb = sb("identb_sb", [128, 128], BF16)
        S.dma('sp', lambda e: e.dma_start(out=identb[:], in_=identb_d[:, :]), w=['identb'])
        gbc = sb("gbc", [128, 2, D], F32)

        def load_gains(gis):
            for j, gi in enumerate(gis):
                S.dma('sp', lambda e: e.dma_start(out=gbc[:, j, :], in_=gains[gi:gi + 1, :].partition_broadcast(128)),
                      w=[('gbc', j)])
        ss = sb("ss", [128, 64], F32)
        rstd = sb("rstd", [128, 64], F32)
        junk = sb("junk", [128, D], F32)

        def sumsq(out, in_, rows, width, rbufs, wbuf):
            S.op('dve', lambda e: e.tensor_tensor(out=junk[:rows, 0:width], in0=in_, in1=in_, op=ALU.mult),
                 r=rbufs, w=['junk'])
            S.op('dve', lambda e: e.tensor_reduce(out=out, in_=junk[:rows, 0:width], axis=AX.X, op=ALU.add),
                 r=['junk'], w=[wbuf])

        def rstd_act(out, in_, n, rbufs, wbuf, post=1.0):
            S.op('act', lambda e: e.activation(out=out, in_=in_, func=AF.Ln, bias=EPS, scale=1.0 / n),
                 r=rbufs, w=[wbuf])
            S.op('act', lambda e: e.activation(out=out, in_=out, func=AF.Exp, scale=-0.5),
                 r=[wbuf], w=[wbuf])
            if post != 1.0:
                S.op('act', lambda e: e.mul(out=out, in_=out, mul=post), r=[wbuf], w=[wbuf])

        def norm_transpose(src, gi, hT, tag, xt, hn):
            for tt in range(NT):
                rows = rows_of(tt)
                sl = tt % 2
                r0 = tt * 128
                S.dma('sp', lambda e: e.dma_start(out=xt[sl][:rows, :], in_=src[r0:r0 + rows, :]), r=[('dram', id(src), tt)], w=[('xt', sl)])
                sumsq(ss[:rows, tt:tt + 1], xt[sl][:rows, :], rows, D, [('xt', sl)], ('ssc', tt))
                rstd_act(rstd[:rows, tt:tt + 1], ss[:rows, tt:tt + 1], D, [('ssc', tt)], ('rstd', tt))
                S.op('dve', lambda e: e.scalar_tensor_tensor(out=hn[sl][:rows, :], in0=xt[sl][:rows, :],
                                                             scalar=rstd[:rows, tt:tt + 1], in1=gbc[:rows, gi, :],
                                                             op0=ALU.mult, op1=ALU.mult),
                     r=[('xt', sl), ('rstd', tt), ('gbc', gi)], w=[('hn', sl)])
                bank = sl
                pv = psb16(bank)
                for c in range(KD):
                    S.op('pe', lambda e, c=c: e.transpose(out=pv[:, c * 128:c * 128 + rows],
                                                          in_=hn[sl][:rows, c * 128:(c + 1) * 128],
                                                          identity=identb[:rows, :rows]),
                         r=[('hn', sl), 'identb'], w=[('ps', bank)], inc=(c == KD - 1))
                srcv = pv[:, 0:1024].rearrange("p (c r) -> p c r", r=128)[:, :, 0:rows]
                S.op('act', lambda e: e.copy(out=hT[:, :, r0:r0 + rows], in_=srcv),
                     r=[('ps', bank)], w=[('hT', tt)])

        def ffn(src, dst, li, g_pre, g_post, dst_is_out):
            load_gains([g_pre, g_post])
            g_pre, g_post = 0, 1
            uT = es2.enter_context(nc.sbuf_tensor(f"uT{li}", [128, KF, T], BF16))
            with ExitStack() as esh:
                hT = esh.enter_context(nc.sbuf_tensor(f"hT{li}", [128, KD, T], BF16))
                wd = esh.enter_context(nc.sbuf_tensor(f"wd{li}", [128, KF, D], BF16))
                with ExitStack() as est:
                    xt = [est.enter_context(nc.sbuf_tensor(f"xt{li}_{i}", [128, D], F32)) for i in range(2)]
                    hn = [est.enter_context(nc.sbuf_tensor(f"hn{li}_{i}", [128, D], BF16)) for i in range(2)]
                    norm_transpose(src, g_pre, hT, ('ffn', li), xt, hn)
                S.freed()
                dbg('ss', ss[:, :], [128, 64], F32, [('ssc', tt) for tt in range(NT)])
                dbg('rstd', rstd[:, :], [128, 64], F32, [('rstd', tt) for tt in range(NT)])
                dbg('hT', hT[:, :, :], [128, KD, T], BF16, [('hT', tt) for tt in range(NT)])
                wgs = [esh.enter_context(nc.sbuf_tensor(f"wg{li}_{i}", [128, KD, 128], BF16)) for i in range(2)]
                wus = [esh.enter_context(nc.sbuf_tensor(f"wu{li}_{i}", [128, KD, 128], BF16)) for i in range(2)]
                sg = [esh.enter_context(nc.sbuf_tensor(f"sg{li}_{i}", [128, 512], F32)) for i in range(2)]
                wg_d = wgate[li].rearrange("(kc p) f -> p kc f", p=128)
                wu_d = wup[li].rearrange("(kc p) f -> p kc f", p=128)
                wd_d = wdown[li].rearrange("(fc p) d -> p fc d", p=128)
                pbi = 0
                for g in range(KF):
                    sl = g % 2
                    S.dma('pool', lambda e: e.dma_start(out=wgs[sl][:], in_=wg_d[:, :, g * 128:(g + 1) * 128]),
                          w=[('wg', sl)])
                    S.dma('pool', lambda e: e.dma_start(out=wus[sl][:], in_=wu_d[:, :, g * 128:(g + 1) * 128]),
                          w=[('wu', sl)])
                    S.dma('pool', lambda e: e.dma_start(out=wd[:, g:g + 1, :], in_=wd_d[:, g:g + 1, :]),
                          w=[('wd', g)])
                    for fl in range(1):
                        fc = g
                        for blk in range(5):
                            c0, n = blk_cols(blk)
                            pb = pbi % 2
                            pbi += 1
                            hbufs = [('hT', tt) for tt in (range(4 * blk, 4 * blk + 4) if blk < 4 else [16])]
                            for kc in range(KD):
                                S.op('pe', lambda e, kc=kc: e.matmul(psb(2 * pb, n), lhsT=wgs[sl][:, kc, fl * 128:(fl + 1) * 128],
                                                                     rhs=hT[:, kc, c0:c0 + n], start=(kc == 0), stop=(kc == KD - 1)),
                                     r=[('wg', sl)] + hbufs, w=[('ps', 2 * pb)], inc=(kc == KD - 1))
                            for kc in range(KD):
                                S.op('pe', lambda e, kc=kc: e.matmul(psb(2 * pb + 1, n), lhsT=wus[sl][:, kc, fl * 128:(fl + 1) * 128],
                                                                     rhs=hT[:, kc, c0:c0 + n], start=(kc == 0), stop=(kc == KD - 1)),
                                     r=[('wu', sl)] + hbufs, w=[('ps', 2 * pb + 1)], inc=(kc == KD - 1))
                            S.op('act', lambda e: e.activation(out=sg[pb][:, 0:n], in_=psb(2 * pb, n), func=AF.Silu),
                                 r=[('ps', 2 * pb)], w=[('sg', pb)])
                            S.op('dve', lambda e: e.tensor_tensor(out=uT[:, fc, c0:c0 + n], in0=sg[pb][:, 0:n],
                                                                  in1=psb(2 * pb + 1, n), op=ALU.mult),
                                 r=[('sg', pb), ('ps', 2 * pb + 1)], w=[('uT', fc, blk)])
                dbg('uT', uT[:, :, :], [128, KF, T], BF16, [('uT', fc, blk) for fc in range(KF) for blk in range(5)])
                xr = [esh.enter_context(nc.sbuf_tensor(f"xr{li}_{i}", [128, D], F32)) for i in range(2)]
                tmp1 = esh.enter_context(nc.sbuf_tensor(f"tmp{li}", [128, D], F32))
                for tt in range(NT):
                    rows = rows_of(tt)
                    r0 = tt * 128
                    yb = tt % 2
                    blk = tt // 4 if tt < 16 else 4
                    S.dma('sp', lambda e: e.dma_start(out=xr[yb][:rows, :], in_=src[r0:r0 + rows, :]), r=[('dram', id(src), tt)], w=[('xr', yb)])
                    for half in range(2):
                        bank = 4 + 2 * yb + half
                        for fc in range(KF):
                            S.op('pe', lambda e, fc=fc: e.matmul(psum[:rows, bank, :], lhsT=uT[:, fc, r0:r0 + rows],
                                                                 rhs=wd[:, fc, half * 512:(half + 1) * 512],
                                                                 start=(fc == 0), stop=(fc == KF - 1)),
                                 r=[('uT', fc, blk), ('wd', fc)], w=[('ps', bank)], inc=(fc == KF - 1))
                    b0 = 4 + 2 * yb
                    S.op('act', lambda e: e.copy(out=tmp1[:rows, :].rearrange("p (a b) -> p a b", a=2),
                                                 in_=psum[:rows, b0:b0 + 2, :]),
                         r=[('ps', b0), ('ps', b0 + 1)], w=['tmp1'])
                    sumsq(ss[:rows, tt:tt + 1], tmp1[:rows, :], rows, D, ['tmp1'], ('ssc', tt))
                    rstd_act(rstd[:rows, tt:tt + 1], ss[:rows, tt:tt + 1], D, [('ssc', tt)], ('rstd', tt))
                    S.op('dve', lambda e: e.scalar_tensor_tensor(out=tmp1[:rows, :], in0=tmp1[:rows, :],
                                                                 scalar=rstd[:rows, tt:tt + 1], in1=gbc[:rows, g_post, :],
                                                                 op0=ALU.mult, op1=ALU.mult),
                         r=['tmp1', ('rstd', tt), ('gbc', g_post)], w=['tmp1'])
                    S.op('dve', lambda e: e.scalar_tensor_tensor(out=xr[yb][:rows, :], in0=tmp1[:rows, :], scalar=0.5,
                                                                 in1=xr[yb][:rows, :], op0=ALU.mult, op1=ALU.add),
                         r=['tmp1', ('xr', yb)], w=[('xr', yb)])
                    S.dma('sp', lambda e: e.dma_start(out=dst[r0:r0 + rows, :], in_=xr[yb][:rows, :]),
                          r=[('xr', yb)], w=[('dram', id(dst), tt)], is_out=dst_is_out)
            S.freed()

        with ExitStack() as es2:
            ffn(xin, x1_s if stage > 1 else y_out, 0, 0, 1, stage <= 1)
        S.freed()

        def token_mix():
            with ExitStack() as em:
                hnT = em.enter_context(nc.sbuf_tensor("hnT", [128, KD, T], BF16))
                load_gains([2, 3])
                with ExitStack() as est:
                    xt = [est.enter_context(nc.sbuf_tensor(f"xtm_{i}", [128, D], F32)) for i in range(2)]
                    hn = [est.enter_context(nc.sbuf_tensor(f"hnm_{i}", [128, D], BF16)) for i in range(2)]
                    norm_transpose(x1_s, 0, hnT, 'mix', xt, hn)
                S.freed()
                win_d = w_in.rearrange("(kc p) f -> p kc f", p=128)
                wt = [em.enter_context(nc.sbuf_tensor(f"wt_{i}", [128, KD, 512], BF16)) for i in range(2)]
                stg = [em.enter_context(nc.sbuf_tensor(f"stg_{i}", [128, 512], F32)) for i in range(2)]
                hbuf_all = [('hT', tt) for tt in range(NT)]
                for gi_, (col0, dst) in enumerate([(512, k_out), (1024, v_out)]):
                    sl = gi_ % 2
                    S.dma('pool', lambda e: e.dma_start(out=wt[sl][:], in_=win_d[:, :, col0:col0 + 512]),
                          w=[('wt', sl, j) for j in range(4)])
                    for tt in range(NT):
                        rows = rows_of(tt)
                        r0 = tt * 128
                        bank = tt % 2
                        for kc in range(KD):
                            S.op('pe', lambda e, kc=kc: e.matmul(psum[:rows, bank, :], lhsT=hnT[:, kc, r0:r0 + rows],
                                                                 rhs=wt[sl][:, kc, :], start=(kc == 0), stop=(kc == KD - 1)),
                                 r=[('wt', sl, j) for j in range(4)] + [('hT', tt)], w=[('ps', bank)], inc=(kc == KD - 1))
                        S.op('act', lambda e: e.copy(out=stg[bank][:rows, :], in_=psum[:rows, bank, :]),
                             r=[('ps', bank)], w=[('stg', bank)])
                        S.dma('sp', lambda e: e.dma_start(out=dst[r0:r0 + rows, :], in_=stg[bank][:rows, :]),
                              r=[('stg', bank)], w=[('dram', id(dst), tt)], is_out=True)
                if stage > 2:
                    mix_body(em, hnT, win_d, wt, stg)
            S.freed()

        def mix_body(em, hnT, win_d, wt, stg):
            def sbm(name, shape, dt):
                return em.enter_context(nc.sbuf_tensor(name, list(shape), dt))
            mixed = sbm("mixed", [128, NT, D], BF16)
            lsm = sbm("lsm", [128, 8], F32)
            c31 = sbm("c31", [128, 8], F32)
            tz = sbm("tz", [128, 4, 256], F32)
            anb = sbm("anb", [128, 128], F32)
            mnb = sbm("mnb", [128, 128], F32)
            qblk = sbm("qblk", [128, 16, 4, 8], BF16)
            ks_s = sbm("ks_s", [128, 4, 64], BF16)
            vs_s = sbm("vs_s", [64, 4, 128], BF16)
            S.op('dve', lambda e: e.memset(qblk[:], 0.0), w=['qblk'])
            etz = ExitStack()

            def sbt(name, shape, dt):
                return etz.enter_context(nc.sbuf_tensor(name, list(shape), dt))
            lq = sbt("lq", [128, 128], F32)
            lk = sbt("lk", [128, 128], F32)
            S.dma('sp', lambda e: e.dma_start(out=lq[:], in_=lam_q[0:1, :].partition_broadcast(128)), w=['lq'])
            S.dma('sp', lambda e: e.dma_start(out=lk[:], in_=lam_k[0:1, :].partition_broadcast(128)), w=['lk'])
            S.op('dve', lambda e: e.tensor_tensor(out=lq[:], in0=lq[:], in1=lk[:], op=ALU.mult), r=['lq', 'lk'], w=['lq'])
            S.op('dve', lambda e: e.tensor_reduce(out=lsm[:, 0:2], in_=lq[:].rearrange("p (a b) -> p a b", a=2),
                                                  axis=AX.X, op=ALU.add), r=['lq'], w=['lsm'])
            S.op('act', lambda e: e.activation(out=lsm[:, 2:4], in_=lsm[:, 0:2], func=AF.Exp), r=['lsm'], w=['lsm'])
            S.op('dve', lambda e: e.tensor_tensor(out=lsm[:, 4:5], in0=lsm[:, 2:3], in1=lsm[:, 3:4], op=ALU.subtract),
                 r=['lsm'], w=['lsm'])
            S.op('dve', lambda e: e.tensor_scalar(out=lsm[:, 5:6], in0=lsm[:, 4:5], scalar1=LAM_INIT, scalar2=-1.0,
                                                  op0=ALU.add, op1=ALU.mult), r=['lsm'], w=['lsm'])
            nlam = lsm[:, 5:6]
            rbx = sbt("rbx", [33, 4], F32)
            oneh = sbt("oneh", [33, 384], F32)
            vecs = sbt("vecs", [4, 384], F32)
            S.op('dve', lambda e: e.memset(rbx[32:33, :], -30000.0), w=['rbx32'])
            S.dma('sp', lambda e: e.dma_start(out=rbx[0:32, :], in_=rel_bias[:, :]), w=['rbx'])
            S.dma('sp', lambda e: e.dma_start(out=oneh[:], in_=onehot_d[:, :]), w=['oneh'])
            S.dma('sp', lambda e: e.dma_start(out=c31[:, 0:4], in_=rel_bias[31:32, :].partition_broadcast(128)), w=['c31'])
            S.dma('sp', lambda e: e.dma_start(out=anb[:], in_=attn_norm[0:1, :].partition_broadcast(128)), w=['anb'])
            S.dma('sp', lambda e: e.dma_start(out=mnb[:], in_=mlstm_norm[0:1, :].partition_broadcast(128)), w=['mnb'])
            S.op('dve', lambda e: e.tensor_scalar(out=c31[:, 4:8], in0=c31[:, 0:4], scalar1=-1.0, scalar2=None, op0=ALU.mult),
                 r=['c31'], w=['c31n'])
            S.op('pe', lambda e: e.matmul(psum[0:4, 7, 0:384], lhsT=rbx[:, :], rhs=oneh[:, :], start=True, stop=True),
                 r=['rbx', 'rbx32', 'oneh'], w=[('ps', 7)])
            S.op('act', lambda e: e.copy(out=vecs[:], in_=psum[0:4, 7, 0:384]), r=[('ps', 7)], w=['vecs'])
            S.dma('sp', lambda e: e.dma_start(out=vec_s[:, :], in_=vecs[:]), r=['vecs'], w=['vec_s'])
            tzr = etz.enter_context(nc.sbuf_tensor("tzr", [128, 4, 256], F32))
            antiI = etz.enter_context(nc.sbuf_tensor("antiI", [128, 128], F32))
            S.dma('sp', lambda e: e.dma_start(out=antiI[:], in_=tri_d[:, :]), w=['antiI'])
            tz_src = bass.AP(vec_h, 0, [[1, 128], [384, 4], [1, 256]])
            S.dma('sp', lambda e: e.dma_start(out=tzr[:], in_=tz_src), r=['vec_s'], w=['tzr'])
            for half in range(2):
                S.op('pe', lambda e: e.matmul(psum[:, 7, :], lhsT=antiI[:, :],
                                              rhs=tzr[:, 2 * half:2 * half + 2, :].rearrange("p a b -> p (a b)"),
                                              start=True, stop=True), r=['antiI', 'tzr'], w=[('ps', 7)])
                for hh in range(2):
                    h = 2 * half + hh
                    S.op('act', lambda e: e.activation(out=tz[:, h, :], in_=psum[:, 7, hh * 256:(hh + 1) * 256], func=AF.Exp,
                                                       bias=c31[:, 4 + h:5 + h]),
                         r=[('ps', 7), 'c31n'], w=['tz'])
            etz.close()
            S.freed()
            dbg('tz', tz[:, :, :], [128, 4, 256], F32, ['tz'])
            eh = ExitStack()

            def sbm(name, shape, dt):
                return eh.enter_context(nc.sbuf_tensor(name, list(shape), dt))
            dbg('lsm', lsm[:, :], [128, 8], F32, ['lsm'])

            qk = [[sbm(f"qk_{i}_{j}", [128, T], BF16) for j in range(2)] for i in range(2)]
            vt = [sbm(f"vt_{i}", [128, NT, 129], BF16) for i in range(2)]
            pT = [[sbm(f"pT_{i}_{c}", [128, 512], BF16) for c in range(2)] for i in range(2)]
            fin = sbm("fin", [128, 128], F32)
            fsm = sbm("fsm", [128, 8], F32)
            for i in range(2):
                S.op('pool', lambda e: e.memset(vt[i][:], 1.0), w=[('vt', i)])

            def proj_fm(dst, dname, wtile, wname, c0):
                for blk in range(5):
                    cc, n = blk_cols(blk)
                    hb = [('hT', tt) for tt in (range(4 * blk, 4 * blk + 4) if blk < 4 else [16])]
                    for kc in range(KD):
                        S.op('pe', lambda e, kc=kc: e.matmul(psum[:, 7, 0:n], lhsT=wtile[:, kc, c0:c0 + 128],
                                                             rhs=hnT[:, kc, cc:cc + n], start=(kc == 0), stop=(kc == KD - 1)),
                             r=[wname] + hb, w=[('ps', 7)], inc=(kc == KD - 1))
                    S.op('act', lambda e: e.copy(out=dst[:, cc:cc + n], in_=psum[:, 7, 0:n]), r=[('ps', 7)], w=[(dname, blk)])

            def proj_tm(consume, wtile, wname, c0, ncols):
                for tt in range(NT):
                    rows = rows_of(tt)
                    r0 = tt * 128
                    for kc in range(KD):
                        S.op('pe', lambda e, kc=kc: e.matmul(psum[:rows, 7, 0:ncols], lhsT=hnT[:, kc, r0:r0 + rows],
                                                             rhs=wtile[:, kc, c0:c0 + ncols], start=(kc == 0), stop=(kc == KD - 1)),
                             r=[wname, ('hT', tt)], w=[('ps', 7)], inc=(kc == KD - 1))
                    consume(tt, rows, psum[:rows, 7, 0:ncols])

            OPS_BANKS = [0, 1, 6]

            def ops(slot):
                b = OPS_BANKS[slot // 3]
                o = (slot % 3) * 129
                return psum[:, b, o:o + 129], ('ps', b)

            def attn_head(h, it):
                sl = it % 2
                for j, col0 in enumerate([h * 128, 512 + h * 128, 1024 + h * 128]):
                    S.dma('pool', lambda e: e.dma_start(out=wt[sl][:, :, j * 128:(j + 1) * 128], in_=win_d[:, :, col0:col0 + 128]),
                          w=[('wt', sl, j)])
                qT, kT, V = qk[sl][0], qk[sl][1], vt[sl]
                qn, kn, vn = ('qT', sl), ('kT', sl), ('vt', sl)
                proj_fm(qT, qn, wt[sl], ('wt', sl, 0), 0)
                proj_fm(kT, kn, wt[sl], ('wt', sl, 1), 128)

                def cons_v(tt, rows, ps_ap):
                    S.op('act', lambda e: e.copy(out=V[:rows, tt, 0:128], in_=ps_ap), r=[('ps', 7), vn], w=[(vn, tt)])
                proj_tm(cons_v, wt[sl], ('wt', sl, 2), 256, 128)
                S.op('dve', lambda e: e.tensor_copy(out=qblk[0:64, :, h, 0:4], in_=qT[0:64, TP:T].rearrange("p (b j) -> p b j", j=4)),
                     r=[(qn, 4), 'qblk'], w=[('qblk', h, 0)])
                S.op('dve', lambda e: e.tensor_copy(out=qblk[64:128, :, h, 4:8], in_=qT[64:128, TP:T].rearrange("p (b j) -> p b j", j=4)),
                     r=[(qn, 4), 'qblk'], w=[('qblk', h, 1)])
                S.op('dve', lambda e: e.tensor_copy(out=ks_s[:, h, :], in_=kT[:, TP:T]), r=[(kn, 4)], w=[('ks_s', h)])
                S.op('dve', lambda e: e.tensor_copy(out=vs_s[:, h, :], in_=V[0:64, 16, 0:128]), r=[(vn, 16)], w=[('vs_s', h)])
                qbufs = lambda q0, n: [(qn, b) for b in range(q0 // 512, (q0 + n - 1) // 512 + 1)]
                itp = 0
                for qb in range(4):
                    for kt in range(4 * qb + 4):
                        qt0 = max(kt, 4 * qb)
                        ncol = (4 * qb + 4 - qt0) * 128
                        q0 = qt0 * 128
                        pair = itp % 2
                        itp += 1
                        for c in range(2):
                            bank = 2 + 2 * pair + c
                            S.op('pe', lambda e: e.matmul(psum[:, bank, 0:ncol], lhsT=kT[64 * c:64 * c + 64, kt * 128:(kt + 1) * 128],
                                                          rhs=qT[64 * c:64 * c + 64, q0:q0 + ncol], start=True, stop=True),
                                 r=[(kn, kt // 4)] + qbufs(q0, ncol), w=[('ps', bank)])
                        for c in range(2):
                            bank = 2 + 2 * pair + c
                            S.op('act', lambda e: e.activation(out=pT[pair][c][:, 0:ncol], in_=psum[:, bank, 0:ncol], func=AF.Exp,
                                                               bias=c31[:, h:h + 1], scale=0.125),
                                 r=[('ps', bank), 'c31'], w=[('pT', pair, c)])
                            if qt0 == kt:
                                S.op('dve', lambda e: e.tensor_tensor(out=pT[pair][c][:, 0:128], in0=pT[pair][c][:, 0:128],
                                                                      in1=tz[:, h, 0:128], op=ALU.mult),
                                     r=[('pT', pair, c), 'tz'], w=[('pT', pair, c)])
                            if qt0 <= kt + 1 <= 4 * qb + 3:
                                off = (kt + 1 - qt0) * 128
                                S.op('dve', lambda e: e.tensor_tensor(out=pT[pair][c][:, off:off + 128], in0=pT[pair][c][:, off:off + 128],
                                                                      in1=tz[:, h, 128:256], op=ALU.mult),
                                     r=[('pT', pair, c), 'tz'], w=[('pT', pair, c)])
                            for qt in range(qt0, 4 * qb + 4):
                                jq = qt - 4 * qb
                                oap, obuf = ops(c * 4 + jq)
                                off = (qt - qt0) * 128
                                S.op('pe', lambda e: e.matmul(oap, lhsT=pT[pair][c][:, off:off + 128], rhs=V[:, kt, :],
                                                              start=(kt == 0 and (c * 4 + jq) % 3 == 0), stop=(kt == qt),
                                                              skip_group_check=True),
                                     r=[('pT', pair, c), (vn, kt)], w=[obuf])
                    for jq in range(4):
                        tt = 4 * qb + jq
                        o0, b0_ = ops(jq)
                        o1, b1_ = ops(4 + jq)
                        S.op('dve', lambda e: e.reciprocal(out=fsm[:, 0:1], in_=o0[:, 128:129]), r=[b0_], w=['fsm'])
                        S.op('dve', lambda e: e.reciprocal(out=fsm[:, 1:2], in_=o1[:, 128:129]), r=[b1_, 'fsm'], w=['fsm'])
                        S.op('dve', lambda e: e.tensor_tensor(out=fsm[:, 1:2], in0=fsm[:, 1:2], in1=nlam, op=ALU.mult),
                             r=['fsm', 'lsm'], w=['fsm'])
                        S.op('dve', lambda e: e.tensor_scalar(out=fin[:], in0=o0[:, 0:128], scalar1=fsm[:, 0:1], scalar2=None, op0=ALU.mult),
                             r=[b0_, 'fsm'], w=['fin'])
                        S.op('dve', lambda e: e.scalar_tensor_tensor(out=fin[:], in0=o1[:, 0:128], scalar=fsm[:, 1:2], in1=fin[:],
                                                                     op0=ALU.mult, op1=ALU.add),
                             r=[b1_, 'fsm', 'fin'], w=['fin'])
                        sumsq(fsm[:, 2:3], fin[:], 128, 128, ['fin'], 'fsm2')
                        rstd_act(fsm[:, 3:4], fsm[:, 2:3], 128, ['fsm2'], 'fsm3', post=1.0 - LAM_INIT)
                        S.op('dve', lambda e: e.scalar_tensor_tensor(out=mixed[:, tt, h * 128:(h + 1) * 128], in0=fin[:],
                                                                     scalar=fsm[:, 3:4], in1=anb[:], op0=ALU.mult, op1=ALU.mult),
                             r=['fin', 'fsm3', 'anb'], w=[('mixed', tt, h)])

            for h in range(4):
                attn_head(h, h)

            SCL = 128.0 ** -0.5
            atm = sbm("atm", [128, NT, 16], F32)
            nAb = sbm("nAb", [128, T], F32)
            selt = sbm("selt", [4, 4, 128], F32)
            trim = sbm("trim_sb", [128, 128], F32)
            identf = sbm("identf_sb", [128, 128], F32)
            mfin = sbm("mfin", [4, 64], F32)
            S.dma('sp', lambda e: e.dma_start(out=selt[:], in_=sel_d[:, :, :]), w=['selt'])
            S.dma('sp', lambda e: e.dma_start(out=trim[:], in_=trim_d[:, :]), w=['trim'])
            S.dma('sp', lambda e: e.dma_start(out=identf[:], in_=identf_d[:, :]), w=['identf'])
            G4 = sbm("G4", [4, T], F32)
            with ExitStack() as eg:
                G1 = eg.enter_context(nc.sbuf_tensor("G1", [4, T], F32))
                G2 = eg.enter_context(nc.sbuf_tensor("G2", [4, T], F32))
                G3 = eg.enter_context(nc.sbuf_tensor("G3", [4, T], F32))
                wgt = eg.enter_context(nc.sbuf_tensor("wgt", [128, KD, 8], BF16))
                bg = eg.enter_context(nc.sbuf_tensor("bg", [4, 4], F32))
                m0T = eg.enter_context(nc.sbuf_tensor("m0T", [4, 16], F32))
                S.dma('pool', lambda e: e.dma_start(out=wgt[:], in_=win_d[:, :, 3584:3592]), w=['wgt'])
                S.dma('sp', lambda e: e.dma_start(out=bg[:, 0:2], in_=b_gates.rearrange("g h -> h g"), allow_slow_non_contiguous=True), w=['bg'])
                S.dma('sp', lambda e: e.dma_start(out=m0T[:], in_=sm_in.rearrange("b h -> h b"), allow_slow_non_contiguous=True), w=['m0T'])
                S.op('dve', lambda e: e.tensor_scalar(out=bg[:, 2:3], in0=bg[:, 1:2], scalar1=-1.0, scalar2=None, op0=ALU.mult),
                     r=['bg'], w=['bgn'])
                for gidx, G in ((0, G1), (1, G2)):
                    for blk in range(5):
                        cc, n = blk_cols(blk)
                        hb = [('hT', tt) for tt in (range(4 * blk, 4 * blk + 4) if blk < 4 else [16])]
                        for kc in range(KD):
                            S.op('pe', lambda e, kc=kc: e.matmul(psum[0:4, 7, 0:n], lhsT=wgt[:, kc, 4 * gidx:4 * gidx + 4],
                                                                 rhs=hnT[:, kc, cc:cc + n], start=(kc == 0), stop=(kc == KD - 1)),
                                 r=['wgt'] + hb, w=[('ps', 7)], inc=(kc == KD - 1))
                        S.op('act', lambda e: e.copy(out=G[:, cc:cc + n], in_=psum[0:4, 7, 0:n]), r=[('ps', 7)], w=[('G', gidx)])
                S.op('act', lambda e: e.activation(out=G2[:, :], in_=G2[:, :], func=AF.Exp, bias=bg[:, 2:3], scale=-1.0),
                     r=[('G', 1), 'bgn'], w=[('G', 1)])
                S.op('act', lambda e: e.activation(out=G2[:, :], in_=G2[:, :], func=AF.Ln, bias=1.0), r=[('G', 1)], w=[('G', 1)])
                S.op('dve', lambda e: e.tensor_tensor_scan(out=G3[:, 0:TP], data0=G2[:, 0:TP], data1=G2[:, 0:TP], initial=0.0,
                                                           op0=ALU.add, op1=ALU.max), r=[('G', 1)], w=[('G', 2)])
                l3 = G2[:, TP:T].rearrange("p (b j) -> p b j", j=4)
                B3 = G3[:, TP:T].rearrange("p (b j) -> p b j", j=4)
                S.op('dve', lambda e: e.tensor_copy(out=B3[:, :, 0], in_=l3[:, :, 0]), r=[('G', 1)], w=[('G', 2)])
                for j in range(1, 4):
                    S.op('dve', lambda e: e.tensor_tensor(out=B3[:, :, j], in0=B3[:, :, j - 1], in1=l3[:, :, j], op=ALU.add),
                         r=[('G', 1), ('G', 2)], w=[('G', 2)])
                S.op('dve', lambda e: e.scalar_tensor_tensor(out=G1[:, :], in0=G1[:, :], scalar=bg[:, 0:1], in1=G3[:, :],
                                                             op0=ALU.add, op1=ALU.add), r=[('G', 0), ('G', 2), 'bg'], w=[('G', 0)])
                S.op('dve', lambda e: e.tensor_tensor_scan(out=G4[:, 0:TP], data0=G1[:, 0:TP], data1=G1[:, 0:TP], initial=0.0,
                                                           op0=ALU.max, op1=ALU.max), r=[('G', 0)], w=[('G', 3)])
                a3 = G1[:, TP:T].rearrange("p (b j) -> p b j", j=4)
                A3 = G4[:, TP:T].rearrange("p (b j) -> p b j", j=4)
                S.op('dve', lambda e: e.tensor_tensor(out=A3[:, :, 0], in0=a3[:, :, 0], in1=m0T[:, :], op=ALU.max),
                     r=[('G', 0), 'm0T'], w=[('G', 3)])
                for j in range(1, 4):
                    S.op('dve', lambda e: e.tensor_tensor(out=A3[:, :, j], in0=A3[:, :, j - 1], in1=a3[:, :, j], op=ALU.max),
                         r=[('G', 0), ('G', 3)], w=[('G', 3)])
                S.op('dve', lambda e: e.tensor_tensor(out=mfin[:, 0:1], in0=G4[:, TP - 1:TP], in1=G3[:, TP - 1:TP], op=ALU.subtract),
                     r=[('G', 2), ('G', 3)], w=['mfin0'])
                S.op('dve', lambda e: e.tensor_tensor(out=mfin[:, 16:32], in0=A3[:, :, 3], in1=B3[:, :, 3], op=ALU.subtract),
                     r=[('G', 2), ('G', 3)], w=['mfin1'])
                S.op('dve', lambda e: e.tensor_tensor(out=mfin[:, 32:48], in0=m0T[:, :], in1=A3[:, :, 3], op=ALU.subtract),
                     r=['m0T', ('G', 3)], w=['mfin2'])
                S.op('act', lambda e: e.activation(out=mfin[:, 32:48], in_=mfin[:, 32:48], func=AF.Exp), r=['mfin2'], w=['mfin2'])
                S.dma('sp', lambda e: e.dma_start(out=mp_out.rearrange("o h -> h o"), in_=mfin[:, 0:1], allow_slow_non_contiguous=True),
                      r=['mfin0'], w=['mp_out'], is_out=True)
                S.dma('sp', lambda e: e.dma_start(out=ms_out.rearrange("b h -> h b"), in_=mfin[:, 16:32], allow_slow_non_contiguous=True),
                      r=['mfin1'], w=['ms_out'], is_out=True)
                S.op('dve', lambda e: e.tensor_tensor(out=G3[:, :], in0=G3[:, :], in1=G4[:, :], op=ALU.subtract),
                     r=[('G', 2), ('G', 3), 'mfin0', 'mfin1'], w=[('G', 2)])
                S.op('act', lambda e: e.activation(out=G3[:, :], in_=G3[:, :], func=AF.Exp), r=[('G', 2)], w=[('G', 2)])
                w3 = G2[:, TP:T].rearrange("p (b j) -> p b j", j=4)
                for j in range(4):
                    S.op('dve', lambda e: e.tensor_tensor(out=w3[:, :, j], in0=m0T[:, :], in1=A3[:, :, j], op=ALU.subtract),
                         r=['m0T', ('G', 3), ('G', 1)], w=[('G', 1)])
                S.op('act', lambda e: e.activation(out=G2[:, TP:T], in_=G2[:, TP:T], func=AF.Exp), r=[('G', 1)], w=[('G', 1)])
                x3 = G2[:, 0:TS].rearrange("p (b j) -> p b j", j=4)
                for j in range(4):
                    S.op('dve', lambda e: e.tensor_tensor(out=x3[:, :, j], in0=a3[:, :, j], in1=A3[:, :, 3], op=ALU.subtract),
                         r=[('G', 0), ('G', 3), ('G', 1)], w=[('G', 1)])
                S.op('act', lambda e: e.activation(out=G2[:, 0:TS], in_=G2[:, 0:TS], func=AF.Exp), r=[('G', 1)], w=[('G', 1)])
                for tt in range(NT):
                    rows = rows_of(tt)
                    r0 = tt * 128
                    srcs = [(G1, 0, ('G', 0), r0), (G3, 4, ('G', 2), r0)]
                    if tt == 16:
                        srcs += [(G2, 8, ('G', 1), r0), (G2, 12, ('G', 1), 0)]
                    for (G, co, gname, c0_) in srcs:
                        S.op('pe', lambda e: e.matmul(psum[:rows, 7, co:co + 4], lhsT=G[0:4, c0_:c0_ + rows], rhs=identf[0:4, 0:4],
                                                      start=True, stop=True), r=[gname, 'identf'], w=[('ps', 7)])
                    ncp = 16 if tt == 16 else 8
                    S.op('act', lambda e: e.copy(out=atm[:rows, tt, 0:ncp], in_=psum[:rows, 7, 0:ncp]), r=[('ps', 7)], w=[('atm', tt)])
                S.op('dve', lambda e: e.tensor_scalar(out=G4[:, :], in0=G4[:, :], scalar1=-1.0, scalar2=None, op0=ALU.mult),
                     r=[('G', 3)], w=[('G', 3)])
                dbg('G1', G1[:, :], [4, T], F32, [('G', 0)])
                dbg('G4', G4[:, :], [4, T], F32, [('G', 3)])
                dbg('G3', G3[:, :], [4, T], F32, [('G', 2)])

            S.freed()
            ktm = [sbm(f"ktm_{i}", [128, NT, 128], BF16) for i in range(2)]
            sig = [sbm(f"sig_{i}", [128, NT, 128], BF16) for i in range(2)]
            dtl = [sbm(f"dt_{i}", [128, 512], F32) for i in range(2)]
            kw = sbm("kw", [128, 128], BF16)
            wsc = sbm("wsc", [128, 16], F32)
            cst = sbm("cst", [128, 129], F32)

            fwb = sbm("fwb", [128, 64], F32)
            for h in range(4):
                S.op('pe', lambda e: e.matmul(psum[:, 7, 0:16], lhsT=selt[0:4, h, :], rhs=mfin[0:4, 32:48], start=True, stop=True),
                     r=['selt', 'mfin2'], w=[('ps', 7)])
                S.op('act', lambda e: e.copy(out=fwb[:, h * 16:(h + 1) * 16], in_=psum[:, 7, 0:16]), r=[('ps', 7)], w=[('fwb', h)])
            C0x = sbm("C0x", [128, 64, 130], BF16)
            snt = sbm("snt", [64, 128], F32)
            n0f = sbm("n0f", [128, 64], F32)
            nnew = sbm("nnew", [128, 64], F32)
            bmask = sbm("bmask_sb", [64, 64], F32)
            rowsel = sbm("rowsel_sb", [64, 16], F32)
            qmask = sbm("qmask", [128, 16, 64], BF16)
            kwm = sbm("kwm", [64, 16, 128], BF16)
            kws = sbm("kws", [64, 128], BF16)
            c0f = [sbm(f"c0f_{i}", [128, 128], F32) for i in range(2)]
            cnew = [sbm(f"cnew_{i}", [128, 129], F32) for i in range(2)]
            dts = sbm("dts", [64, 64], F32)
            pTs = sbm("pTs", [64, 64], BF16)
            ist = sbm("ist", [64, 129], F32)
            comb = sbm("comb", [64, 129], F32)
            sC_v = sC_in.rearrange("bh k v -> k bh v")
            for g in range(8):
                S.dma('pool', lambda e: e.dma_start(out=C0x[:, g * 8:(g + 1) * 8, 0:128], in_=sC_v[:, g * 8:(g + 1) * 8, :]),
                      w=[('C0x', g)])
            S.dma('sp', lambda e: e.dma_start(out=snt[:], in_=sn_in[:, :]), w=['snt'])
            S.op('pe', lambda e: e.matmul(psum[:, 7, 0:64], lhsT=snt[:, :], rhs=identf[0:64, 0:64], start=True, stop=True),
                 r=['snt', 'identf'], w=[('ps', 7)])
            S.op('act', lambda e: e.copy(out=n0f[:, :], in_=psum[:, 7, 0:64]), r=[('ps', 7)], w=['n0f'])
            S.op('act', lambda e: e.copy(out=C0x[:, :, 128], in_=psum[:, 7, 0:64]), r=[('ps', 7)], w=['C0xn'])
            S.dma('sp', lambda e: e.dma_start(out=bmask[:], in_=bmask_d[:, :]), w=['bmask'])
            S.dma('sp', lambda e: e.dma_start(out=rowsel[:], in_=rowsel_d[:, :]), w=['rowsel'])
            S.op('dve', lambda e: e.memset(qmask[:], 0.0), w=['qmask'])

            def mlstm_sample(h, sl):
                qT, kT, V = qk[sl][0], qk[sl][1], vt[sl]
                qn, kn, vn = ('qT', sl), ('kT', sl), ('vt', sl)
                if 'ms_a' not in SKIP:
                    mlstm_sample_a(h, sl)
                mlstm_sample_b(h, sl)

            def mlstm_sample_a(h, sl):
                qT, kT, V = qk[sl][0], qk[sl][1], vt[sl]
                qn, kn, vn = ('qT', sl), ('kT', sl), ('vt', sl)
                S.op('pe', lambda e: e.matmul(psum[0:64, 2, 0:64], lhsT=kT[:, TP:T], rhs=qT[:, TP:T], start=True, stop=True),
                     r=[(kn, 4), (qn, 4)], w=[('ps', 2)])
                S.op('act', lambda e: e.activation(out=dts[:, :], in_=nAb[0:64, TP:T], func=AF.Exp, bias=atm[0:64, 16, h:h + 1]),
                     r=[('nAb', 4), ('atm', 16)], w=['dts'])
                S.op('dve', lambda e: e.tensor_tensor(out=dts[:, :], in0=dts[:, :], in1=bmask[:, :], op=ALU.mult), r=['dts', 'bmask'], w=['dts'])
                S.op('dve', lambda e: e.scalar_tensor_tensor(out=pTs[:, :], in0=psum[0:64, 2, 0:64], scalar=SCL, in1=dts[:, :],
                                                             op0=ALU.mult, op1=ALU.mult), r=[('ps', 2), 'dts'], w=['pTs'])
                S.op('pe', lambda e: e.matmul(psum[0:64, 3, 0:129], lhsT=pTs[:, :], rhs=V[0:64, 16, :], start=True, stop=True),
                     r=['pTs', (vn, 16)], w=[('ps', 3)])
                for b in range(16):
                    S.op('dve', lambda e: e.tensor_copy(out=qmask[:, b, 4 * b:4 * b + 4], in_=qT[:, TP + 4 * b:TP + 4 * b + 4]),
                         r=[(qn, 4), 'qmask'], w=[('qmask', b)])
                for b in range(16):
                    S.op('pe', lambda e: e.matmul(psum[0:64, 4, 0:129], lhsT=qmask[:, b, :], rhs=C0x[:, b * 4 + h, 0:129],
                                                  start=(b == 0), stop=(b == 15)),
                         r=[('qmask', b), ('C0x', b // 2), 'C0xn'], w=[('ps', 4)], inc=(b == 15))
                S.op('act', lambda e: e.copy(out=ist[:, :], in_=psum[0:64, 4, 0:129]), r=[('ps', 4)], w=['ist'])
                S.op('dve', lambda e: e.scalar_tensor_tensor(out=comb[:, :], in0=ist[:, :], scalar=atm[0:64, 16, 8 + h:9 + h],
                                                             in1=psum[0:64, 3, 0:129], op0=ALU.mult, op1=ALU.add),
                     r=['ist', ('atm', 16), ('ps', 3)], w=['comb'])
                mlstm_finalize(h, sl, 16, 64, comb, ['comb'])

            def mlstm_sample_b(h, sl):
                qT, kT, V = qk[sl][0], qk[sl][1], vt[sl]
                qn, kn, vn = ('qT', sl), ('kT', sl), ('vt', sl)
                if 'ms_b' in SKIP:
                    return
                S.op('dve', lambda e: e.tensor_scalar(out=kws[:, :], in0=ktm[sl][0:64, 16, :], scalar1=atm[0:64, 16, 12 + h:13 + h], scalar2=SCL,
                                                      op0=ALU.mult, op1=ALU.mult), r=[('ktm', sl, 16), ('atm', 16)], w=['kws'])
                for b in range(16):
                    S.op('dve', lambda e: e.tensor_scalar(out=kwm[:, b, :], in0=kws[:, :], scalar1=rowsel[:, b:b + 1], scalar2=None, op0=ALU.mult),
                         r=['kws', 'rowsel'], w=[('kwm', b)])
                for b in range(16):
                    cb = b % 2
                    bank = 5 + cb
                    bh = b * 4 + h
                    S.dma('sp', lambda e: e.dma_start(out=c0f[cb][:], in_=sC_in[bh]), w=[('c0f', cb)])
                    S.op('pe', lambda e: e.matmul(psum[:, bank, 0:129], lhsT=kwm[:, b, :], rhs=V[0:64, 16, :], start=True, stop=True),
                         r=[('kwm', b), (vn, 16)], w=[('ps', bank)])
                    S.op('dve', lambda e: e.scalar_tensor_tensor(out=cnew[cb][:, 0:128], in0=c0f[cb][:, :], scalar=fwb[:, h * 16 + b:h * 16 + b + 1],
                                                                 in1=psum[:, bank, 0:128], op0=ALU.mult, op1=ALU.add),
                         r=[('c0f', cb), ('fwb', h), ('ps', bank)], w=[('cnew', cb)])
                    S.op('dve', lambda e: e.scalar_tensor_tensor(out=nnew[:, bh:bh + 1], in0=n0f[:, bh:bh + 1], scalar=fwb[:, h * 16 + b:h * 16 + b + 1],
                                                                 in1=psum[:, bank, 128:129], op0=ALU.mult, op1=ALU.add),
                         r=['n0f', ('fwb', h), ('ps', bank)], w=[('nnew', bh)])
                    S.dma('sp', lambda e: e.dma_start(out=cs_out[bh], in_=cnew[cb][:, 0:128]), r=[('cnew', cb)], w=[('cs_out', bh)], is_out=True)


            def mlstm_head(h, it):
                sl = it % 2
                for j, col0 in enumerate([1536 + h * 128, 2048 + h * 128, 2560 + h * 128, 3072 + h * 128]):
                    S.dma('pool', lambda e: e.dma_start(out=wt[sl][:, :, j * 128:(j + 1) * 128], in_=win_d[:, :, col0:col0 + 128]),
                          w=[('wt', sl, j)])
                qT, kT, V = qk[sl][0], qk[sl][1], vt[sl]
                qn, kn, vn = ('qT', sl), ('kT', sl), ('vt', sl)
                for blk in range(5):
                    cc, n = blk_cols(blk)
                    S.op('pe', lambda e: e.matmul(psum[:, 7, 0:n], lhsT=selt[0:4, h, :], rhs=G4[0:4, cc:cc + n], start=True, stop=True),
                         r=['selt', ('G', 3)], w=[('ps', 7)])
                    S.op('act', lambda e: e.copy(out=nAb[:, cc:cc + n], in_=psum[:, 7, 0:n]), r=[('ps', 7)], w=[('nAb', blk)])
                proj_fm(qT, qn, wt[sl], ('wt', sl, 0), 0)
                proj_fm(kT, kn, wt[sl], ('wt', sl, 1), 128)

                def cons_k(tt, rows, ps_ap):
                    S.op('act', lambda e: e.copy(out=ktm[sl][:rows, tt, :], in_=ps_ap), r=[('ps', 7)], w=[('ktm', sl, tt)])
                proj_tm(cons_k, wt[sl], ('wt', sl, 1), 128, 128)

                def cons_v(tt, rows, ps_ap):
                    S.op('act', lambda e: e.copy(out=V[:rows, tt, 0:128], in_=ps_ap), r=[('ps', 7), vn], w=[(vn, tt)])
                proj_tm(cons_v, wt[sl], ('wt', sl, 2), 256, 128)

                def cons_o(tt, rows, ps_ap):
                    S.op('act', lambda e: e.activation(out=sig[sl][:rows, tt, :], in_=ps_ap, func=AF.Sigmoid),
                         r=[('ps', 7)], w=[('sig', sl, tt)])
                proj_tm(cons_o, wt[sl], ('wt', sl, 3), 384, 128)
                qbufs = lambda q0, n: [(qn, b) for b in range(q0 // 512, (q0 + n - 1) // 512 + 1)]
                nbufs = lambda q0, n: [('nAb', b) for b in range(q0 // 512, (q0 + n - 1) // 512 + 1)]
                itp = 0
                for qb in range(0 if 'mloop' not in SKIP else 0, 4 if 'mloop' not in SKIP else 0):
                    for kt in range(4 * qb + 4):
                        qt0 = max(kt, 4 * qb)
                        ncol = (4 * qb + 4 - qt0) * 128
                        q0 = qt0 * 128
                        bank = 2 + itp % 4
                        db = itp % 2
                        itp += 1
                        S.op('pe', lambda e: e.matmul(psum[:, bank, 0:ncol], lhsT=kT[:, kt * 128:(kt + 1) * 128],
                                                      rhs=qT[:, q0:q0 + ncol], start=True, stop=True),
                             r=[(kn, kt // 4)] + qbufs(q0, ncol), w=[('ps', bank)])
                        S.op('act', lambda e: e.activation(out=dtl[db][:, 0:ncol], in_=nAb[:, q0:q0 + ncol], func=AF.Exp,
                                                           bias=atm[:, kt, h:h + 1]),
                             r=nbufs(q0, ncol) + [('atm', kt)], w=[('dt', db)])
                        if qt0 == kt:
                            S.op('dve', lambda e: e.tensor_tensor(out=dtl[db][:, 0:128], in0=dtl[db][:, 0:128], in1=trim[:, :], op=ALU.mult),
                                 r=[('dt', db), 'trim'], w=[('dt', db)])
                        S.op('dve', lambda e: e.scalar_tensor_tensor(out=pT[db][0][:, 0:ncol], in0=psum[:, bank, 0:ncol], scalar=SCL,
                                                                     in1=dtl[db][:, 0:ncol], op0=ALU.mult, op1=ALU.mult),
                             r=[('ps', bank), ('dt', db)], w=[('pT', db, 0)])
                        for qt in range(qt0, 4 * qb + 4):
                            jq = qt - 4 * qb
                            oap, obuf = ops(jq)
                            off = (qt - qt0) * 128
                            S.op('pe', lambda e: e.matmul(oap, lhsT=pT[db][0][:, off:off + 128], rhs=V[:, kt, :],
                                                          start=(kt == 0 and jq % 3 == 0), stop=(kt == qt), skip_group_check=True),
                                 r=[('pT', db, 0), (vn, kt)], w=[obuf])
                    for jq in range(4):
                        tt = 4 * qb + jq
                        oap, obuf = ops(jq)
                        mlstm_finalize(h, sl, tt, 128, oap, [obuf])
                S.op('act', lambda e: e.activation(out=wsc[:, 0:16], in_=atm[:, 0:16, h], func=AF.Exp, bias=nAb[:, TP - 1:TP]),
                     r=[('atm', tt) for tt in range(16)] + [('nAb', 3)], w=['wsc'])
                for kt in range(16 if 'mstate' not in SKIP else 0):
                    S.op('dve', lambda e: e.tensor_scalar(out=kw[:, :], in0=ktm[sl][:, kt, :], scalar1=wsc[:, kt:kt + 1], scalar2=SCL,
                                                          op0=ALU.mult, op1=ALU.mult),
                         r=[('ktm', sl, kt), 'wsc'], w=['kw'])
                    S.op('pe', lambda e: e.matmul(psum[:, 6, 0:129], lhsT=kw[:, :], rhs=V[:, kt, :], start=(kt == 0), stop=(kt == 15)),
                         r=['kw', (vn, kt)], w=[('ps', 6)])
                S.op('act', lambda e: e.copy(out=cst[:, :], in_=psum[:, 6, 0:129]), r=[('ps', 6)], w=['cst'])
                S.dma('sp', lambda e: e.dma_start(out=cp_out[h], in_=cst[:, 0:128]), r=['cst'], w=[('cp_out', h)], is_out=True)
                S.dma('sp', lambda e: e.dma_start(out=np_out[h:h + 1, :].rearrange("o d -> d o"), in_=cst[:, 128:129],
                                                  allow_slow_non_contiguous=True), r=['cst'], w=[('np_out', h)], is_out=True)
                if 'msample' not in SKIP:
                    mlstm_sample(h, sl)

            def mlstm_finalize(h, sl, tt, rows, oap, obufs):
                S.op('dve', lambda e: e.tensor_scalar(out=fsm[:rows, 5:6], in0=oap[:rows, 128:129], scalar1=-1.0, scalar2=None, op0=ALU.mult),
                     r=obufs, w=['fsm5'])
                S.op('dve', lambda e: e.tensor_tensor(out=fsm[:rows, 4:5], in0=oap[:rows, 128:129], in1=fsm[:rows, 5:6], op=ALU.max),
                     r=obufs + ['fsm5'], w=['fsm4'])
                S.op('dve', lambda e: e.tensor_tensor(out=fsm[:rows, 4:5], in0=fsm[:rows, 4:5], in1=atm[:rows, tt, 4 + h:5 + h], op=ALU.max),
                     r=['fsm4', ('atm', tt)], w=['fsm4'])
                S.op('dve', lambda e: e.reciprocal(out=fsm[:rows, 4:5], in_=fsm[:rows, 4:5]), r=['fsm4'], w=['fsm4'])
                S.op('dve', lambda e: e.tensor_scalar(out=fin[:rows, :], in0=oap[:rows, 0:128], scalar1=fsm[:rows, 4:5], scalar2=None, op0=ALU.mult),
                     r=obufs + ['fsm4'], w=['fin'])
                sumsq(fsm[:rows, 2:3], fin[:rows, :], rows, 128, ['fin'], 'fsm2')
                rstd_act(fsm[:rows, 3:4], fsm[:rows, 2:3], 128, ['fsm2'], 'fsm3')
                S.op('dve', lambda e: e.scalar_tensor_tensor(out=fin[:rows, :], in0=fin[:rows, :], scalar=fsm[:rows, 3:4], in1=mnb[:rows, :],
                                                             op0=ALU.mult, op1=ALU.mult), r=['fin', 'fsm3', 'mnb'], w=['fin'])
                S.op('dve', lambda e: e.tensor_tensor(out=mixed[:rows, tt, 512 + h * 128:512 + (h + 1) * 128], in0=fin[:rows, :],
                                                      in1=sig[sl][:rows, tt, :], op=ALU.mult),
                     r=['fin', ('sig', sl, tt)], w=[('mixed', tt, 4 + h)])

            for h in range(4):
                if 'mheads' not in SKIP:
                    mlstm_head(h, h)
            if 'mheads' not in SKIP and 'msample' not in SKIP and 'ms_b' not in SKIP:
                S.op('pe', lambda e: e.matmul(psum[0:64, 7, 0:128], lhsT=nnew[:, :], rhs=identf[:, :], start=True, stop=True),
                     r=[('nnew', bh) for bh in range(64)] + ['identf'], w=[('ps', 7)])
                S.op('act', lambda e: e.copy(out=snt[:, :], in_=psum[0:64, 7, 0:128]), r=[('ps', 7)], w=['snt'])
                S.dma('sp', lambda e: e.dma_start(out=ns_out[:, :], in_=snt[:, :]), r=['snt'], w=['ns_out'], is_out=True)
            eh.close()
            S.freed()
            if stage > 4:
                sample_attn(hnT, qblk, ks_s, vs_s, tz, lsm, anb)
                S.freed()
            dbg('mixed', mixed[:, :, :], [128, NT, D], BF16, [('mixed', tt, h) for tt in range(16) for h in range(8)])
            dbg('hnT', hnT[:, :, :], [128, KD, T], BF16, [('hT', 16)])
            if stage > 3:
                out_proj(em, hnT, mixed, wt)

        def sample_attn(hnT, qblk, ks_s, vs_s, tz, lsm, anb):
            nlam = lsm[:, 5:6]
            with ExitStack() as ea:
                def sba(name, shape, dt):
                    return ea.enter_context(nc.sbuf_tensor(name, list(shape), dt))
                ptb = sba("ptb", [128, 256], I32)
                iot = sba("iot", [128, 256], I32)
                idx = sba("idx", [128, 256], I32)
                identf = sba("identf_sa", [128, 128], F32)
                rmod = sba("rmod_sb", [4, 64], F32)
                rowsel = sba("rowsel_sa", [64, 16], F32)
                tzN = sba("tzN", [4, 32], F32)
                tmpN = sba("tmpN", [64, 32], F32)
                corrN = sba("corrN", [64, 16, 32], F32)
                corr15 = sba("corr15", [128, 32], F32)
                Kb = [sba(f"Kb_{i}", [128, 16, 512], BF16) for i in range(2)]
                Vb = [sba(f"Vb_{i}", [128, 16, 512], BF16) for i in range(2)]
                onesb = sba("onesb", [128, 2], BF16)
                sel01 = sba("sel01_sb", [32, 32], F32)
                dsel = sba("dsel", [32, 16], F32)
                bm16 = sba("bm16_sb", [16, 512], F32)
                anb4 = sba("anb4", [16, 512], F32)
                rs32 = sba("rs32", [32, 2], F32)
                on32 = sba("on32", [32, 512], F32)
                a16 = sba("a16", [16, 512], F32)
                rs16 = sba("rs16", [16, 4], F32)
                kTb = sba("kTb", [128, 4, 2048], BF16)
                pTb = sba("pTb", [128, 512], BF16)
                pTn = sba("pTn", [64, 32], BF16)
                S.dma('sp', lambda e: e.dma_start(out=ptb[:], in_=pt_in[0:1, :].partition_broadcast(128)), w=['ptb'])
                S.dma('sp', lambda e: e.dma_start(out=identf[:], in_=identf_d[:, :]), w=['identf'])
                S.dma('sp', lambda e: e.dma_start(out=rmod[:], in_=rmod_d[:, :]), w=['rmod'])
                S.dma('sp', lambda e: e.dma_start(out=rowsel[:], in_=rowsel_d[:, :]), w=['rowsel'])
                S.op('pool', lambda e: e.iota(iot[:], pattern=[[0, 256]], base=0, channel_multiplier=1), w=['iot'])
                S.op('dve', lambda e: e.scalar_tensor_tensor(out=idx[:], in0=ptb[:], scalar=128, in1=iot[:], op0=ALU.mult, op1=ALU.add),
                     r=['ptb', 'iot'], w=['idx'])
                S.op('dve', lambda e: e.memset(onesb[:], 1.0), w=['onesb'])
                S.dma('sp', lambda e: e.dma_start(out=sel01[:], in_=sel01_d[:, :]), w=['sel01'])
                S.dma('sp', lambda e: e.dma_start(out=bm16[:], in_=bm16_d[:, :]), w=['bm16'])
                for h in range(4):
                    S.dma('sp', lambda e: e.dma_start(out=anb4[:, h * 128:(h + 1) * 128], in_=attn_norm[0:1, :].partition_broadcast(16)),
                          w=['anb4'])
                S.op('dve', lambda e: e.scalar_tensor_tensor(out=dsel[:, :], in0=sel01[:, 16:32], scalar=nlam[0:32, :], in1=sel01[:, 0:16],
                                                             op0=ALU.mult, op1=ALU.add), r=['sel01', 'lsm'], w=['dsel'])
                for h in range(4):
                    for c in range(2):
                        S.op('dve', lambda e: e.tensor_copy(out=corr15[:, h * 8 + c * 4:h * 8 + c * 4 + 4], in_=tz[:, h, 128:132]),
                             r=['tz'], w=[('corr15', h, c)])
                        S.op('dve', lambda e: e.tensor_copy(out=tzN[:, h * 8 + c * 4:h * 8 + c * 4 + 4], in_=tz[0:4, h, 0:4]),
                             r=['tz'], w=[('tzN', h, c)])
                S.op('pe', lambda e: e.matmul(psum[0:64, 7, 0:32], lhsT=rmod[:, :], rhs=tzN[:, :], start=True, stop=True),
                     r=['rmod'] + [('tzN', h, c) for h in range(4) for c in range(2)], w=[('ps', 7)])
                S.op('act', lambda e: e.copy(out=tmpN[:, :], in_=psum[0:64, 7, 0:32]), r=[('ps', 7)], w=['tmpN'])
                for b in range(16):
                    S.op('dve', lambda e: e.tensor_scalar(out=corrN[:, b, :], in0=tmpN[:, :], scalar1=rowsel[:, b:b + 1], scalar2=None, op0=ALU.mult),
                         r=['tmpN', 'rowsel'], w=[('corrN', b)])
                c15 = [('corr15', h, c) for h in range(4) for c in range(2)]

                def gather(b):
                    sl = b % 2
                    for pg in range(16):
                        col = b * 16 + pg
                        S.dma('pool', lambda e: e.indirect_dma_start(out=Kb[sl][:, pg, :], out_offset=None, in_=ck_in[:, :],
                                                                      in_offset=bass.IndirectOffsetOnAxis(ap=idx[:, col:col + 1], axis=0)),
                              r=['idx'], w=[('Kb', sl, pg)])
                        S.dma('pool', lambda e: e.indirect_dma_start(out=Vb[sl][:, pg, :], out_offset=None, in_=cv_in[:, :],
                                                                      in_offset=bass.IndirectOffsetOnAxis(ap=idx[:, col:col + 1], axis=0)),
                              r=['idx'], w=[('Vb', sl, pg)])

                gather(0)
                for b in range(16):
                    sl = b % 2
                    if b + 1 < 16:
                        gather(b + 1)
                    tb = 0
                    for h in range(4):
                        for half in range(2):
                            bank = tb % 2
                            tb += 1
                            pv = psb16(bank)
                            for p8 in range(8):
                                pg = half * 8 + p8
                                S.op('pe', lambda e: e.transpose(out=pv[:, p8 * 128:(p8 + 1) * 128], in_=Kb[sl][:, pg, h * 128:(h + 1) * 128],
                                                                 identity=identb[:, :]),
                                     r=[('Kb', sl, pg), 'identb'], w=[('ps', bank)], inc=(p8 == 7))
                            S.op('act', lambda e: e.copy(out=kTb[:, h, half * 1024:(half + 1) * 1024], in_=pv[:, 0:1024]),
                                 r=[('ps', bank)], w=[('kTb', h, half)])
                    for h in range(4):
                        for pg in range(16):
                            S.op('pe', lambda e: e.matmul(psum[:, 2, pg * 32 + h * 8:pg * 32 + h * 8 + 8], lhsT=kTb[:, h, pg * 128:(pg + 1) * 128],
                                                          rhs=qblk[:, b, h, :], start=True, stop=True, skip_group_check=True),
                                 r=[('kTb', h, pg // 8), ('qblk', h, 0), ('qblk', h, 1)], w=[('ps', 2)], inc=(h == 3 and pg == 15))
                    for h in range(4):
                        S.op('pe', lambda e: e.matmul(psum[0:64, 3, h * 8:h * 8 + 8], lhsT=ks_s[:, h, :], rhs=qblk[:, b, h, :],
                                                      start=True, stop=True, skip_group_check=True),
                             r=[('ks_s', h), ('qblk', h, 0), ('qblk', h, 1)], w=[('ps', 3)], inc=(h == 3))
                    S.op('act', lambda e: e.activation(out=pTb[:, :], in_=psum[:, 2, :], func=AF.Exp, scale=0.125), r=[('ps', 2)], w=['pTb'])
                    S.op('dve', lambda e: e.tensor_tensor(out=pTb[:, 480:512], in0=pTb[:, 480:512], in1=corr15[:, :], op=ALU.mult),
                         r=['pTb'] + c15, w=['pTb'])
                    S.op('act', lambda e: e.activation(out=pTn[:, :], in_=psum[0:64, 3, 0:32], func=AF.Exp, scale=0.125), r=[('ps', 3)], w=['pTn'])
                    S.op('dve', lambda e: e.tensor_tensor(out=pTn[:, :], in0=pTn[:, :], in1=corrN[:, b, :], op=ALU.mult),
                         r=['pTn', ('corrN', b)], w=['pTn'])
                    for pg in range(16):
                        S.op('pe', lambda e: e.matmul(psum[0:32, 4, :], lhsT=pTb[:, pg * 32:(pg + 1) * 32], rhs=Vb[sl][:, pg, :],
                                                      start=(pg == 0), stop=False), r=['pTb', ('Vb', sl, pg)], w=[('ps', 4)], inc=False)
                    S.op('pe', lambda e: e.matmul(psum[0:32, 4, :], lhsT=pTn[:, :], rhs=vs_s[:, :, :].rearrange("p h d -> p (h d)"),
                                                  start=False, stop=True), r=['pTn'] + [('vs_s', h) for h in range(4)], w=[('ps', 4)])
                    for pg in range(16):
                        S.op('pe', lambda e: e.matmul(psum[0:32, 5, 0:1], lhsT=pTb[:, pg * 32:(pg + 1) * 32], rhs=onesb[:, 0:1],
                                                      start=(pg == 0), stop=False), r=['pTb', 'onesb'], w=[('ps', 5)], inc=False)
                    S.op('pe', lambda e: e.matmul(psum[0:32, 5, 0:1], lhsT=pTn[:, :], rhs=onesb[0:64, 0:1], start=False, stop=True),
                         r=['pTn', 'onesb'], w=[('ps', 5)])
                    S.op('dve', lambda e: e.reciprocal(out=rs32[:, 0:1], in_=psum[0:32, 5, 0:1]), r=[('ps', 5)], w=['rs32'])
                    S.op('dve', lambda e: e.tensor_scalar(out=on32[:, :], in0=psum[0:32, 4, :], scalar1=rs32[:, 0:1], scalar2=None, op0=ALU.mult),
                         r=[('ps', 4), 'rs32'], w=['on32'])
                    S.op('pe', lambda e: e.matmul(psum[0:16, 6, :], lhsT=dsel[:, :], rhs=on32[:, :], start=True, stop=True),
                         r=['dsel', 'on32'], w=[('ps', 6)])
                    S.op('dve', lambda e: e.tensor_tensor(out=a16[:, :], in0=psum[0:16, 6, :], in1=bm16[:, :], op=ALU.mult),
                         r=[('ps', 6), 'bm16'], w=['a16'])
                    sumsq(rs16[:, 0:1], a16[:, :], 16, 512, ['a16'], 'rs16a')
                    rstd_act(rs16[:, 1:2], rs16[:, 0:1], 128, ['rs16a'], 'rs16b', post=1.0 - LAM_INIT)
                    S.op('dve', lambda e: e.scalar_tensor_tensor(out=a16[:, :], in0=a16[:, :], scalar=rs16[:, 1:2], in1=anb4[:, :],
                                                                 op0=ALU.mult, op1=ALU.mult), r=['a16', 'rs16b', 'anb4'], w=['a16'])
                    for h in range(4):
                        S.op('pe', lambda e: e.matmul(psum[:, 7, h * 16:(h + 1) * 16], lhsT=a16[:, h * 128:(h + 1) * 128], rhs=identf[0:16, 0:16],
                                                      start=True, stop=True, skip_group_check=True),
                             r=['a16', 'identf'], w=[('ps', 7)], inc=(h == 3))
                    S.op('act', lambda e: e.copy(out=hnT[:, 0:4, TP + 4 * b:TP + 4 * b + 4],
                                                 in_=psum[:, 7, 0:80].rearrange("p (h x) -> p h x", x=20)[:, :, 0:4]),
                         r=[('ps', 7)], w=[('hTs', b)])

        def out_proj(em, hnT, mixed, wt):
            xr = [em.enter_context(nc.sbuf_tensor(f"xro_{i}", [128, D], F32)) for i in range(2)]
            tmp1 = em.enter_context(nc.sbuf_tensor("tmpo", [128, D], F32))
            wo_d = w_out.rearrange("(kc p) f -> p kc f", p=128)
            for half in range(2):
                S.dma('pool', lambda e: e.dma_start(out=wt[half][:], in_=wo_d[:, :, half * 512:(half + 1) * 512]),
                      w=[('wo', half)] + [('wt', half, j) for j in range(4)])
            for tt in range(NT):
                rows = rows_of(tt)
                r0 = tt * 128
                bank = tt % 2
                pv = psb16(bank)
                c_lo = 4 if (tt == 16 and stage > 4) else 0
                for c in range(c_lo, KD):
                    S.op('pe', lambda e, c=c: e.transpose(out=pv[:, c * 128:c * 128 + rows], in_=mixed[:rows, tt, c * 128:(c + 1) * 128],
                                                          identity=identb[:rows, :rows]),
                         r=[('mixed', tt, c), 'identb'], w=[('ps', bank)], inc=(c == KD - 1))
                srcv = pv[:, 0:1024].rearrange("p (c r) -> p c r", r=128)[:, c_lo:KD, 0:rows]
                S.op('act', lambda e: e.copy(out=hnT[:, c_lo:KD, r0:r0 + rows], in_=srcv),
                     r=[('ps', bank)] + ([('hTs', b) for b in range(16)] if c_lo else []), w=[('hT', tt)])
            for tt in range(NT):
                rows = rows_of(tt)
                r0 = tt * 128
                yb = tt % 2
                S.dma('sp', lambda e: e.dma_start(out=xr[yb][:rows, :], in_=x1_s[r0:r0 + rows, :]), r=[('dram', id(x1_s), tt)], w=[('xro', yb)])
                for half in range(2):
                    bank = 4 + 2 * yb + half
                    for kc in range(KD):
                        S.op('pe', lambda e, kc=kc: e.matmul(psum[:rows, bank, :], lhsT=hnT[:, kc, r0:r0 + rows], rhs=wt[half][:, kc, :],
                                                             start=(kc == 0), stop=(kc == KD - 1)),
                             r=[('hT', tt), ('wo', half)], w=[('ps', bank)], inc=(kc == KD - 1))
                b0 = 4 + 2 * yb
                S.op('act', lambda e: e.copy(out=tmp1[:rows, :].rearrange("p (a b) -> p a b", a=2), in_=psum[:rows, b0:b0 + 2, :]),
                     r=[('ps', b0), ('ps', b0 + 1)], w=['tmpo'])
                sumsq(ss[:rows, tt:tt + 1], tmp1[:rows, :], rows, D, ['tmpo'], ('ssc', tt))
                rstd_act(rstd[:rows, tt:tt + 1], ss[:rows, tt:tt + 1], D, [('ssc', tt)], ('rstd', tt))
                S.op('dve', lambda e: e.scalar_tensor_tensor(out=tmp1[:rows, :], in0=tmp1[:rows, :], scalar=rstd[:rows, tt:tt + 1],
                                                             in1=gbc[:rows, 1, :], op0=ALU.mult, op1=ALU.mult),
                     r=['tmpo', ('rstd', tt), ('gbc', 1)], w=['tmpo'])
                S.op('dve', lambda e: e.tensor_tensor(out=xr[yb][:rows, :], in0=tmp1[:rows, :], in1=xr[yb][:rows, :], op=ALU.add),
                     r=['tmpo', ('xro', yb)], w=[('xro', yb)])
                S.dma('sp', lambda e: e.dma_start(out=x2_s[r0:r0 + rows, :], in_=xr[yb][:rows, :]),
                      r=[('xro', yb)], w=[('dram', id(x2_s), tt)])

        if stage > 1:
            token_mix()
        if stage > 3:
            with ExitStack() as es2:
                ffn(x2_s, y_out, 1, 4, 5, True)
            S.freed()
        S.finish()
    return nc


_CONST = {}


def consts():
    if not _CONST:
        import ml_dtypes
        _CONST['identb'] = np.eye(128, dtype=np.float32).astype(ml_dtypes.bfloat16)
        _CONST['identf'] = np.eye(128, dtype=np.float32)
        oh = np.zeros((33, 384), np.float32)
        for r in range(384):
            rel = r - 127
            if rel < 0:
                oh[32, r] = 1.0
            else:
                if rel < 16:
                    b = rel
                else:
                    b = 16 + int(np.float32(np.log(np.float32(max(rel, 16)) / np.float32(16))) / np.float32(np.log(8.0)) * np.float32(16))
                    b = min(b, 31)
                oh[b, r] = 1.0
        _CONST['onehot'] = oh
        sel = np.zeros((4, 4, 128), np.float32)
        for h in range(4):
            sel[h, h, :] = 1.0
        _CONST['sel'] = sel
        _CONST['trim'] = np.triu(np.ones((128, 128), np.float32))
        bm = np.zeros((64, 64), np.float32)
        rs = np.zeros((64, 16), np.float32)
        for p in range(64):
            rs[p, p // 4] = 1.0
            for t in range(64):
                if p // 4 == t // 4 and p <= t:
                    bm[p, t] = 1.0
        _CONST['bmask'] = bm
        _CONST['rowsel'] = rs
        rm = np.zeros((4, 64), np.float32)
        for p in range(64):
            rm[p % 4, p] = 1.0
        _CONST['rmod'] = rm
        s01 = np.zeros((32, 32), np.float32)
        b16 = np.zeros((16, 512), np.float32)
        for h in range(4):
            for q in range(4):
                s01[h * 8 + q, h * 4 + q] = 1.0
                s01[h * 8 + 4 + q, 16 + h * 4 + q] = 1.0
                b16[h * 4 + q, h * 128:(h + 1) * 128] = 1.0
        _CONST['sel01'] = s01
        _CONST['bm16'] = b16
        _CONST['tri'] = np.ascontiguousarray(np.eye(128, dtype=np.float32)[::-1])
    return _CONST


def make_in_maps(inputs, cores):
    c = consts()
    maps = []
    xp = inputs['x_prompt']
    xs = inputs['x_sample'].reshape(128 * 4, D)
    for k in cores:
        m = {
            'xin': np.ascontiguousarray(np.concatenate([xp[k], xs[k * 64:(k + 1) * 64]], axis=0)),
            'gains': np.ascontiguousarray(inputs['norm_gains'][0]),
            'wgate': np.ascontiguousarray(inputs['ffn_w_gate'][0]),
            'wup': np.ascontiguousarray(inputs['ffn_w_up'][0]),
            'wdown': np.ascontiguousarray(inputs['ffn_w_down'][0]),
            'identb': c['identb'],
            'w_in': np.ascontiguousarray(inputs['w_in'][0]),
            'rel_bias': np.ascontiguousarray(inputs['rel_bias']),
            'b_gates': np.ascontiguousarray(inputs['b_gates'][0]),
            'lam_q': np.ascontiguousarray(inputs['lam_q'][0].reshape(1, 128)),
            'lam_k': np.ascontiguousarray(inputs['lam_k'][0].reshape(1, 128)),
            'attn_norm': np.ascontiguousarray(inputs['attn_norm']),
            'mlstm_norm': np.ascontiguousarray(inputs['mlstm_norm']),
            'w_out': np.ascontiguousarray(inputs['w_out'][0]),
            'onehot': c['onehot'], 'identf': c['identf'], 'tri': c['tri'], 'sel': c['sel'], 'trim': c['trim'],
            'sm': np.ascontiguousarray(inputs['state_m'][0, k * 16:(k + 1) * 16]),
            'sC': np.ascontiguousarray(inputs['state_C'][0, k * 16:(k + 1) * 16].reshape(64, 128, 128)),
            'sn': np.ascontiguousarray(inputs['state_n'][0, k * 16:(k + 1) * 16].reshape(64, 128)),
            'bmask': c['bmask'], 'rowsel': c['rowsel'],
            'rmod': c['rmod'], 'sel01': c['sel01'], 'bm16': c['bm16'],
            'cache_k': inputs['cache_k'].reshape(2560 * 128, 512),
            'cache_v': inputs['cache_v'].reshape(2560 * 128, 512),
            'pt': np.ascontiguousarray(inputs['page_table'][k * 16:(k + 1) * 16].reshape(1, 256)).astype(np.int32),
        }
        maps.append(m)
    return maps


def kernel(**inputs):
    inputs = {k: np.asarray(v) for k, v in inputs.items()}
    nc = build_nc()
    maps = make_in_maps(inputs, list(range(NCORES)))
    res = run_bass_kernel_spmd(nc, maps, core_ids=list(range(NCORES)))
    R = res.results
    f32 = np.float32
    y = np.stack([r['y_out'] for r in R])
    y_prompt = np.ascontiguousarray(y[:, :TP, :]).astype(f32)
    y_sample = np.ascontiguousarray(y[:, TP:, :]).reshape(128, 4, D).astype(f32)
    ko = np.stack([r['k_out'] for r in R]); vo = np.stack([r['v_out'] for r in R])
    k_prompt = ko[:, :TP].reshape(1, 8, TP, 4, 128).astype(f32)
    v_prompt = vo[:, :TP].reshape(1, 8, TP, 4, 128).astype(f32)
    k_sample = ko[:, TP:].reshape(1, 128, 4, 4, 128).astype(f32)
    v_sample = vo[:, TP:].reshape(1, 128, 4, 4, 128).astype(f32)

    def get(name, shape):
        if name in R[0]:
            return np.stack([r[name] for r in R]).reshape(shape).astype(f32)
        return np.zeros(shape, f32)
    C_prompt = get('cp_out', (1, 8, 4, 128, 128))
    n_prompt = get('np_out', (1, 8, 4, 128))
    m_prompt = get('mp_out', (1, 8, 4))
    C_sample = get('cs_out', (1, 128, 4, 128, 128))
    n_sample = get('ns_out', (1, 128, 4, 128))
    m_sample = get('ms_out', (1, 128, 4))
    return (y_prompt, y_sample, k_prompt, v_prompt, C_prompt, n_prompt, m_prompt,
            k_sample, v_sample, C_sample, n_sample, m_sample)
```

```python
import numpy as np
from contextlib import ExitStack
import concourse.bass as bass
import concourse.mybir as mybir
from concourse.bass_utils import run_bass_kernel_spmd

F32 = mybir.dt.float32
BF16 = mybir.dt.bfloat16
I32 = mybir.dt.int32
ALU = mybir.AluOpType
AF = mybir.ActivationFunctionType
AX = mybir.AxisListType

NCORES = 8
TP, TS = 2048, 64
T = TP + TS
NT = 17
D, FF, KD, KF = 1024, 2816, 8, 22
DIN = 3592
EPS = 1e-6
NDS = 40
LAM_INIT = 0.2


def rows_of(tt):
    return 128 if tt < 16 else 64


def blk_cols(blk):
    return (blk * 512, 512) if blk < 4 else (2048, 64)


class Sched:
    def __init__(self, nc, es):
        self.nc = nc
        self.E = {'pe': nc.tensor, 'act': nc.scalar, 'dve': nc.vector, 'pool': nc.gpsimd, 'sp': nc.sync}
        self.sem = {k: es.enter_context(nc.semaphore(f"sem_{k}")) for k in self.E}
        self.cnt = {k: 0 for k in self.E}
        self.pending = {k: False for k in self.E}
        self.dsem = [es.enter_context(nc.semaphore(f"dsem{i}")) for i in range(NDS)]
        self.dcnt = [0] * NDS
        self.dnext = 0
        self.waited = {}
        self.lastw = {}
        self.readers = {}
        self.out_dmas = []
        self.alias_deps = {}
        for h in list(self.sem.values()) + self.dsem:
            nc.gpsimd.sem_clear(h)
        nc.all_engine_barrier()

    def _semof(self, prod):
        return self.sem[prod] if isinstance(prod, str) else self.dsem[prod[1]]

    def _wait(self, e, prod, val):
        key = (e, prod)
        if self.waited.get(key, 0) >= val:
            return
        self.waited[key] = val
        self.E[e].wait_ge(self._semof(prod), val)

    def _deps(self, r, w):
        deps = {}

        def add(p, v):
            if deps.get(p, 0) < v:
                deps[p] = v
        for b in r:
            lw = self.lastw.get(b)
            if lw is not None:
                add(*lw)
            elif b not in self.readers:
                for p, v in self.alias_deps.items():
                    add(p, v)
        for b in w:
            lw = self.lastw.get(b)
            if lw is not None:
                add(*lw)
            else:
                for p, v in self.alias_deps.items():
                    add(p, v)
            for p, v in self.readers.get(b, {}).items():
                add(p, v)
        return deps

    def freed(self):
        d = {}
        for e in self.E:
            v = self.cnt[e] + (1 if self.pending[e] else 0)
            if v > 0:
                d[e] = v
        for j in range(NDS):
            if self.dcnt[j] > 0:
                d[('d', j)] = self.dcnt[j]
        self.alias_deps = d
        self.lastw = {}
        self.readers = {}

    def _record(self, me, r, w):
        for b in w:
            self.lastw[b] = me
            self.readers[b] = {}
        for b in r:
            d = self.readers.setdefault(b, {})
            if d.get(me[0], 0) < me[1]:
                d[me[0]] = me[1]

    def op(self, e, fn, r=(), w=(), inc=True):
        deps = self._deps(r, w)
        for p, v in deps.items():
            if p == e and e == 'pe':
                continue
            self._wait(e, p, v)
        ins = fn(self.E[e])
        if inc:
            self.cnt[e] += 1
            ins.then_inc(self.sem[e], 1)
            me = (e, self.cnt[e])
            self.pending[e] = False
        else:
            me = (e, self.cnt[e] + 1)
            self.pending[e] = True
        self._record(me, r, w)

    def dma(self, q, fn, r=(), w=(), is_out=False):
        deps = self._deps(r, w)
        j = self.dnext
        self.dnext = (self.dnext + 1) % NDS
        if self.dcnt[j] > 0:
            self._wait(q, ('d', j), self.dcnt[j])
        for p, v in deps.items():
            self._wait(q, p, v)
        ins = fn(self.E[q])
        self.dcnt[j] += 16
        ins.then_inc(self.dsem[j], 16)
        me = (('d', j), self.dcnt[j])
        self._record(me, r, w)
        if is_out:
            self.out_dmas.append(me)

    def finish(self):
        for j in range(NDS):
            if self.dcnt[j] > 0:
                self._wait('sp', ('d', j), self.dcnt[j])
        for e in ('pe', 'act', 'dve', 'pool'):
            assert not self.pending[e], e
            if self.cnt[e] > 0:
                self._wait('sp', e, self.cnt[e])
        self.nc.all_engine_barrier()
        for h in list(self.sem.values()) + self.dsem:
            self.nc.gpsimd.sem_clear(h)
        self.nc.all_engine_barrier()


DEBUG = set()
SKIP = set()


def build_nc(stage=99):
    nc = bass.Bass("TRN2", target_bir_lowering=False)

    def din(name, shape, dt=F32):
        return nc.dram_tensor(name, list(shape), dt, kind="ExternalInput").ap()

    def dout(name, shape, dt=F32):
        return nc.dram_tensor(name, list(shape), dt, kind="ExternalOutput").ap()

    def dscr(name, shape, dt=F32):
        return nc.dram_tensor(name, list(shape), dt, kind="Internal").ap()

    xin = din("xin", [T, D])
    gains = din("gains", [6, D])
    wgate = din("wgate", [2, D, FF])
    wup = din("wup", [2, D, FF])
    wdown = din("wdown", [2, FF, D])
    identb_d = din("identb", [128, 128], BF16)

    w_in = din("w_in", [D, DIN])
    rel_bias = din("rel_bias", [32, 4])
    b_gates = din("b_gates", [2, 4])
    lam_q = din("lam_q", [1, 128])
    lam_k = din("lam_k", [1, 128])
    attn_norm = din("attn_norm", [1, 128])
    mlstm_norm = din("mlstm_norm", [1, 128])
    w_out = din("w_out", [D, D])
    onehot_d = din("onehot", [33, 384])
    identf_d = din("identf", [128, 128])
    tri_d = din("tri", [128, 128])
    sel_d = din("sel", [4, 4, 128])
    trim_d = din("trim", [128, 128])
    sm_in = din("sm", [16, 4])
    ck_in = din("cache_k", [2560 * 128, 512]) if stage > 4 else None
    cv_in = din("cache_v", [2560 * 128, 512]) if stage > 4 else None
    pt_in = din("pt", [1, 256], I32)
    rmod_d = din("rmod", [4, 64])
    sel01_d = din("sel01", [32, 32])
    bm16_d = din("bm16", [16, 512])
    sC_in = din("sC", [64, 128, 128])
    sn_in = din("sn", [64, 128])
    bmask_d = din("bmask", [64, 64])
    rowsel_d = din("rowsel", [64, 16])
    cs_out = dout("cs_out", [64, 128, 128])
    ns_out = dout("ns_out", [64, 128])
    ms_out = dout("ms_out", [16, 4])
    cp_out = dout("cp_out", [4, 128, 128])
    np_out = dout("np_out", [4, 128])
    mp_out = dout("mp_out", [1, 4])
    vec_h = nc.dram_tensor("vec_s", [4, 384], F32, kind="Internal")
    vec_s = vec_h.ap()
    y_out = dout("y_out", [T, D])
    k_out = dout("k_out", [T, 512])
    v_out = dout("v_out", [T, 512])
    x1_s = dscr("x1_s", [T, D])
    x2_s = dout("x2_s", [T, D]) if "x2" in DEBUG else dscr("x2_s", [T, D])

    with ExitStack() as es:
        S = Sched(nc, es)

        def sb(name, shape, dt):
            return es.enter_context(nc.sbuf_tensor(name, list(shape), dt))

        psum = es.enter_context(nc.psum_tensor("psum", [128, 8, 512], F32))

        def dbg(name, ap, shape, dt, rbufs):
            if name not in DEBUG:
                return
            o = nc.dram_tensor("dbg_" + name, list(shape), dt, kind="ExternalOutput").ap()
            S.dma('sp', lambda e: e.dma_start(out=o, in_=ap), r=rbufs, w=[('dbg', name)])

        def psb(b, n=512):
            return psum[:, b, 0:n]

        def psb16(b):
            return psum[:, b, :].bitcast(BF16)

        identb = sb("identb_sb", [128, 128], BF16)
        S.dma('sp', lambda e: e.dma_start(out=identb[:], in_=identb_d[:, :]), w=['identb'])
        gbc = sb("gbc", [128, 2, D], F32)

        def load_gains(gis):
            for j, gi in enumerate(gis):
                S.dma('sp', lambda e: e.dma_start(out=gbc[:, j, :], in_=gains[gi:gi + 1, :].partition_broadcast(128)),
                      w=[('gbc', j)])
        ss = sb("ss", [128, 64], F32)
        rstd = sb("rstd", [128, 64], F32)
        junk = sb("junk", [128, D], F32)

        def sumsq(out, in_, rows, width, rbufs, wbuf):
            S.op('dve', lambda e: e.tensor_tensor(out=junk[:rows, 0:width], in0=in_, in1=in_, op=ALU.mult),
                 r=rbufs, w=['junk'])
            S.op('dve', lambda e: e.tensor_reduce(out=out, in_=junk[:rows, 0:width], axis=AX.X, op=ALU.add),
                 r=['junk'], w=[wbuf])

        def rstd_act(out, in_, n, rbufs, wbuf, post=1.0):
            S.op('act', lambda e: e.activation(out=out, in_=in_, func=AF.Ln, bias=EPS, scale=1.0 / n),
                 r=rbufs, w=[wbuf])
            S.op('act', lambda e: e.activation(out=out, in_=out, func=AF.Exp, scale=-0.5),
                 r=[wbuf], w=[wbuf])
            if post != 1.0:
                S.op('act', lambda e: e.mul(out=out, in_=out, mul=post), r=[wbuf], w=[wbuf])

        def norm_transpose(src, gi, hT, tag, xt, hn):
            for tt in range(NT):
                rows = rows_of(tt)
                sl = tt % 2
                r0 = tt * 128
                S.dma('sp', lambda e: e.dma_start(out=xt[sl][:rows, :], in_=src[r0:r0 + rows, :]), r=[('dram', id(src), tt)], w=[('xt', sl)])
                sumsq(ss[:rows, tt:tt + 1], xt[sl][:rows, :], rows, D, [('xt', sl)], ('ssc', tt))
                rstd_act(rstd[:rows, tt:tt + 1], ss[:rows, tt:tt + 1], D, [('ssc', tt)], ('rstd', tt))
                S.op('dve', lambda e: e.scalar_tensor_tensor(out=hn[sl][:rows, :], in0=xt[sl][:rows, :],
                                                             scalar=rstd[:rows, tt:tt + 1], in1=gbc[:rows, gi, :],
                                                             op0=ALU.mult, op1=ALU.mult),
                     r=[('xt', sl), ('rstd', tt), ('gbc', gi)], w=[('hn', sl)])
                bank = sl
                pv = psb16(bank)
                for c in range(KD):
                    S.op('pe', lambda e, c=c: e.transpose(out=pv[:, c * 128:c * 128 + rows],
                                                          in_=hn[sl][:rows, c * 128:(c + 1) * 128],
                                                          identity=identb[:rows, :rows]),
                         r=[('hn', sl), 'identb'], w=[('ps', bank)], inc=(c == KD - 1))
                srcv = pv[:, 0:1024].rearrange("p (c r) -> p c r", r=128)[:, :, 0:rows]
                S.op('act', lambda e: e.copy(out=hT[:, :, r0:r0 + rows], in_=srcv),
                     r=[('ps', bank)], w=[('hT', tt)])

        def ffn(src, dst, li, g_pre, g_post, dst_is_out):
            load_gains([g_pre, g_post])
            g_pre, g_post = 0, 1
            uT = es2.enter_context(nc.sbuf_tensor(f"uT{li}", [128, KF, T], BF16))
            with ExitStack() as esh:
                hT = esh.enter_context(nc.sbuf_tensor(f"hT{li}", [128, KD, T], BF16))
                wd = esh.enter_context(nc.sbuf_tensor(f"wd{li}", [128, KF, D], BF16))
                with ExitStack() as est:
                    xt = [est.enter_context(nc.sbuf_tensor(f"xt{li}_{i}", [128, D], F32)) for i in range(2)]
                    hn = [est.enter_context(nc.sbuf_tensor(f"hn{li}_{i}", [128, D], BF16)) for i in range(2)]
                    norm_transpose(src, g_pre, hT, ('ffn', li), xt, hn)
                S.freed()
                dbg('ss', ss[:, :], [128, 64], F32, [('ssc', tt) for tt in range(NT)])
                dbg('rstd', rstd[:, :], [128, 64], F32, [('rstd', tt) for tt in range(NT)])
                dbg('hT', hT[:, :, :], [128, KD, T], BF16, [('hT', tt) for tt in range(NT)])
                wgs = [esh.enter_context(nc.sbuf_tensor(f"wg{li}_{i}", [128, KD, 128], BF16)) for i in range(2)]
                wus = [esh.enter_context(nc.sbuf_tensor(f"wu{li}_{i}", [128, KD, 128], BF16)) for i in range(2)]
                sg = [esh.enter_context(nc.sbuf_tensor(f"sg{li}_{i}", [128, 512], F32)) for i in range(2)]
                wg_d = wgate[li].rearrange("(kc p) f -> p kc f", p=128)
                wu_d = wup[li].rearrange("(kc p) f -> p kc f", p=128)
                wd_d = wdown[li].rearrange("(fc p) d -> p fc d", p=128)
                pbi = 0
                for g in range(KF):
                    sl = g % 2
                    S.dma('pool', lambda e: e.dma_start(out=wgs[sl][:], in_=wg_d[:, :, g * 128:(g + 1) * 128]),
                          w=[('wg', sl)])
                    S.dma('pool', lambda e: e.dma_start(out=wus[sl][:], in_=wu_d[:, :, g * 128:(g + 1) * 128]),
                          w=[('wu', sl)])
                    S.dma('pool', lambda e: e.dma_start(out=wd[:, g:g + 1, :], in_=wd_d[:, g:g + 1, :]),
                          w=[('wd', g)])
                    for fl in range(1):
                        fc = g
                        for blk in range(5):
                            c0, n = blk_cols(blk)
                            pb = pbi % 2
                            pbi += 1
                            hbufs = [('hT', tt) for tt in (range(4 * blk, 4 * blk + 4) if blk < 4 else [16])]
                            for kc in range(KD):
                                S.op('pe', lambda e, kc=kc: e.matmul(psb(2 * pb, n), lhsT=wgs[sl][:, kc, fl * 128:(fl + 1) * 128],
                                                                     rhs=hT[:, kc, c0:c0 + n], start=(kc == 0), stop=(kc == KD - 1)),
                                     r=[('wg', sl)] + hbufs, w=[('ps', 2 * pb)], inc=(kc == KD - 1))
                            for kc in range(KD):
                                S.op('pe', lambda e, kc=kc: e.matmul(psb(2 * pb + 1, n), lhsT=wus[sl][:, kc, fl * 128:(fl + 1) * 128],
                                                                     rhs=hT[:, kc, c0:c0 + n], start=(kc == 0), stop=(kc == KD - 1)),
                                     r=[('wu', sl)] + hbufs, w=[('ps', 2 * pb + 1)], inc=(kc == KD - 1))
                            S.op('act', lambda e: e.activation(out=sg[pb][:, 0:n], in_=psb(2 * pb, n), func=AF.Silu),
                                 r=[('ps', 2 * pb)], w=[('sg', pb)])
                            S.op('dve', lambda e: e.tensor_tensor(out=uT[:, fc, c0:c0 + n], in0=sg[pb][:, 0:n],
                                                                  in1=psb(2 * pb + 1, n), op=ALU.mult),
                                 r=[('sg', pb), ('ps', 2 * pb + 1)], w=[('uT', fc, blk)])
                dbg('uT', uT[:, :, :], [128, KF, T], BF16, [('uT', fc, blk) for fc in range(KF) for blk in range(5)])
                xr = [esh.enter_context(nc.sbuf_tensor(f"xr{li}_{i}", [128, D], F32)) for i in range(2)]
                tmp1 = esh.enter_context(nc.sbuf_tensor(f"tmp{li}", [128, D], F32))
                for tt in range(NT):
                    rows = rows_of(tt)
                    r0 = tt * 128
                    yb = tt % 2
                    blk = tt // 4 if tt < 16 else 4
                    S.dma('sp', lambda e: e.dma_start(out=xr[yb][:rows, :], in_=src[r0:r0 + rows, :]), r=[('dram', id(src), tt)], w=[('xr', yb)])
                    for half in range(2):
                        bank = 4 + 2 * yb + half
                        for fc in range(KF):
                            S.op('pe', lambda e, fc=fc: e.matmul(psum[:rows, bank, :], lhsT=uT[:, fc, r0:r0 + rows],
                                                                 rhs=wd[:, fc, half * 512:(half + 1) * 512],
                                                                 start=(fc == 0), stop=(fc == KF - 1)),
                                 r=[('uT', fc, blk), ('wd', fc)], w=[('ps', bank)], inc=(fc == KF - 1))
                    b0 = 4 + 2 * yb
                    S.op('act', lambda e: e.copy(out=tmp1[:rows, :].rearrange("p (a b) -> p a b", a=2),
                                                 in_=psum[:rows, b0:b0 + 2, :]),
                         r=[('ps', b0), ('ps', b0 + 1)], w=['tmp1'])
                    sumsq(ss[:rows, tt:tt + 1], tmp1[:rows, :], rows, D, ['tmp1'], ('ssc', tt))
                    rstd_act(rstd[:rows, tt:tt + 1], ss[:rows, tt:tt + 1], D, [('ssc', tt)], ('rstd', tt))
                    S.op('dve', lambda e: e.scalar_tensor_tensor(out=tmp1[:rows, :], in0=tmp1[:rows, :],
                                                                 scalar=rstd[:rows, tt:tt + 1], in1=gbc[:rows, g_post, :],
                                                                 op0=ALU.mult, op1=ALU.mult),
                         r=['tmp1', ('rstd', tt), ('gbc', g_post)], w=['tmp1'])
                    S.op('dve', lambda e: e.scalar_tensor_tensor(out=xr[yb][:rows, :], in0=tmp1[:rows, :], scalar=0.5,
                                                                 in1=xr[yb][:rows, :], op0=ALU.mult, op1=ALU.add),
                         r=['tmp1', ('xr', yb)], w=[('xr', yb)])
                    S.dma('sp', lambda e: e.dma_start(out=dst[r0:r0 + rows, :], in_=xr[yb][:rows, :]),
                          r=[('xr', yb)], w=[('dram', id(dst), tt)], is_out=dst_is_out)
            S.freed()

        with ExitStack() as es2:
            ffn(xin, x1_s if stage > 1 else y_out, 0, 0, 1, stage <= 1)
        S.freed()

        def token_mix():
            with ExitStack() as em:
                hnT = em.enter_context(nc.sbuf_tensor("hnT", [128, KD, T], BF16))
                load_gains([2, 3])
                with ExitStack() as est:
                    xt = [est.enter_context(nc.sbuf_tensor(f"xtm_{i}", [128, D], F32)) for i in range(2)]
                    hn = [est.enter_context(nc.sbuf_tensor(f"hnm_{i}", [128, D], BF16)) for i in range(2)]
                    norm_transpose(x1_s, 0, hnT, 'mix', xt, hn)
                S.freed()
                win_d = w_in.rearrange("(kc p) f -> p kc f", p=128)
                wt = [em.enter_context(nc.sbuf_tensor(f"wt_{i}", [128, KD, 512], BF16)) for i in range(2)]
                stg = [em.enter_context(nc.sbuf_tensor(f"stg_{i}", [128, 512], F32)) for i in range(2)]
                hbuf_all = [('hT', tt) for tt in range(NT)]
                for gi_, (col0, dst) in enumerate([(512, k_out), (1024, v_out)]):
                    sl = gi_ % 2
                    S.dma('pool', lambda e: e.dma_start(out=wt[sl][:], in_=win_d[:, :, col0:col0 + 512]),
                          w=[('wt', sl, j) for j in range(4)])
                    for tt in range(NT):
                        rows = rows_of(tt)
                        r0 = tt * 128
                        bank = tt % 2
                        for kc in range(KD):
                            S.op('pe', lambda e, kc=kc: e.matmul(psum[:rows, bank, :], lhsT=hnT[:, kc, r0:r0 + rows],
                                                                 rhs=wt[sl][:, kc, :], start=(kc == 0), stop=(kc == KD - 1)),
                                 r=[('wt', sl, j) for j in range(4)] + [('hT', tt)], w=[('ps', bank)], inc=(kc == KD - 1))
                        S.op('act', lambda e: e.copy(out=stg[bank][:rows, :], in_=psum[:rows, bank, :]),
                             r=[('ps', bank)], w=[('stg', bank)])
                        S.dma('sp', lambda e: e.dma_start(out=dst[r0:r0 + rows, :], in_=stg[bank][:rows, :]),
                              r=[('stg', bank)], w=[('dram', id(dst), tt)], is_out=True)
                if stage > 2:
                    mix_body(em, hnT, win_d, wt, stg)
            S.freed()

        def mix_body(em, hnT, win_d, wt, stg):
            def sbm(name, shape, dt):
                return em.enter_context(nc.sbuf_tensor(name, list(shape), dt))
            mixed = sbm("mixed", [128, NT, D], BF16)
            lsm = sbm("lsm", [128, 8], F32)
            c31 = sbm("c31", [128, 8], F32)
            tz = sbm("tz", [128, 4, 256], F32)
            anb = sbm("anb", [128, 128], F32)
            mnb = sbm("mnb", [128, 128], F32)
            qblk = sbm("qblk", [128, 16, 4, 8], BF16)
            ks_s = sbm("ks_s", [128, 4, 64], BF16)
            vs_s = sbm("vs_s", [64, 4, 128], BF16)
            S.op('dve', lambda e: e.memset(qblk[:], 0.0), w=['qblk'])
            etz = ExitStack()

            def sbt(name, shape, dt):
                return etz.enter_context(nc.sbuf_tensor(name, list(shape), dt))
            lq = sbt("lq", [128, 128], F32)
            lk = sbt("lk", [128, 128], F32)
            S.dma('sp', lambda e: e.dma_start(out=lq[:], in_=lam_q[0:1, :].partition_broadcast(128)), w=['lq'])
            S.dma('sp', lambda e: e.dma_start(out=lk[:], in_=lam_k[0:1, :].partition_broadcast(128)), w=['lk'])
            S.op('dve', lambda e: e.tensor_tensor(out=lq[:], in0=lq[:], in1=lk[:], op=ALU.mult), r=['lq', 'lk'], w=['lq'])
            S.op('dve', lambda e: e.tensor_reduce(out=lsm[:, 0:2], in_=lq[:].rearrange("p (a b) -> p a b", a=2),
                                                  axis=AX.X, op=ALU.add), r=['lq'], w=['lsm'])
            S.op('act', lambda e: e.activation(out=lsm[:, 2:4], in_=lsm[:, 0:2], func=AF.Exp), r=['lsm'], w=['lsm'])
            S.op('dve', lambda e: e.tensor_tensor(out=lsm[:, 4:5], in0=lsm[:, 2:3], in1=lsm[:, 3:4], op=ALU.subtract),
                 r=['lsm'], w=['lsm'])
            S.op('dve', lambda e: e.tensor_scalar(out=lsm[:, 5:6], in0=lsm[:, 4:5], scalar1=LAM_INIT, scalar2=-1.0,
                                                  op0=ALU.add, op1=ALU.mult), r=['lsm'], w=['lsm'])
            nlam = lsm[:, 5:6]
            rbx = sbt("rbx", [33, 4], F32)
            oneh = sbt("oneh", [33, 384], F32)
            vecs = sbt("vecs", [4, 384], F32)
            S.op('dve', lambda e: e.memset(rbx[32:33, :], -30000.0), w=['rbx32'])
            S.dma('sp', lambda e: e.dma_start(out=rbx[0:32, :], in_=rel_bias[:, :]), w=['rbx'])
            S.dma('sp', lambda e: e.dma_start(out=oneh[:], in_=onehot_d[:, :]), w=['oneh'])
            S.dma('sp', lambda e: e.dma_start(out=c31[:, 0:4], in_=rel_bias[31:32, :].partition_broadcast(128)), w=['c31'])
            S.dma('sp', lambda e: e.dma_start(out=anb[:], in_=attn_norm[0:1, :].partition_broadcast(128)), w=['anb'])
            S.dma('sp', lambda e: e.dma_start(out=mnb[:], in_=mlstm_norm[0:1, :].partition_broadcast(128)), w=['mnb'])
            S.op('dve', lambda e: e.tensor_scalar(out=c31[:, 4:8], in0=c31[:, 0:4], scalar1=-1.0, scalar2=None, op0=ALU.mult),
                 r=['c31'], w=['c31n'])
            S.op('pe', lambda e: e.matmul(psum[0:4, 7, 0:384], lhsT=rbx[:, :], rhs=oneh[:, :], start=True, stop=True),
                 r=['rbx', 'rbx32', 'oneh'], w=[('ps', 7)])
            S.op('act', lambda e: e.copy(out=vecs[:], in_=psum[0:4, 7, 0:384]), r=[('ps', 7)], w=['vecs'])
            S.dma('sp', lambda e: e.dma_start(out=vec_s[:, :], in_=vecs[:]), r=['vecs'], w=['vec_s'])
            tzr = etz.enter_context(nc.sbuf_tensor("tzr", [128, 4, 256], F32))
            antiI = etz.enter_context(nc.sbuf_tensor("antiI", [128, 128], F32))
            S.dma('sp', lambda e: e.dma_start(out=antiI[:], in_=tri_d[:, :]), w=['antiI'])
            tz_src = bass.AP(vec_h, 0, [[1, 128], [384, 4], [1, 256]])
            S.dma('sp', lambda e: e.dma_start(out=tzr[:], in_=tz_src), r=['vec_s'], w=['tzr'])
            for half in range(2):
                S.op('pe', lambda e: e.matmul(psum[:, 7, :], lhsT=antiI[:, :],
                                              rhs=tzr[:, 2 * half:2 * half + 2, :].rearrange("p a b -> p (a b)"),
                                              start=True, stop=True), r=['antiI', 'tzr'], w=[('ps', 7)])
                for hh in range(2):
                    h = 2 * half + hh
                    S.op('act', lambda e: e.activation(out=tz[:, h, :], in_=psum[:, 7, hh * 256:(hh + 1) * 256], func=AF.Exp,
                                                       bias=c31[:, 4 + h:5 + h]),
                         r=[('ps', 7), 'c31n'], w=['tz'])
            etz.close()
            S.freed()
            dbg('tz', tz[:, :, :], [128, 4, 256], F32, ['tz'])
            eh = ExitStack()

            def sbm(name, shape, dt):
                return eh.enter_context(nc.sbuf_tensor(name, list(shape), dt))
            dbg('lsm', lsm[:, :], [128, 8], F32, ['lsm'])

            qk = [[sbm(f"qk_{i}_{j}", [128, T], BF16) for j in range(2)] for i in range(2)]
            vt = [sbm(f"vt_{i}", [128, NT, 129], BF16) for i in range(2)]
            pT = [[sbm(f"pT_{i}_{c}", [128, 512], BF16) for c in range(2)] for i in range(2)]
            fin4 = sbm("fin4", [128, 4, 128], F32)
            fin = fin4[:, 0, :]
            osb8 = sbm("osb8", [128, 8, 129], F32)
            osbf = osb8[:, :, :].rearrange("p a b -> p (a b)")
            fsm = sbm("fsm", [128, 8], F32)
            fs4 = sbm("fs4", [128, 16], F32)
            for i in range(2):
                S.op('pool', lambda e: e.memset(vt[i][:], 1.0), w=[('vt', i)])

            PROT = [7, 6, 1, 0]
            prot = [0]

            def pbank():
                prot[0] += 1
                return PROT[prot[0] % 4], ('act' if prot[0] % 2 == 0 else 'dve')

            def evac(eng, out, in_, r, w):
                if eng == 'act':
                    S.op('act', lambda e: e.copy(out=out, in_=in_), r=r, w=w)
                else:
                    S.op('dve', lambda e: e.tensor_copy(out=out, in_=in_), r=r, w=w)

            def proj_fm(dst, dname, wtile, wname, c0):
                for blk in range(5):
                    cc, n = blk_cols(blk)
                    hb = [('hT', tt) for tt in (range(4 * blk, 4 * blk + 4) if blk < 4 else [16])]
                    pbk, eng = pbank()
                    for kc in range(KD):
                        S.op('pe', lambda e, kc=kc: e.matmul(psum[:, pbk, 0:n], lhsT=wtile[:, kc, c0:c0 + 128],
                                                             rhs=hnT[:, kc, cc:cc + n], start=(kc == 0), stop=(kc == KD - 1)),
                             r=[wname] + hb, w=[('ps', pbk)], inc=(kc == KD - 1))
                    evac(eng, dst[:, cc:cc + n], psum[:, pbk, 0:n], [('ps', pbk)], [(dname, blk)])

            def proj_tm(consume, wtile, wname, c0, ncols):
                for tt in range(NT):
                    rows = rows_of(tt)
                    r0 = tt * 128
                    pbk, eng = pbank()
                    for kc in range(KD):
                        S.op('pe', lambda e, kc=kc: e.matmul(psum[:rows, pbk, 0:ncols], lhsT=hnT[:, kc, r0:r0 + rows],
                                                             rhs=wtile[:, kc, c0:c0 + ncols], start=(kc == 0), stop=(kc == KD - 1)),
                             r=[wname, ('hT', tt)], w=[('ps', pbk)], inc=(kc == KD - 1))
                    consume(tt, rows, psum[:rows, pbk, 0:ncols], pbk, eng)

            OPS_BANKS = [0, 1, 6]

            def ops(slot):
                b = OPS_BANKS[slot // 3]
                o = (slot % 3) * 129
                return psum[:, b, o:o + 129], ('ps', b)

            def BC4(ap):
                return ap.rearrange("p (a o) -> p a o", o=1).broadcast_to([128, 4, 128])

            def block_norm(x4, dst, gain, mul3, post, wbufs, extra_r=()):
                j4 = junk[:, 0:512].rearrange("p (a b) -> p a b", a=4)
                S.op('dve', lambda e: e.tensor_tensor(out=j4, in0=x4[:, :, :], in1=x4[:, :, :], op=ALU.mult), r=['fin'], w=['junk'])
                S.op('dve', lambda e: e.tensor_reduce(out=fs4[:, 8:12], in_=j4, axis=AX.X, op=ALU.add), r=['junk'], w=['fs4b'])
                rstd_act(fs4[:, 12:16], fs4[:, 8:12], 128, ['fs4b'], 'fs4c', post=post)
                S.op('dve', lambda e: e.tensor_tensor(out=x4[:, :, :], in0=x4[:, :, :], in1=BC4(fs4[:, 12:16]), op=ALU.mult),
                     r=['fin', 'fs4c'], w=['fin'])
                g3 = gain[:, :].rearrange("p (o d) -> p o d", o=1).broadcast_to([128, 4, 128])
                gname = 'anb' if gain is anb else 'mnb'
                if mul3 is None:
                    S.op('dve', lambda e: e.tensor_tensor(out=dst, in0=x4[:, :, :], in1=g3, op=ALU.mult), r=['fin', gname], w=list(wbufs))
                else:
                    S.op('dve', lambda e: e.tensor_tensor(out=x4[:, :, :], in0=x4[:, :, :], in1=g3, op=ALU.mult), r=['fin', gname], w=['fin'])
                    S.op('dve', lambda e: e.tensor_tensor(out=dst, in0=x4[:, :, :], in1=mul3, op=ALU.mult),
                         r=['fin'] + list(extra_r), w=list(wbufs))

            def attn_head(h, it):
                sl = it % 2
                for j, col0 in enumerate([h * 128, 512 + h * 128, 1024 + h * 128]):
                    S.dma('pool', lambda e: e.dma_start(out=wt[sl][:, :, j * 128:(j + 1) * 128], in_=win_d[:, :, col0:col0 + 128]),
                          w=[('wt', sl, j)])
                qT, kT, V = qk[sl][0], qk[sl][1], vt[sl]
                qn, kn, vn = ('qT', sl), ('kT', sl), ('vt', sl)
                proj_fm(qT, qn, wt[sl], ('wt', sl, 0), 0)
                proj_fm(kT, kn, wt[sl], ('wt', sl, 1), 128)

                def cons_v(tt, rows, ps_ap, pbk, eng):
                    evac(eng, V[:rows, tt, 0:128], ps_ap, [('ps', pbk), vn], [(vn, tt)])
                proj_tm(cons_v, wt[sl], ('wt', sl, 2), 256, 128)
                S.op('dve', lambda e: e.tensor_copy(out=qblk[0:64, :, h, 0:4], in_=qT[0:64, TP:T].rearrange("p (b j) -> p b j", j=4)),
                     r=[(qn, 4), 'qblk'], w=[('qblk', h, 0)])
                S.op('dve', lambda e: e.tensor_copy(out=qblk[64:128, :, h, 4:8], in_=qT[64:128, TP:T].rearrange("p (b j) -> p b j", j=4)),
                     r=[(qn, 4), 'qblk'], w=[('qblk', h, 1)])
                S.op('dve', lambda e: e.tensor_copy(out=ks_s[:, h, :], in_=kT[:, TP:T]), r=[(kn, 4)], w=[('ks_s', h)])
                S.op('dve', lambda e: e.tensor_copy(out=vs_s[:, h, :], in_=V[0:64, 16, 0:128]), r=[(vn, 16)], w=[('vs_s', h)])
                qbufs = lambda q0, n: [(qn, b) for b in range(q0 // 512, (q0 + n - 1) // 512 + 1)]
                iters = [(qb, kt) for qb in range(4) for kt in range(4 * qb + 4)]

                def geom(i):
                    qb, kt = iters[i]
                    qt0 = max(kt, 4 * qb)
                    return qb, kt, qt0, (4 * qb + 4 - qt0) * 128, qt0 * 128, i % 2

                def emit_qk(i):
                    qb, kt, qt0, ncol, q0, pair = geom(i)
                    for c in range(2):
                        bank = 2 + 2 * pair + c
                        S.op('pe', lambda e: e.matmul(psum[:, bank, 0:ncol], lhsT=kT[64 * c:64 * c + 64, kt * 128:(kt + 1) * 128],
                                                      rhs=qT[64 * c:64 * c + 64, q0:q0 + ncol], start=True, stop=True),
                             r=[(kn, kt // 4)] + qbufs(q0, ncol), w=[('ps', bank)])
                    for c in range(2):
                        bank = 2 + 2 * pair + c
                        S.op('act', lambda e: e.activation(out=pT[pair][c][:, 0:ncol], in_=psum[:, bank, 0:ncol], func=AF.Exp,
                                                           bias=c31[:, h:h + 1], scale=0.125),
                             r=[('ps', bank), 'c31'], w=[('pT', pair, c)])
                        if qt0 == kt:
                            S.op('dve', lambda e: e.tensor_tensor(out=pT[pair][c][:, 0:128], in0=pT[pair][c][:, 0:128],
                                                                  in1=tz[:, h, 0:128], op=ALU.mult),
                                 r=[('pT', pair, c), 'tz'], w=[('pT', pair, c)])
                        if qt0 <= kt + 1 <= 4 * qb + 3:
                            off = (kt + 1 - qt0) * 128
                            S.op('dve', lambda e: e.tensor_tensor(out=pT[pair][c][:, off:off + 128], in0=pT[pair][c][:, off:off + 128],
                                                                  in1=tz[:, h, 128:256], op=ALU.mult),
                                 r=[('pT', pair, c), 'tz'], w=[('pT', pair, c)])

                def emit_av(i):
                    qb, kt, qt0, ncol, q0, pair = geom(i)
                    for c in range(2):
                        for qt in range(qt0, 4 * qb + 4):
                            jq = qt - 4 * qb
                            oap, obuf = ops(c * 4 + jq)
                            off = (qt - qt0) * 128
                            S.op('pe', lambda e: e.matmul(oap, lhsT=pT[pair][c][:, off:off + 128], rhs=V[:, kt, :],
                                                          start=(kt == 0 and (c * 4 + jq) % 3 == 0), stop=(kt == qt),
                                                          skip_group_check=True),
                                 r=[('pT', pair, c), (vn, kt)], w=[obuf])
                    if kt == 4 * qb + 3:
                        S.op('dve', lambda e: e.tensor_copy(out=osbf[:, 0:387], in_=psum[:, 0, 0:387]), r=[('ps', 0)], w=['osb8a'])
                        S.op('dve', lambda e: e.tensor_copy(out=osbf[:, 387:774], in_=psum[:, 1, 0:387]), r=[('ps', 1)], w=['osb8b'])
                        S.op('dve', lambda e: e.tensor_copy(out=osbf[:, 774:1032], in_=psum[:, 6, 0:258]), r=[('ps', 6)], w=['osb8c'])
                        ob = ['osb8a', 'osb8b', 'osb8c']
                        t0_ = 4 * qb
                        S.op('dve', lambda e: e.reciprocal(out=fs4[:, 0:8], in_=osb8[:, :, 128]), r=ob, w=['fs4'])
                        S.op('dve', lambda e: e.tensor_scalar(out=fs4[:, 4:8], in0=fs4[:, 4:8], scalar1=nlam, scalar2=None, op0=ALU.mult),
                             r=['fs4', 'lsm'], w=['fs4'])
                        S.op('dve', lambda e: e.tensor_tensor(out=fin4[:, :, :], in0=osb8[:, 0:4, 0:128], in1=BC4(fs4[:, 0:4]), op=ALU.mult),
                             r=ob + ['fs4'], w=['fin'])
                        j3 = junk[:, 512:1024].rearrange("p (a b) -> p a b", a=4)
                        S.op('dve', lambda e: e.tensor_tensor(out=j3, in0=osb8[:, 4:8, 0:128], in1=BC4(fs4[:, 4:8]), op=ALU.mult),
                             r=ob + ['fs4'], w=['junk'])
                        S.op('dve', lambda e: e.tensor_tensor(out=fin4[:, :, :], in0=fin4[:, :, :], in1=j3, op=ALU.add), r=['fin', 'junk'], w=['fin'])
                        block_norm(fin4, mixed[:, t0_:t0_ + 4, h * 128:(h + 1) * 128], anb, None, 1.0 - LAM_INIT,
                                   [('mixed', t0_ + j, h) for j in range(4)])

                for i in range(len(iters) + 1):
                    if i < len(iters):
                        emit_qk(i)
                    if i >= 1:
                        emit_av(i - 1)

            for h in range(4):
                attn_head(h, h)

            SCL = 128.0 ** -0.5
            atm = sbm("atm", [128, NT, 16], F32)
            nAb = sbm("nAb", [128, T], F32)
            selt = sbm("selt", [4, 4, 128], F32)
            trim = sbm("trim_sb", [128, 128], F32)
            identf = sbm("identf_sb", [128, 128], F32)
            mfin = sbm("mfin", [4, 64], F32)
            S.dma('sp', lambda e: e.dma_start(out=selt[:], in_=sel_d[:, :, :]), w=['selt'])
            S.dma('sp', lambda e: e.dma_start(out=trim[:], in_=trim_d[:, :]), w=['trim'])
            S.dma('sp', lambda e: e.dma_start(out=identf[:], in_=identf_d[:, :]), w=['identf'])
            G4 = sbm("G4", [4, T], F32)
            with ExitStack() as eg:
                G1 = eg.enter_context(nc.sbuf_tensor("G1", [4, T], F32))
                G2 = eg.enter_context(nc.sbuf_tensor("G2", [4, T], F32))
                G3 = eg.enter_context(nc.sbuf_tensor("G3", [4, T], F32))
                wgt = eg.enter_context(nc.sbuf_tensor("wgt", [128, KD, 8], BF16))
                bg = eg.enter_context(nc.sbuf_tensor("bg", [4, 4], F32))
                m0T = eg.enter_context(nc.sbuf_tensor("m0T", [4, 16], F32))
                S.dma('pool', lambda e: e.dma_start(out=wgt[:], in_=win_d[:, :, 3584:3592]), w=['wgt'])
                S.dma('sp', lambda e: e.dma_start(out=bg[:, 0:2], in_=b_gates.rearrange("g h -> h g"), allow_slow_non_contiguous=True), w=['bg'])
                S.dma('sp', lambda e: e.dma_start(out=m0T[:], in_=sm_in.rearrange("b h -> h b"), allow_slow_non_contiguous=True), w=['m0T'])
                S.op('dve', lambda e: e.tensor_scalar(out=bg[:, 2:3], in0=bg[:, 1:2], scalar1=-1.0, scalar2=None, op0=ALU.mult),
                     r=['bg'], w=['bgn'])
                for gidx, G in ((0, G1), (1, G2)):
                    for blk in range(5):
                        cc, n = blk_cols(blk)
                        hb = [('hT', tt) for tt in (range(4 * blk, 4 * blk + 4) if blk < 4 else [16])]
                        for kc in range(KD):
                            S.op('pe', lambda e, kc=kc: e.matmul(psum[0:4, 7, 0:n], lhsT=wgt[:, kc, 4 * gidx:4 * gidx + 4],
                                                                 rhs=hnT[:, kc, cc:cc + n], start=(kc == 0), stop=(kc == KD - 1)),
                                 r=['wgt'] + hb, w=[('ps', 7)], inc=(kc == KD - 1))
                        S.op('act', lambda e: e.copy(out=G[:, cc:cc + n], in_=psum[0:4, 7, 0:n]), r=[('ps', 7)], w=[('G', gidx)])
                S.op('act', lambda e: e.activation(out=G2[:, :], in_=G2[:, :], func=AF.Exp, bias=bg[:, 2:3], scale=-1.0),
                     r=[('G', 1), 'bgn'], w=[('G', 1)])
                S.op('act', lambda e: e.activation(out=G2[:, :], in_=G2[:, :], func=AF.Ln, bias=1.0), r=[('G', 1)], w=[('G', 1)])
                S.op('dve', lambda e: e.tensor_tensor_scan(out=G3[:, 0:TP], data0=G2[:, 0:TP], data1=G2[:, 0:TP], initial=0.0,
                                                           op0=ALU.add, op1=ALU.max), r=[('G', 1)], w=[('G', 2)])
                l3 = G2[:, TP:T].rearrange("p (b j) -> p b j", j=4)
                B3 = G3[:, TP:T].rearrange("p (b j) -> p b j", j=4)
                S.op('dve', lambda e: e.tensor_copy(out=B3[:, :, 0], in_=l3[:, :, 0]), r=[('G', 1)], w=[('G', 2)])
                for j in range(1, 4):
                    S.op('dve', lambda e: e.tensor_tensor(out=B3[:, :, j], in0=B3[:, :, j - 1], in1=l3[:, :, j], op=ALU.add),
                         r=[('G', 1), ('G', 2)], w=[('G', 2)])
                S.op('dve', lambda e: e.scalar_tensor_tensor(out=G1[:, :], in0=G1[:, :], scalar=bg[:, 0:1], in1=G3[:, :],
                                                             op0=ALU.add, op1=ALU.add), r=[('G', 0), ('G', 2), 'bg'], w=[('G', 0)])
                S.op('dve', lambda e: e.tensor_tensor_scan(out=G4[:, 0:TP], data0=G1[:, 0:TP], data1=G1[:, 0:TP], initial=0.0,
                                                           op0=ALU.max, op1=ALU.max), r=[('G', 0)], w=[('G', 3)])
                a3 = G1[:, TP:T].rearrange("p (b j) -> p b j", j=4)
                A3 = G4[:, TP:T].rearrange("p (b j) -> p b j", j=4)
                S.op('dve', lambda e: e.tensor_tensor(out=A3[:, :, 0], in0=a3[:, :, 0], in1=m0T[:, :], op=ALU.max),
                     r=[('G', 0), 'm0T'], w=[('G', 3)])
                for j in range(1, 4):
                    S.op('dve', lambda e: e.tensor_tensor(out=A3[:, :, j], in0=A3[:, :, j - 1], in1=a3[:, :, j], op=ALU.max),
                         r=[('G', 0), ('G', 3)], w=[('G', 3)])
                S.op('dve', lambda e: e.tensor_tensor(out=mfin[:, 0:1], in0=G4[:, TP - 1:TP], in1=G3[:, TP - 1:TP], op=ALU.subtract),
                     r=[('G', 2), ('G', 3)], w=['mfin0'])
                S.op('dve', lambda e: e.tensor_tensor(out=mfin[:, 16:32], in0=A3[:, :, 3], in1=B3[:, :, 3], op=ALU.subtract),
                     r=[('G', 2), ('G', 3)], w=['mfin1'])
                S.op('dve', lambda e: e.tensor_tensor(out=mfin[:, 32:48], in0=m0T[:, :], in1=A3[:, :, 3], op=ALU.subtract),
                     r=['m0T', ('G', 3)], w=['mfin2'])
                S.op('act', lambda e: e.activation(out=mfin[:, 32:48], in_=mfin[:, 32:48], func=AF.Exp), r=['mfin2'], w=['mfin2'])
                S.dma('sp', lambda e: e.dma_start(out=mp_out.rearrange("o h -> h o"), in_=mfin[:, 0:1], allow_slow_non_contiguous=True),
                      r=['mfin0'], w=['mp_out'], is_out=True)
                S.dma('sp', lambda e: e.dma_start(out=ms_out.rearrange("b h -> h b"), in_=mfin[:, 16:32], allow_slow_non_contiguous=True),
                      r=['mfin1'], w=['ms_out'], is_out=True)
                S.op('dve', lambda e: e.tensor_tensor(out=G3[:, :], in0=G3[:, :], in1=G4[:, :], op=ALU.subtract),
                     r=[('G', 2), ('G', 3), 'mfin0', 'mfin1'], w=[('G', 2)])
                S.op('act', lambda e: e.activation(out=G3[:, :], in_=G3[:, :], func=AF.Exp), r=[('G', 2)], w=[('G', 2)])
                w3 = G2[:, TP:T].rearrange("p (b j) -> p b j", j=4)
                for j in range(4):
                    S.op('dve', lambda e: e.tensor_tensor(out=w3[:, :, j], in0=m0T[:, :], in1=A3[:, :, j], op=ALU.subtract),
                         r=['m0T', ('G', 3), ('G', 1)], w=[('G', 1)])
                S.op('act', lambda e: e.activation(out=G2[:, TP:T], in_=G2[:, TP:T], func=AF.Exp), r=[('G', 1)], w=[('G', 1)])
                x3 = G2[:, 0:TS].rearrange("p (b j) -> p b j", j=4)
                for j in range(4):
                    S.op('dve', lambda e: e.tensor_tensor(out=x3[:, :, j], in0=a3[:, :, j], in1=A3[:, :, 3], op=ALU.subtract),
                         r=[('G', 0), ('G', 3), ('G', 1)], w=[('G', 1)])
                S.op('act', lambda e: e.activation(out=G2[:, 0:TS], in_=G2[:, 0:TS], func=AF.Exp), r=[('G', 1)], w=[('G', 1)])
                for tt in range(NT):
                    rows = rows_of(tt)
                    r0 = tt * 128
                    srcs = [(G1, 0, ('G', 0), r0), (G3, 4, ('G', 2), r0)]
                    if tt == 16:
                        srcs += [(G2, 8, ('G', 1), r0), (G2, 12, ('G', 1), 0)]
                    for (G, co, gname, c0_) in srcs:
                        S.op('pe', lambda e: e.matmul(psum[:rows, 7, co:co + 4], lhsT=G[0:4, c0_:c0_ + rows], rhs=identf[0:4, 0:4],
                                                      start=True, stop=True), r=[gname, 'identf'], w=[('ps', 7)])
                    ncp = 16 if tt == 16 else 8
                    S.op('act', lambda e: e.copy(out=atm[:rows, tt, 0:ncp], in_=psum[:rows, 7, 0:ncp]), r=[('ps', 7)], w=[('atm', tt)])
                S.op('dve', lambda e: e.tensor_scalar(out=G4[:, :], in0=G4[:, :], scalar1=-1.0, scalar2=None, op0=ALU.mult),
                     r=[('G', 3)], w=[('G', 3)])
                dbg('G1', G1[:, :], [4, T], F32, [('G', 0)])
                dbg('G4', G4[:, :], [4, T], F32, [('G', 3)])
                dbg('G3', G3[:, :], [4, T], F32, [('G', 2)])

            S.freed()
            ktm = [sbm(f"ktm_{i}", [128, NT, 128], BF16) for i in range(2)]
            sig = [sbm(f"sig_{i}", [128, NT, 128], BF16) for i in range(2)]
            dtl = [sbm(f"dt_{i}", [128, 512], F32) for i in range(2)]
            kw = sbm("kw", [128, 128], BF16)
            wsc = sbm("wsc", [128, 16], F32)
            cst = sbm("cst", [128, 129], F32)

            fwb = sbm("fwb", [128, 64], F32)
            for h in range(4):
                S.op('pe', lambda e: e.matmul(psum[:, 7, 0:16], lhsT=selt[0:4, h, :], rhs=mfin[0:4, 32:48], start=True, stop=True),
                     r=['selt', 'mfin2'], w=[('ps', 7)])
                S.op('act', lambda e: e.copy(out=fwb[:, h * 16:(h + 1) * 16], in_=psum[:, 7, 0:16]), r=[('ps', 7)], w=[('fwb', h)])
            C0x = sbm("C0x", [128, 16, 130], BF16)
            snt = sbm("snt", [64, 128], F32)
            n0f = sbm("n0f", [128, 64], F32)
            nnew = sbm("nnew", [128, 64], F32)
            bmask = sbm("bmask_sb", [64, 64], F32)
            rowsel = sbm("rowsel_sb", [64, 16], F32)
            qmask = sbm("qmask", [128, 16, 64], BF16)
            kwm = sbm("kwm", [64, 16, 128], BF16)
            kws = sbm("kws", [64, 128], BF16)
            c0f = [sbm(f"c0f_{i}", [128, 128], F32) for i in range(2)]
            cnew = [sbm(f"cnew_{i}", [128, 129], F32) for i in range(2)]
            dts = sbm("dts", [64, 64], F32)
            pTs = sbm("pTs", [64, 64], BF16)
            ist = sbm("ist", [64, 129], F32)
            sC_v = sC_in.rearrange("bh k v -> k bh v")
            sC_h = sC_in.rearrange("(b h) k v -> k h b v", h=4)
            S.dma('sp', lambda e: e.dma_start(out=snt[:], in_=sn_in[:, :]), w=['snt'])
            S.op('pe', lambda e: e.matmul(psum[:, 7, 0:64], lhsT=snt[:, :], rhs=identf[0:64, 0:64], start=True, stop=True),
                 r=['snt', 'identf'], w=[('ps', 7)])
            S.op('act', lambda e: e.copy(out=n0f[:, :], in_=psum[:, 7, 0:64]), r=[('ps', 7)], w=['n0f'])
            S.dma('sp', lambda e: e.dma_start(out=bmask[:], in_=bmask_d[:, :]), w=['bmask'])
            S.dma('sp', lambda e: e.dma_start(out=rowsel[:], in_=rowsel_d[:, :]), w=['rowsel'])
            S.op('dve', lambda e: e.memset(qmask[:], 0.0), w=['qmask'])

            def mlstm_sample(h, sl):
                qT, kT, V = qk[sl][0], qk[sl][1], vt[sl]
                qn, kn, vn = ('qT', sl), ('kT', sl), ('vt', sl)
                if 'ms_a' not in SKIP:
                    mlstm_sample_a(h, sl)
                mlstm_sample_b(h, sl)

            def mlstm_sample_a(h, sl):
                qT, kT, V = qk[sl][0], qk[sl][1], vt[sl]
                qn, kn, vn = ('qT', sl), ('kT', sl), ('vt', sl)
                S.op('pe', lambda e: e.matmul(psum[0:64, 2, 0:64], lhsT=kT[:, TP:T], rhs=qT[:, TP:T], start=True, stop=True),
                     r=[(kn, 4), (qn, 4)], w=[('ps', 2)])
                S.op('act', lambda e: e.activation(out=dts[:, :], in_=nAb[0:64, TP:T], func=AF.Exp, bias=atm[0:64, 16, h:h + 1]),
                     r=[('nAb', 4), ('atm', 16)], w=['dts'])
                S.op('dve', lambda e: e.tensor_tensor(out=dts[:, :], in0=dts[:, :], in1=bmask[:, :], op=ALU.mult), r=['dts', 'bmask'], w=['dts'])
                S.op('dve', lambda e: e.scalar_tensor_tensor(out=pTs[:, :], in0=psum[0:64, 2, 0:64], scalar=SCL, in1=dts[:, :],
                                                             op0=ALU.mult, op1=ALU.mult), r=[('ps', 2), 'dts'], w=['pTs'])
                S.op('pe', lambda e: e.matmul(psum[0:64, 3, 0:129], lhsT=pTs[:, :], rhs=V[0:64, 16, :], start=True, stop=True),
                     r=['pTs', (vn, 16)], w=[('ps', 3)])
                for b in range(16):
                    S.op('dve', lambda e: e.tensor_copy(out=qmask[:, b, 4 * b:4 * b + 4], in_=qT[:, TP + 4 * b:TP + 4 * b + 4]),
                         r=[(qn, 4), 'qmask'], w=[('qmask', b)])
                for g in range(2):
                    S.dma('pool', lambda e: e.dma_start(out=C0x[:, g * 8:(g + 1) * 8, 0:128], in_=sC_h[:, h, g * 8:(g + 1) * 8, :]),
                          w=[('C0x', g)])
                S.op('act', lambda e: e.copy(out=C0x[:, :, 128], in_=n0f[:, :].rearrange("p (b h) -> p h b", h=4)[:, h, :]),
                     r=['n0f'], w=['C0xn'])
                for b in range(16):
                    S.op('pe', lambda e: e.matmul(psum[0:64, 4, 0:129], lhsT=qmask[:, b, :], rhs=C0x[:, b, 0:129],
                                                  start=(b == 0), stop=(b == 15)),
                         r=[('qmask', b), ('C0x', b // 8), 'C0xn'], w=[('ps', 4)], inc=(b == 15))
                S.op('act', lambda e: e.copy(out=ist[:, :], in_=psum[0:64, 4, 0:129]), r=[('ps', 4)], w=['ist'])
                S.op('dve', lambda e: e.scalar_tensor_tensor(out=cst[0:64, :], in0=ist[:, :], scalar=atm[0:64, 16, 8 + h:9 + h],
                                                             in1=psum[0:64, 3, 0:129], op0=ALU.mult, op1=ALU.add),
                     r=['ist', ('atm', 16), ('ps', 3)], w=['cst'])
                mlstm_finalize(h, sl, 16, 64, cst, ['cst'])

            def mlstm_sample_b(h, sl):
                qT, kT, V = qk[sl][0], qk[sl][1], vt[sl]
                qn, kn, vn = ('qT', sl), ('kT', sl), ('vt', sl)
                if 'ms_b' in SKIP:
                    return
                S.op('dve', lambda e: e.tensor_scalar(out=kws[:, :], in0=ktm[sl][0:64, 16, :], scalar1=atm[0:64, 16, 12 + h:13 + h], scalar2=SCL,
                                                      op0=ALU.mult, op1=ALU.mult), r=[('ktm', sl, 16), ('atm', 16)], w=['kws'])
                for b in range(16):
                    S.op('dve', lambda e: e.tensor_scalar(out=kwm[:, b, :], in0=kws[:, :], scalar1=rowsel[:, b:b + 1], scalar2=None, op0=ALU.mult),
                         r=['kws', 'rowsel'], w=[('kwm', b)])
                for b in range(16):
                    cb = b % 2
                    bank = 5 + cb
                    bh = b * 4 + h
                    S.dma('sp', lambda e: e.dma_start(out=c0f[cb][:], in_=sC_in[bh]), w=[('c0f', cb)])
                    S.op('pe', lambda e: e.matmul(psum[:, bank, 0:129], lhsT=kwm[:, b, :], rhs=V[0:64, 16, :], start=True, stop=True),
                         r=[('kwm', b), (vn, 16)], w=[('ps', bank)])
                    S.op('dve', lambda e: e.scalar_tensor_tensor(out=cnew[cb][:, 0:128], in0=c0f[cb][:, :], scalar=fwb[:, h * 16 + b:h * 16 + b + 1],
                                                                 in1=psum[:, bank, 0:128], op0=ALU.mult, op1=ALU.add),
                         r=[('c0f', cb), ('fwb', h), ('ps', bank)], w=[('cnew', cb)])
                    S.op('dve', lambda e: e.scalar_tensor_tensor(out=nnew[:, bh:bh + 1], in0=n0f[:, bh:bh + 1], scalar=fwb[:, h * 16 + b:h * 16 + b + 1],
                                                                 in1=psum[:, bank, 128:129], op0=ALU.mult, op1=ALU.add),
                         r=['n0f', ('fwb', h), ('ps', bank)], w=[('nnew', bh)])
                    S.dma('sp', lambda e: e.dma_start(out=cs_out[bh], in_=cnew[cb][:, 0:128]), r=[('cnew', cb)], w=[('cs_out', bh)], is_out=True)


            def mlstm_head(h, it):
                sl = it % 2
                for j, col0 in enumerate([1536 + h * 128, 2048 + h * 128, 2560 + h * 128, 3072 + h * 128]):
                    S.dma('pool', lambda e: e.dma_start(out=wt[sl][:, :, j * 128:(j + 1) * 128], in_=win_d[:, :, col0:col0 + 128]),
                          w=[('wt', sl, j)])
                qT, kT, V = qk[sl][0], qk[sl][1], vt[sl]
                qn, kn, vn = ('qT', sl), ('kT', sl), ('vt', sl)
                for blk in range(5):
                    cc, n = blk_cols(blk)
                    S.op('pe', lambda e: e.matmul(psum[:, 7, 0:n], lhsT=selt[0:4, h, :], rhs=G4[0:4, cc:cc + n], start=True, stop=True),
                         r=['selt', ('G', 3)], w=[('ps', 7)])
                    S.op('act', lambda e: e.copy(out=nAb[:, cc:cc + n], in_=psum[:, 7, 0:n]), r=[('ps', 7)], w=[('nAb', blk)])
                proj_fm(qT, qn, wt[sl], ('wt', sl, 0), 0)
                proj_fm(kT, kn, wt[sl], ('wt', sl, 1), 128)

                def cons_k(tt, rows, ps_ap, pbk, eng):
                    evac(eng, ktm[sl][:rows, tt, :], ps_ap, [('ps', pbk)], [('ktm', sl, tt)])
                proj_tm(cons_k, wt[sl], ('wt', sl, 1), 128, 128)

                def cons_v(tt, rows, ps_ap, pbk, eng):
                    evac(eng, V[:rows, tt, 0:128], ps_ap, [('ps', pbk), vn], [(vn, tt)])
                proj_tm(cons_v, wt[sl], ('wt', sl, 2), 256, 128)

                def cons_o(tt, rows, ps_ap, pbk, eng):
                    S.op('act', lambda e: e.activation(out=sig[sl][:rows, tt, :], in_=ps_ap, func=AF.Sigmoid),
                         r=[('ps', pbk)], w=[('sig', sl, tt)])
                proj_tm(cons_o, wt[sl], ('wt', sl, 3), 384, 128)
                qbufs = lambda q0, n: [(qn, b) for b in range(q0 // 512, (q0 + n - 1) // 512 + 1)]
                nbufs = lambda q0, n: [('nAb', b) for b in range(q0 // 512, (q0 + n - 1) // 512 + 1)]
                iters = [(qb, kt) for qb in range(4) for kt in range(4 * qb + 4)]

                def geom(i):
                    qb, kt = iters[i]
                    qt0 = max(kt, 4 * qb)
                    return qb, kt, qt0, (4 * qb + 4 - qt0) * 128, qt0 * 128, 2 + i % 4, i % 2

                def emit_qk(i):
                    qb, kt, qt0, ncol, q0, bank, db = geom(i)
                    S.op('pe', lambda e: e.matmul(psum[:, bank, 0:ncol], lhsT=kT[:, kt * 128:(kt + 1) * 128],
                                                  rhs=qT[:, q0:q0 + ncol], start=True, stop=True),
                         r=[(kn, kt // 4)] + qbufs(q0, ncol), w=[('ps', bank)])
                    S.op('act', lambda e: e.activation(out=dtl[db][:, 0:ncol], in_=nAb[:, q0:q0 + ncol], func=AF.Exp,
                                                       bias=atm[:, kt, h:h + 1]),
                         r=nbufs(q0, ncol) + [('atm', kt)], w=[('dt', db)])
                    if qt0 == kt:
                        S.op('dve', lambda e: e.tensor_tensor(out=dtl[db][:, 0:128], in0=dtl[db][:, 0:128], in1=trim[:, :], op=ALU.mult),
                             r=[('dt', db), 'trim'], w=[('dt', db)])
                    S.op('dve', lambda e: e.scalar_tensor_tensor(out=pT[db][0][:, 0:ncol], in0=psum[:, bank, 0:ncol], scalar=SCL,
                                                                 in1=dtl[db][:, 0:ncol], op0=ALU.mult, op1=ALU.mult),
                         r=[('ps', bank), ('dt', db)], w=[('pT', db, 0)])

                def emit_av(i):
                    qb, kt, qt0, ncol, q0, bank, db = geom(i)
                    for qt in range(qt0, 4 * qb + 4):
                        jq = qt - 4 * qb
                        oap, obuf = ops(jq)
                        off = (qt - qt0) * 128
                        S.op('pe', lambda e: e.matmul(oap, lhsT=pT[db][0][:, off:off + 128], rhs=V[:, kt, :],
                                                      start=(kt == 0 and jq % 3 == 0), stop=(kt == qt), skip_group_check=True),
                             r=[('pT', db, 0), (vn, kt)], w=[obuf])
                    if kt == 4 * qb + 3:
                        S.op('dve', lambda e: e.tensor_copy(out=osbf[:, 0:387], in_=psum[:, 0, 0:387]), r=[('ps', 0)], w=['osb8a'])
                        S.op('dve', lambda e: e.tensor_copy(out=osbf[:, 387:516], in_=psum[:, 1, 0:129]), r=[('ps', 1)], w=['osb8b'])
                        ob = ['osb8a', 'osb8b']
                        t0_ = 4 * qb
                        S.op('dve', lambda e: e.tensor_scalar(out=fs4[:, 0:4], in0=osb8[:, 0:4, 128], scalar1=-1.0, scalar2=None, op0=ALU.mult),
                             r=ob, w=['fs4'])
                        S.op('dve', lambda e: e.tensor_tensor(out=fs4[:, 0:4], in0=fs4[:, 0:4], in1=osb8[:, 0:4, 128], op=ALU.max),
                             r=ob + ['fs4'], w=['fs4'])
                        S.op('dve', lambda e: e.tensor_tensor(out=fs4[:, 0:4], in0=fs4[:, 0:4], in1=atm[:, t0_:t0_ + 4, 4 + h], op=ALU.max),
                             r=['fs4'] + [('atm', t0_ + j) for j in range(4)], w=['fs4'])
                        S.op('dve', lambda e: e.reciprocal(out=fs4[:, 0:4], in_=fs4[:, 0:4]), r=['fs4'], w=['fs4'])
                        S.op('dve', lambda e: e.tensor_tensor(out=fin4[:, :, :], in0=osb8[:, 0:4, 0:128], in1=BC4(fs4[:, 0:4]), op=ALU.mult),
                             r=ob + ['fs4'], w=['fin'])
                        block_norm(fin4, mixed[:, t0_:t0_ + 4, 512 + h * 128:512 + (h + 1) * 128], mnb, sig[sl][:, t0_:t0_ + 4, :], 1.0,
                                   [('mixed', t0_ + j, 4 + h) for j in range(4)], extra_r=[('sig', sl, t0_ + j) for j in range(4)])

                if 'mloop' not in SKIP:
                    for i in range(len(iters) + 1):
                        if i < len(iters):
                            emit_qk(i)
                        if i >= 1:
                            emit_av(i - 1)
                S.op('act', lambda e: e.activation(out=wsc[:, 0:16], in_=atm[:, 0:16, h], func=AF.Exp, bias=nAb[:, TP - 1:TP]),
                     r=[('atm', tt) for tt in range(16)] + [('nAb', 3)], w=['wsc'])
                for kt in range(16 if 'mstate' not in SKIP else 0):
                    S.op('dve', lambda e: e.tensor_scalar(out=kw[:, :], in0=ktm[sl][:, kt, :], scalar1=wsc[:, kt:kt + 1], scalar2=SCL,
                                                          op0=ALU.mult, op1=ALU.mult),
                         r=[('ktm', sl, kt), 'wsc'], w=['kw'])
                    S.op('pe', lambda e: e.matmul(psum[:, 6, 0:129], lhsT=kw[:, :], rhs=V[:, kt, :], start=(kt == 0), stop=(kt == 15)),
                         r=['kw', (vn, kt)], w=[('ps', 6)])
                S.op('act', lambda e: e.copy(out=cst[:, :], in_=psum[:, 6, 0:129]), r=[('ps', 6)], w=['cst'])
                S.dma('sp', lambda e: e.dma_start(out=cp_out[h], in_=cst[:, 0:128]), r=['cst'], w=[('cp_out', h)], is_out=True)
                S.dma('sp', lambda e: e.dma_start(out=np_out[h:h + 1, :].rearrange("o d -> d o"), in_=cst[:, 128:129],
                                                  allow_slow_non_contiguous=True), r=['cst'], w=[('np_out', h)], is_out=True)
                if 'msample' not in SKIP:
                    mlstm_sample(h, sl)

            def mlstm_finalize(h, sl, tt, rows, oap, obufs):
                S.op('dve', lambda e: e.tensor_scalar(out=fsm[:rows, 5:6], in0=oap[:rows, 128:129], scalar1=-1.0, scalar2=None, op0=ALU.mult),
                     r=obufs, w=['fsm5'])
                S.op('dve', lambda e: e.tensor_tensor(out=fsm[:rows, 4:5], in0=oap[:rows, 128:129], in1=fsm[:rows, 5:6], op=ALU.max),
                     r=obufs + ['fsm5'], w=['fsm4'])
                S.op('dve', lambda e: e.tensor_tensor(out=fsm[:rows, 4:5], in0=fsm[:rows, 4:5], in1=atm[:rows, tt, 4 + h:5 + h], op=ALU.max),
                     r=['fsm4', ('atm', tt)], w=['fsm4'])
                S.op('dve', lambda e: e.reciprocal(out=fsm[:rows, 4:5], in_=fsm[:rows, 4:5]), r=['fsm4'], w=['fsm4'])
                S.op('dve', lambda e: e.tensor_scalar(out=fin[:rows, :], in0=oap[:rows, 0:128], scalar1=fsm[:rows, 4:5], scalar2=None, op0=ALU.mult),
                     r=obufs + ['fsm4'], w=['fin'])
                sumsq(fsm[:rows, 2:3], fin[:rows, :], rows, 128, ['fin'], 'fsm2')
                rstd_act(fsm[:rows, 3:4], fsm[:rows, 2:3], 128, ['fsm2'], 'fsm3')
                S.op('dve', lambda e: e.scalar_tensor_tensor(out=fin[:rows, :], in0=fin[:rows, :], scalar=fsm[:rows, 3:4], in1=mnb[:rows, :],
                                                             op0=ALU.mult, op1=ALU.mult), r=['fin', 'fsm3', 'mnb'], w=['fin'])
                S.op('dve', lambda e: e.tensor_tensor(out=mixed[:rows, tt, 512 + h * 128:512 + (h + 1) * 128], in0=fin[:rows, :],
                                                      in1=sig[sl][:rows, tt, :], op=ALU.mult),
                     r=['fin', ('sig', sl, tt)], w=[('mixed', tt, 4 + h)])

            for h in range(4):
                if 'mheads' not in SKIP:
                    mlstm_head(h, h)
            if 'mheads' not in SKIP and 'msample' not in SKIP and 'ms_b' not in SKIP:
                S.op('pe', lambda e: e.matmul(psum[0:64, 7, 0:128], lhsT=nnew[:, :], rhs=identf[:, :], start=True, stop=True),
                     r=[('nnew', bh) for bh in range(64)] + ['identf'], w=[('ps', 7)])
                S.op('act', lambda e: e.copy(out=snt[:, :], in_=psum[0:64, 7, 0:128]), r=[('ps', 7)], w=['snt'])
                S.dma('sp', lambda e: e.dma_start(out=ns_out[:, :], in_=snt[:, :]), r=['snt'], w=['ns_out'], is_out=True)
            eh.close()
            S.freed()
            if stage > 4:
                sample_attn(hnT, qblk, ks_s, vs_s, tz, lsm, anb)
                S.freed()
            dbg('mixed', mixed[:, :, :], [128, NT, D], BF16, [('mixed', tt, h) for tt in range(16) for h in range(8)])
            dbg('hnT', hnT[:, :, :], [128, KD, T], BF16, [('hT', 16)])
            if stage > 3:
                out_proj(em, hnT, mixed, wt)

        def sample_attn(hnT, qblk, ks_s, vs_s, tz, lsm, anb):
            nlam = lsm[:, 5:6]
            with ExitStack() as ea:
                def sba(name, shape, dt):
                    return ea.enter_context(nc.sbuf_tensor(name, list(shape), dt))
                ptb = sba("ptb", [128, 256], I32)
                iot = sba("iot", [128, 256], I32)
                idx = sba("idx", [128, 256], I32)
                identf = sba("identf_sa", [128, 128], F32)
                rmod = sba("rmod_sb", [4, 64], F32)
                rowsel = sba("rowsel_sa", [64, 16], F32)
                tzN = sba("tzN", [4, 32], F32)
                tmpN = sba("tmpN", [64, 32], F32)
                corrN = sba("corrN", [64, 16, 32], F32)
                corr15 = sba("corr15", [128, 32], F32)
                Kb = [sba(f"Kb_{i}", [128, 16, 512], BF16) for i in range(2)]
                Vb = [sba(f"Vb_{i}", [128, 16, 512], BF16) for i in range(2)]
                onesb = sba("onesb", [128, 2], BF16)
                sel01 = sba("sel01_sb", [32, 32], F32)
                dsel = sba("dsel", [32, 16], F32)
                bm16 = sba("bm16_sb", [16, 512], F32)
                anb4 = sba("anb4", [16, 512], F32)
                rs32 = sba("rs32", [32, 2], F32)
                on32 = sba("on32", [32, 512], F32)
                a16 = sba("a16", [16, 512], F32)
                rs16 = sba("rs16", [16, 4], F32)
                kTb = sba("kTb", [128, 4, 2048], BF16)
                pTb = sba("pTb", [128, 512], BF16)
                pTn = sba("pTn", [64, 32], BF16)
                S.dma('sp', lambda e: e.dma_start(out=ptb[:], in_=pt_in[0:1, :].partition_broadcast(128)), w=['ptb'])
                S.dma('sp', lambda e: e.dma_start(out=identf[:], in_=identf_d[:, :]), w=['identf'])
                S.dma('sp', lambda e: e.dma_start(out=rmod[:], in_=rmod_d[:, :]), w=['rmod'])
                S.dma('sp', lambda e: e.dma_start(out=rowsel[:], in_=rowsel_d[:, :]), w=['rowsel'])
                S.op('pool', lambda e: e.iota(iot[:], pattern=[[0, 256]], base=0, channel_multiplier=1), w=['iot'])
                S.op('dve', lambda e: e.scalar_tensor_tensor(out=idx[:], in0=ptb[:], scalar=128, in1=iot[:], op0=ALU.mult, op1=ALU.add),
                     r=['ptb', 'iot'], w=['idx'])
                S.op('dve', lambda e: e.memset(onesb[:], 1.0), w=['onesb'])
                S.dma('sp', lambda e: e.dma_start(out=sel01[:], in_=sel01_d[:, :]), w=['sel01'])
                S.dma('sp', lambda e: e.dma_start(out=bm16[:], in_=bm16_d[:, :]), w=['bm16'])
                for h in range(4):
                    S.dma('sp', lambda e: e.dma_start(out=anb4[:, h * 128:(h + 1) * 128], in_=attn_norm[0:1, :].partition_broadcast(16)),
                          w=['anb4'])
                S.op('dve', lambda e: e.scalar_tensor_tensor(out=dsel[:, :], in0=sel01[:, 16:32], scalar=nlam[0:32, :], in1=sel01[:, 0:16],
                                                             op0=ALU.mult, op1=ALU.add), r=['sel01', 'lsm'], w=['dsel'])
                for h in range(4):
                    for c in range(2):
                        S.op('dve', lambda e: e.tensor_copy(out=corr15[:, h * 8 + c * 4:h * 8 + c * 4 + 4], in_=tz[:, h, 128:132]),
                             r=['tz'], w=[('corr15', h, c)])
                        S.op('dve', lambda e: e.tensor_copy(out=tzN[:, h * 8 + c * 4:h * 8 + c * 4 + 4], in_=tz[0:4, h, 0:4]),
                             r=['tz'], w=[('tzN', h, c)])
                S.op('pe', lambda e: e.matmul(psum[0:64, 7, 0:32], lhsT=rmod[:, :], rhs=tzN[:, :], start=True, stop=True),
                     r=['rmod'] + [('tzN', h, c) for h in range(4) for c in range(2)], w=[('ps', 7)])
                S.op('act', lambda e: e.copy(out=tmpN[:, :], in_=psum[0:64, 7, 0:32]), r=[('ps', 7)], w=['tmpN'])
                for b in range(16):
                    S.op('dve', lambda e: e.tensor_scalar(out=corrN[:, b, :], in0=tmpN[:, :], scalar1=rowsel[:, b:b + 1], scalar2=None, op0=ALU.mult),
                         r=['tmpN', 'rowsel'], w=[('corrN', b)])
                c15 = [('corr15', h, c) for h in range(4) for c in range(2)]

                def gather(b):
                    sl = b % 2
                    for pg in range(16):
                        col = b * 16 + pg
                        S.dma('pool', lambda e: e.indirect_dma_start(out=Kb[sl][:, pg, :], out_offset=None, in_=ck_in[:, :],
                                                                      in_offset=bass.IndirectOffsetOnAxis(ap=idx[:, col:col + 1], axis=0)),
                              r=['idx'], w=[('Kb', sl, pg)])
                        S.dma('pool', lambda e: e.indirect_dma_start(out=Vb[sl][:, pg, :], out_offset=None, in_=cv_in[:, :],
                                                                      in_offset=bass.IndirectOffsetOnAxis(ap=idx[:, col:col + 1], axis=0)),
                              r=['idx'], w=[('Vb', sl, pg)])

                gather(0)
                for b in range(16):
                    sl = b % 2
                    if b + 1 < 16:
                        gather(b + 1)
                    tb = 0
                    for h in range(4):
                        for half in range(2):
                            bank = tb % 2
                            tb += 1
                            pv = psb16(bank)
                            for p8 in range(8):
                                pg = half * 8 + p8
                                S.op('pe', lambda e: e.transpose(out=pv[:, p8 * 128:(p8 + 1) * 128], in_=Kb[sl][:, pg, h * 128:(h + 1) * 128],
                                                                 identity=identb[:, :]),
                                     r=[('Kb', sl, pg), 'identb'], w=[('ps', bank)], inc=(p8 == 7))
                            S.op('act', lambda e: e.copy(out=kTb[:, h, half * 1024:(half + 1) * 1024], in_=pv[:, 0:1024]),
                                 r=[('ps', bank)], w=[('kTb', h, half)])
                    for h in range(4):
                        for pg in range(16):
                            S.op('pe', lambda e: e.matmul(psum[:, 2, pg * 32 + h * 8:pg * 32 + h * 8 + 8], lhsT=kTb[:, h, pg * 128:(pg + 1) * 128],
                                                          rhs=qblk[:, b, h, :], start=True, stop=True, skip_group_check=True),
                                 r=[('kTb', h, pg // 8), ('qblk', h, 0), ('qblk', h, 1)], w=[('ps', 2)], inc=(h == 3 and pg == 15))
                    for h in range(4):
                        S.op('pe', lambda e: e.matmul(psum[0:64, 3, h * 8:h * 8 + 8], lhsT=ks_s[:, h, :], rhs=qblk[:, b, h, :],
                                                      start=True, stop=True, skip_group_check=True),
                             r=[('ks_s', h), ('qblk', h, 0), ('qblk', h, 1)], w=[('ps', 3)], inc=(h == 3))
                    S.op('act', lambda e: e.activation(out=pTb[:, :], in_=psum[:, 2, :], func=AF.Exp, scale=0.125), r=[('ps', 2)], w=['pTb'])
                    S.op('dve', lambda e: e.tensor_tensor(out=pTb[:, 480:512], in0=pTb[:, 480:512], in1=corr15[:, :], op=ALU.mult),
                         r=['pTb'] + c15, w=['pTb'])
                    S.op('act', lambda e: e.activation(out=pTn[:, :], in_=psum[0:64, 3, 0:32], func=AF.Exp, scale=0.125), r=[('ps', 3)], w=['pTn'])
                    S.op('dve', lambda e: e.tensor_tensor(out=pTn[:, :], in0=pTn[:, :], in1=corrN[:, b, :], op=ALU.mult),
                         r=['pTn', ('corrN', b)], w=['pTn'])
                    for pg in range(16):
                        S.op('pe', lambda e: e.matmul(psum[0:32, 4, :], lhsT=pTb[:, pg * 32:(pg + 1) * 32], rhs=Vb[sl][:, pg, :],
                                                      start=(pg == 0), stop=False), r=['pTb', ('Vb', sl, pg)], w=[('ps', 4)], inc=False)
                    S.op('pe', lambda e: e.matmul(psum[0:32, 4, :], lhsT=pTn[:, :], rhs=vs_s[:, :, :].rearrange("p h d -> p (h d)"),
                                                  start=False, stop=True), r=['pTn'] + [('vs_s', h) for h in range(4)], w=[('ps', 4)])
                    for pg in range(16):
                        S.op('pe', lambda e: e.matmul(psum[0:32, 5, 0:1], lhsT=pTb[:, pg * 32:(pg + 1) * 32], rhs=onesb[:, 0:1],
                                                      start=(pg == 0), stop=False), r=['pTb', 'onesb'], w=[('ps', 5)], inc=False)
                    S.op('pe', lambda e: e.matmul(psum[0:32, 5, 0:1], lhsT=pTn[:, :], rhs=onesb[0:64, 0:1], start=False, stop=True),
                         r=['pTn', 'onesb'], w=[('ps', 5)])
                    S.op('dve', lambda e: e.reciprocal(out=rs32[:, 0:1], in_=psum[0:32, 5, 0:1]), r=[('ps', 5)], w=['rs32'])
                    S.op('dve', lambda e: e.tensor_scalar(out=on32[:, :], in0=psum[0:32, 4, :], scalar1=rs32[:, 0:1], scalar2=None, op0=ALU.mult),
                         r=[('ps', 4), 'rs32'], w=['on32'])
                    S.op('pe', lambda e: e.matmul(psum[0:16, 6, :], lhsT=dsel[:, :], rhs=on32[:, :], start=True, stop=True),
                         r=['dsel', 'on32'], w=[('ps', 6)])
                    S.op('dve', lambda e: e.tensor_tensor(out=a16[:, :], in0=psum[0:16, 6, :], in1=bm16[:, :], op=ALU.mult),
                         r=[('ps', 6), 'bm16'], w=['a16'])
                    sumsq(rs16[:, 0:1], a16[:, :], 16, 512, ['a16'], 'rs16a')
                    rstd_act(rs16[:, 1:2], rs16[:, 0:1], 128, ['rs16a'], 'rs16b', post=1.0 - LAM_INIT)
                    S.op('dve', lambda e: e.scalar_tensor_tensor(out=a16[:, :], in0=a16[:, :], scalar=rs16[:, 1:2], in1=anb4[:, :],
                                                                 op0=ALU.mult, op1=ALU.mult), r=['a16', 'rs16b', 'anb4'], w=['a16'])
                    for h in range(4):
                        S.op('pe', lambda e: e.matmul(psum[:, 7, h * 16:(h + 1) * 16], lhsT=a16[:, h * 128:(h + 1) * 128], rhs=identf[0:16, 0:16],
                                                      start=True, stop=True, skip_group_check=True),
                             r=['a16', 'identf'], w=[('ps', 7)], inc=(h == 3))
                    S.op('act', lambda e: e.copy(out=hnT[:, 0:4, TP + 4 * b:TP + 4 * b + 4],
                                                 in_=psum[:, 7, 0:80].rearrange("p (h x) -> p h x", x=20)[:, :, 0:4]),
                         r=[('ps', 7)], w=[('hTs', b)])

        def out_proj(em, hnT, mixed, wt):
            xr = [em.enter_context(nc.sbuf_tensor(f"xro_{i}", [128, D], F32)) for i in range(2)]
            tmp1 = em.enter_context(nc.sbuf_tensor("tmpo", [128, D], F32))
            wo_d = w_out.rearrange("(kc p) f -> p kc f", p=128)
            for half in range(2):
                S.dma('pool', lambda e: e.dma_start(out=wt[half][:], in_=wo_d[:, :, half * 512:(half + 1) * 512]),
                      w=[('wo', half)] + [('wt', half, j) for j in range(4)])
            for tt in range(NT):
                rows = rows_of(tt)
                r0 = tt * 128
                bank = tt % 2
                pv = psb16(bank)
                c_lo = 4 if (tt == 16 and stage > 4) else 0
                for c in range(c_lo, KD):
                    S.op('pe', lambda e, c=c: e.transpose(out=pv[:, c * 128:c * 128 + rows], in_=mixed[:rows, tt, c * 128:(c + 1) * 128],
                                                          identity=identb[:rows, :rows]),
                         r=[('mixed', tt, c), 'identb'], w=[('ps', bank)], inc=(c == KD - 1))
                srcv = pv[:, 0:1024].rearrange("p (c r) -> p c r", r=128)[:, c_lo:KD, 0:rows]
                S.op('act', lambda e: e.copy(out=hnT[:, c_lo:KD, r0:r0 + rows], in_=srcv),
                     r=[('ps', bank)] + ([('hTs', b) for b in range(16)] if c_lo else []), w=[('hT', tt)])
            for tt in range(NT):
                rows = rows_of(tt)
                r0 = tt * 128
                yb = tt % 2
                S.dma('sp', lambda e: e.dma_start(out=xr[yb][:rows, :], in_=x1_s[r0:r0 + rows, :]), r=[('dram', id(x1_s), tt)], w=[('xro', yb)])
                for half in range(2):
                    bank = 4 + 2 * yb + half
                    for kc in range(KD):
                        S.op('pe', lambda e, kc=kc: e.matmul(psum[:rows, bank, :], lhsT=hnT[:, kc, r0:r0 + rows], rhs=wt[half][:, kc, :],
                                                             start=(kc == 0), stop=(kc == KD - 1)),
                             r=[('hT', tt), ('wo', half)], w=[('ps', bank)], inc=(kc == KD - 1))
                b0 = 4 + 2 * yb
                S.op('act', lambda e: e.copy(out=tmp1[:rows, :].rearrange("p (a b) -> p a b", a=2), in_=psum[:rows, b0:b0 + 2, :]),
                     r=[('ps', b0), ('ps', b0 + 1)], w=['tmpo'])
                sumsq(ss[:rows, tt:tt + 1], tmp1[:rows, :], rows, D, ['tmpo'], ('ssc', tt))
                rstd_act(rstd[:rows, tt:tt + 1], ss[:rows, tt:tt + 1], D, [('ssc', tt)], ('rstd', tt))
                S.op('dve', lambda e: e.scalar_tensor_tensor(out=tmp1[:rows, :], in0=tmp1[:rows, :], scalar=rstd[:rows, tt:tt + 1],
                                                             in1=gbc[:rows, 1, :], op0=ALU.mult, op1=ALU.mult),
                     r=['tmpo', ('rstd', tt), ('gbc', 1)], w=['tmpo'])
                S.op('dve', lambda e: e.tensor_tensor(out=xr[yb][:rows, :], in0=tmp1[:rows, :], in1=xr[yb][:rows, :], op=ALU.add),
                     r=['tmpo', ('xro', yb)], w=[('xro', yb)])
                S.dma('sp', lambda e: e.dma_start(out=x2_s[r0:r0 + rows, :], in_=xr[yb][:rows, :]),
                      r=[('xro', yb)], w=[('dram', id(x2_s), tt)])

        if stage > 1:
            token_mix()
        if stage > 3:
            with ExitStack() as es2:
                ffn(x2_s, y_out, 1, 4, 5, True)
            S.freed()
        S.finish()
    return nc


_CONST = {}


def consts():
    if not _CONST:
        import ml_dtypes
        _CONST['identb'] = np.eye(128, dtype=np.float32).astype(ml_dtypes.bfloat16)
        _CONST['identf'] = np.eye(128, dtype=np.float32)
        oh = np.zeros((33, 384), np.float32)
        for r in range(384):
            rel = r - 127
            if rel < 0:
                oh[32, r] = 1.0
            else:
                if rel < 16:
                    b = rel
                else:
                    b = 16 + int(np.float32(np.log(np.float32(max(rel, 16)) / np.float32(16))) / np.float32(np.log(8.0)) * np.float32(16))
                    b = min(b, 31)
                oh[b, r] = 1.0
        _CONST['onehot'] = oh
        sel = np.zeros((4, 4, 128), np.float32)
        for h in range(4):
            sel[h, h, :] = 1.0
        _CONST['sel'] = sel
        _CONST['trim'] = np.triu(np.ones((128, 128), np.float32))
        bm = np.zeros((64, 64), np.float32)
        rs = np.zeros((64, 16), np.float32)
        for p in range(64):
            rs[p, p // 4] = 1.0
            for t in range(64):
                if p // 4 == t // 4 and p <= t:
                    bm[p, t] = 1.0
        _CONST['bmask'] = bm
        _CONST['rowsel'] = rs
        rm = np.zeros((4, 64), np.float32)
        for p in range(64):
            rm[p % 4, p] = 1.0
        _CONST['rmod'] = rm
        s01 = np.zeros((32, 32), np.float32)
        b16 = np.zeros((16, 512), np.float32)
        for h in range(4):
            for q in range(4):
                s01[h * 8 + q, h * 4 + q] = 1.0
                s01[h * 8 + 4 + q, 16 + h * 4 + q] = 1.0
                b16[h * 4 + q, h * 128:(h + 1) * 128] = 1.0
        _CONST['sel01'] = s01
        _CONST['bm16'] = b16
        _CONST['tri'] = np.ascontiguousarray(np.eye(128, dtype=np.float32)[::-1])
    return _CONST


def make_in_maps(inputs, cores):
    c = consts()
    maps = []
    xp = inputs['x_prompt']
    xs = inputs['x_sample'].reshape(128 * 4, D)
    for k in cores:
        m = {
            'xin': np.ascontiguousarray(np.concatenate([xp[k], xs[k * 64:(k + 1) * 64]], axis=0)),
            'gains': np.ascontiguousarray(inputs['norm_gains'][0]),
            'wgate': np.ascontiguousarray(inputs['ffn_w_gate'][0]),
            'wup': np.ascontiguousarray(inputs['ffn_w_up'][0]),
            'wdown': np.ascontiguousarray(inputs['ffn_w_down'][0]),
            'identb': c['identb'],
            'w_in': np.ascontiguousarray(inputs['w_in'][0]),
            'rel_bias': np.ascontiguousarray(inputs['rel_bias']),
            'b_gates': np.ascontiguousarray(inputs['b_gates'][0]),
            'lam_q': np.ascontiguousarray(inputs['lam_q'][0].reshape(1, 128)),
            'lam_k': np.ascontiguousarray(inputs['lam_k'][0].reshape(1, 128)),
            'attn_norm': np.ascontiguousarray(inputs['attn_norm']),
            'mlstm_norm': np.ascontiguousarray(inputs['mlstm_norm']),
            'w_out': np.ascontiguousarray(inputs['w_out'][0]),
            'onehot': c['onehot'], 'identf': c['identf'], 'tri': c['tri'], 'sel': c['sel'], 'trim': c['trim'],
            'sm': np.ascontiguousarray(inputs['state_m'][0, k * 16:(k + 1) * 16]),
            'sC': np.ascontiguousarray(inputs['state_C'][0, k * 16:(k + 1) * 16].reshape(64, 128, 128)),
            'sn': np.ascontiguousarray(inputs['state_n'][0, k * 16:(k + 1) * 16].reshape(64, 128)),
            'bmask': c['bmask'], 'rowsel': c['rowsel'],
            'rmod': c['rmod'], 'sel01': c['sel01'], 'bm16': c['bm16'],
            **({'cache_k': inputs['cache_k'].reshape(2560 * 128, 512),
                'cache_v': inputs['cache_v'].reshape(2560 * 128, 512)} if 'cache_k' in inputs else {}),
            'pt': np.ascontiguousarray(inputs['page_table'][k * 16:(k + 1) * 16].reshape(1, 256)).astype(np.int32),
        }
        maps.append(m)
    return maps


def kernel(**inputs):
    inputs = {k: np.asarray(v) for k, v in inputs.items()}
    nc = build_nc()
    maps = make_in_maps(inputs, list(range(NCORES)))
    res = run_bass_kernel_spmd(nc, maps, core_ids=list(range(NCORES)))
    R = res.results
    f32 = np.float32
    y = np.stack([r['y_out'] for r in R])
    y_prompt = np.ascontiguousarray(y[:, :TP, :]).astype(f32)
    y_sample = np.ascontiguousarray(y[:, TP:, :]).reshape(128, 4, D).astype(f32)
    ko = np.stack([r['k_out'] for r in R]); vo = np.stack([r['v_out'] for r in R])
    k_prompt = ko[:, :TP].reshape(1, 8, TP, 4, 128).astype(f32)
    v_prompt = vo[:, :TP].reshape(1, 8, TP, 4, 128).astype(f32)
    k_sample = ko[:, TP:].reshape(1, 128, 4, 4, 128).astype(f32)
    v_sample = vo[:, TP:].reshape(1, 128, 4, 4, 128).astype(f32)

    def get(name, shape):
        if name in R[0]:
            return np.stack([r[name] for r in R]).reshape(shape).astype(f32)
        return np.zeros(shape, f32)
    C_prompt = get('cp_out', (1, 8, 4, 128, 128))
    n_prompt = get('np_out', (1, 8, 4, 128))
    m_prompt = get('mp_out', (1, 8, 4))
    C_sample = get('cs_out', (1, 128, 4, 128, 128))
    n_sample = get('ns_out', (1, 128, 4, 128))
    m_sample = get('ms_out', (1, 128, 4))
    return (y_prompt, y_sample, k_prompt, v_prompt, C_prompt, n_prompt, m_prompt,
            k_sample, v_sample, C_sample, n_sample, m_sample)
```

```python
import numpy as np
from contextlib import ExitStack
import concourse.bass as bass
import concourse.mybir as mybir
from concourse.bass_utils import run_bass_kernel_spmd

F32 = mybir.dt.float32
BF16 = mybir.dt.bfloat16
I32 = mybir.dt.int32
ALU = mybir.AluOpType
AF = mybir.ActivationFunctionType
AX = mybir.AxisListType

NCORES = 8
TP, TS = 2048, 64
T = TP + TS
NT = 17
D, FF, KD, KF = 1024, 2816, 8, 22
DIN = 3592
EPS = 1e-6
NDS = 40
LAM_INIT = 0.2


def rows_of(tt):
    return 128 if tt < 16 else 64


def blk_cols(blk):
    return (blk * 512, 512) if blk < 4 else (2048, 64)


class Sched:
    def __init__(self, nc, es):
        self.nc = nc
        self.E = {'pe': nc.tensor, 'act': nc.scalar, 'dve': nc.vector, 'pool': nc.gpsimd, 'sp': nc.sync}
        self.sem = {k: es.enter_context(nc.semaphore(f"sem_{k}")) for k in self.E}
        self.cnt = {k: 0 for k in self.E}
        self.pending = {k: False for k in self.E}
        self.dsem = [es.enter_context(nc.semaphore(f"dsem{i}")) for i in range(NDS)]
        self.dcnt = [0] * NDS
        self.dnext = 0
        self.waited = {}
        self.lastw = {}
        self.readers = {}
        self.out_dmas = []
        self.alias_deps = {}
        for h in list(self.sem.values()) + self.dsem:
            nc.gpsimd.sem_clear(h)
        nc.all_engine_barrier()

    def _semof(self, prod):
        return self.sem[prod] if isinstance(prod, str) else self.dsem[prod[1]]

    def _wait(self, e, prod, val):
        key = (e, prod)
        if self.waited.get(key, 0) >= val:
            return
        self.waited[key] = val
        self.E[e].wait_ge(self._semof(prod), val)

    def _deps(self, r, w):
        deps = {}

        def add(p, v):
            if deps.get(p, 0) < v:
                deps[p] = v
        for b in r:
            lw = self.lastw.get(b)
            if lw is not None:
                add(*lw)
            elif b not in self.readers:
                for p, v in self.alias_deps.items():
                    add(p, v)
        for b in w:
            lw = self.lastw.get(b)
            if lw is not None:
                add(*lw)
            else:
                for p, v in self.alias_deps.items():
                    add(p, v)
            for p, v in self.readers.get(b, {}).items():
                add(p, v)
        return deps

    def freed(self):
        d = {}
        for e in self.E:
            v = self.cnt[e] + (1 if self.pending[e] else 0)
            if v > 0:
                d[e] = v
        for j in range(NDS):
            if self.dcnt[j] > 0:
                d[('d', j)] = self.dcnt[j]
        self.alias_deps = d
        self.lastw = {}
        self.readers = {}

    def _record(self, me, r, w):
        for b in w:
            self.lastw[b] = me
            self.readers[b] = {}
        for b in r:
            d = self.readers.setdefault(b, {})
            if d.get(me[0], 0) < me[1]:
                d[me[0]] = me[1]

    def op(self, e, fn, r=(), w=(), inc=True):
        deps = self._deps(r, w)
        for p, v in deps.items():
            if p == e and e == 'pe':
                continue
            self._wait(e, p, v)
        ins = fn(self.E[e])
        if inc:
            self.cnt[e] += 1
            ins.then_inc(self.sem[e], 1)
            me = (e, self.cnt[e])
            self.pending[e] = False
        else:
            me = (e, self.cnt[e] + 1)
            self.pending[e] = True
        self._record(me, r, w)

    def dma(self, q, fn, r=(), w=(), is_out=False):
        deps = self._deps(r, w)
        j = self.dnext
        self.dnext = (self.dnext + 1) % NDS
        if self.dcnt[j] > 0:
            self._wait(q, ('d', j), self.dcnt[j])
        for p, v in deps.items():
            self._wait(q, p, v)
        ins = fn(self.E[q])
        self.dcnt[j] += 16
        ins.then_inc(self.dsem[j], 16)
        me = (('d', j), self.dcnt[j])
        self._record(me, r, w)
        if is_out:
            self.out_dmas.append(me)

    def finish(self):
        for j in range(NDS):
            if self.dcnt[j] > 0:
                self._wait('sp', ('d', j), self.dcnt[j])
        for e in ('pe', 'act', 'dve', 'pool'):
            assert not self.pending[e], e
            if self.cnt[e] > 0:
                self._wait('sp', e, self.cnt[e])
        self.nc.all_engine_barrier()
        for h in list(self.sem.values()) + self.dsem:
            self.nc.gpsimd.sem_clear(h)
        self.nc.all_engine_barrier()


DEBUG = set()
SKIP = set()


def build_nc(stage=99):
    nc = bass.Bass("TRN2", target_bir_lowering=False)

    def din(name, shape, dt=F32):
        return nc.dram_tensor(name, list(shape), dt, kind="ExternalInput").ap()

    def dout(name, shape, dt=F32):
        return nc.dram_tensor(name, list(shape), dt, kind="ExternalOutput").ap()

    def dscr(name, shape, dt=F32):
        return nc.dram_tensor(name, list(shape), dt, kind="Internal").ap()

    xin = din("xin", [T, D])
    gains = din("gains", [6, D])
    wgate = din("wgate", [2, D, FF])
    wup = din("wup", [2, D, FF])
    wdown = din("wdown", [2, FF, D])
    identb_d = din("identb", [128, 128], BF16)

    w_in = din("w_in", [D, DIN])
    rel_bias = din("rel_bias", [32, 4])
    b_gates = din("b_gates", [2, 4])
    lam_q = din("lam_q", [1, 128])
    lam_k = din("lam_k", [1, 128])
    attn_norm = din("attn_norm", [1, 128])
    mlstm_norm = din("mlstm_norm", [1, 128])
    w_out = din("w_out", [D, D])
    onehot_d = din("onehot", [33, 384])
    identf_d = din("identf", [128, 128])
    tri_d = din("tri", [128, 128])
    sel_d = din("sel", [4, 4, 128])
    trim_d = din("trim", [128, 128])
    sm_in = din("sm", [16, 4])
    kv_in = din("cache_kv", [2560 * 128, 1024]) if stage > 4 else None
    pt_in = din("pt", [1, 256], I32)
    rmod_d = din("rmod", [4, 64])
    sel01_d = din("sel01", [32, 32])
    bm16_d = din("bm16", [16, 512])
    sC_in = din("sC", [64, 128, 128])
    sn_in = din("sn", [64, 128])
    bmask_d = din("bmask", [64, 64])
    rowsel_d = din("rowsel", [64, 16])
    cs_out = dout("cs_out", [64, 128, 128])
    ns_out = dout("ns_out", [64, 128])
    ms_out = dout("ms_out", [16, 4])
    cp_out = dout("cp_out", [4, 128, 128])
    np_out = dout("np_out", [4, 128])
    mp_out = dout("mp_out", [1, 4])
    vec_h = nc.dram_tensor("vec_s", [4, 384], F32, kind="Internal")
    vec_s = vec_h.ap()
    y_out = dout("y_out", [T, D])
    k_out = dout("k_out", [T, 512])
    v_out = dout("v_out", [T, 512])
    x1_s = dscr("x1_s", [T, D])
    x2_s = dout("x2_s", [T, D]) if "x2" in DEBUG else dscr("x2_s", [T, D])

    with ExitStack() as es:
        S = Sched(nc, es)

        def sb(name, shape, dt):
            return es.enter_context(nc.sbuf_tensor(name, list(shape), dt))

        psum = es.enter_context(nc.psum_tensor("psum", [128, 8, 512], F32))

        def dbg(name, ap, shape, dt, rbufs):
            if name not in DEBUG:
                return
            o = nc.dram_tensor("dbg_" + name, list(shape), dt, kind="ExternalOutput").ap()
            S.dma('sp', lambda e: e.dma_start(out=o, in_=ap), r=rbufs, w=[('dbg', name)])

        def psb(b, n=512):
            return psum[:, b, 0:n]

        def psb16(b):
            return psum[:, b, :].bitcast(BF16)

        identb = sb("identb_sb", [128, 128], BF16)
        S.dma('sp', lambda e: e.dma_start(out=identb[:], in_=identb_d[:, :]), w=['identb'])
        gbc = sb("gbc", [128, 2, D], F32)

        def load_gains(gis):
            for j, gi in enumerate(gis):
                S.dma('sp', lambda e: e.dma_start(out=gbc[:, j, :], in_=gains[gi:gi + 1, :].partition_broadcast(128)),
                      w=[('gbc', j)])
        ss = sb("ss", [128, 64], F32)
        rstd = sb("rstd", [128, 64], F32)
        junk = sb("junk", [128, D], F32)

        def sumsq(out, in_, rows, width, rbufs, wbuf):
            S.op('dve', lambda e: e.tensor_tensor(out=junk[:rows, 0:width], in0=in_, in1=in_, op=ALU.mult),
                 r=rbufs, w=['junk'])
            S.op('dve', lambda e: e.tensor_reduce(out=out, in_=junk[:rows, 0:width], axis=AX.X, op=ALU.add),
                 r=['junk'], w=[wbuf])

        def rstd_act(out, in_, n, rbufs, wbuf, post=1.0):
            S.op('act', lambda e: e.activation(out=out, in_=in_, func=AF.Ln, bias=EPS, scale=1.0 / n),
                 r=rbufs, w=[wbuf])
            S.op('act', lambda e: e.activation(out=out, in_=out, func=AF.Exp, scale=-0.5),
                 r=[wbuf], w=[wbuf])
            if post != 1.0:
                S.op('act', lambda e: e.mul(out=out, in_=out, mul=post), r=[wbuf], w=[wbuf])

        def norm_transpose(src, gi, hT, tag, xt, hn):
            for tt in range(NT):
                rows = rows_of(tt)
                sl = tt % 2
                r0 = tt * 128
                S.dma('sp', lambda e: e.dma_start(out=xt[sl][:rows, :], in_=src[r0:r0 + rows, :]), r=[('dram', id(src), tt)], w=[('xt', sl)])
                sumsq(ss[:rows, tt:tt + 1], xt[sl][:rows, :], rows, D, [('xt', sl)], ('ssc', tt))
                rstd_act(rstd[:rows, tt:tt + 1], ss[:rows, tt:tt + 1], D, [('ssc', tt)], ('rstd', tt))
                S.op('dve', lambda e: e.scalar_tensor_tensor(out=hn[sl][:rows, :], in0=xt[sl][:rows, :],
                                                             scalar=rstd[:rows, tt:tt + 1], in1=gbc[:rows, gi, :],
                                                             op0=ALU.mult, op1=ALU.mult),
                     r=[('xt', sl), ('rstd', tt), ('gbc', gi)], w=[('hn', sl)])
                bank = sl
                pv = psb16(bank)
                for c in range(KD):
                    S.op('pe', lambda e, c=c: e.transpose(out=pv[:, c * 128:c * 128 + rows],
                                                          in_=hn[sl][:rows, c * 128:(c + 1) * 128],
                                                          identity=identb[:rows, :rows]),
                         r=[('hn', sl), 'identb'], w=[('ps', bank)], inc=(c == KD - 1))
                srcv = pv[:, 0:1024].rearrange("p (c r) -> p c r", r=128)[:, :, 0:rows]
                S.op('act', lambda e: e.copy(out=hT[:, :, r0:r0 + rows], in_=srcv),
                     r=[('ps', bank)], w=[('hT', tt)])

        def ffn(src, dst, li, g_pre, g_post, dst_is_out):
            load_gains([g_pre, g_post])
            g_pre, g_post = 0, 1
            uT = es2.enter_context(nc.sbuf_tensor(f"uT{li}", [128, KF, T], BF16))
            with ExitStack() as esh:
                hT = esh.enter_context(nc.sbuf_tensor(f"hT{li}", [128, KD, T], BF16))
                wd = esh.enter_context(nc.sbuf_tensor(f"wd{li}", [128, KF, D], BF16))
                with ExitStack() as est:
                    xt = [est.enter_context(nc.sbuf_tensor(f"xt{li}_{i}", [128, D], F32)) for i in range(2)]
                    hn = [est.enter_context(nc.sbuf_tensor(f"hn{li}_{i}", [128, D], BF16)) for i in range(2)]
                    norm_transpose(src, g_pre, hT, ('ffn', li), xt, hn)
                S.freed()
                dbg('ss', ss[:, :], [128, 64], F32, [('ssc', tt) for tt in range(NT)])
                dbg('rstd', rstd[:, :], [128, 64], F32, [('rstd', tt) for tt in range(NT)])
                dbg('hT', hT[:, :, :], [128, KD, T], BF16, [('hT', tt) for tt in range(NT)])
                wgs = [esh.enter_context(nc.sbuf_tensor(f"wg{li}_{i}", [128, KD, 128], BF16)) for i in range(2)]
                wus = [esh.enter_context(nc.sbuf_tensor(f"wu{li}_{i}", [128, KD, 128], BF16)) for i in range(2)]
                sg = [esh.enter_context(nc.sbuf_tensor(f"sg{li}_{i}", [128, 512], F32)) for i in range(2)]
                wg_d = wgate[li].rearrange("(kc p) f -> p kc f", p=128)
                wu_d = wup[li].rearrange("(kc p) f -> p kc f", p=128)
                wd_d = wdown[li].rearrange("(fc p) d -> p fc d", p=128)
                pbi = 0
                for g in range(KF):
                    sl = g % 2
                    S.dma('pool', lambda e: e.dma_start(out=wgs[sl][:], in_=wg_d[:, :, g * 128:(g + 1) * 128]),
                          w=[('wg', sl)])
                    S.dma('pool', lambda e: e.dma_start(out=wus[sl][:], in_=wu_d[:, :, g * 128:(g + 1) * 128]),
                          w=[('wu', sl)])
                    S.dma('pool', lambda e: e.dma_start(out=wd[:, g:g + 1, :], in_=wd_d[:, g:g + 1, :]),
                          w=[('wd', g)])
                    for fl in range(1):
                        fc = g
                        for blk in range(5):
                            c0, n = blk_cols(blk)
                            pb = pbi % 2
                            pbi += 1
                            hbufs = [('hT', tt) for tt in (range(4 * blk, 4 * blk + 4) if blk < 4 else [16])]
                            for kc in range(KD):
                                S.op('pe', lambda e, kc=kc: e.matmul(psb(2 * pb, n), lhsT=wgs[sl][:, kc, fl * 128:(fl + 1) * 128],
                                                                     rhs=hT[:, kc, c0:c0 + n], start=(kc == 0), stop=(kc == KD - 1)),
                                     r=[('wg', sl)] + hbufs, w=[('ps', 2 * pb)], inc=(kc == KD - 1))
                            for kc in range(KD):
                                S.op('pe', lambda e, kc=kc: e.matmul(psb(2 * pb + 1, n), lhsT=wus[sl][:, kc, fl * 128:(fl + 1) * 128],
                                                                     rhs=hT[:, kc, c0:c0 + n], start=(kc == 0), stop=(kc == KD - 1)),
                                     r=[('wu', sl)] + hbufs, w=[('ps', 2 * pb + 1)], inc=(kc == KD - 1))
                            S.op('act', lambda e: e.activation(out=sg[pb][:, 0:n], in_=psb(2 * pb, n), func=AF.Silu),
                                 r=[('ps', 2 * pb)], w=[('sg', pb)])
                            S.op('dve', lambda e: e.tensor_tensor(out=uT[:, fc, c0:c0 + n], in0=sg[pb][:, 0:n],
                                                                  in1=psb(2 * pb + 1, n), op=ALU.mult),
                                 r=[('sg', pb), ('ps', 2 * pb + 1)], w=[('uT', fc, blk)])
                dbg('uT', uT[:, :, :], [128, KF, T], BF16, [('uT', fc, blk) for fc in range(KF) for blk in range(5)])
                xr = [esh.enter_context(nc.sbuf_tensor(f"xr{li}_{i}", [128, D], F32)) for i in range(2)]
                tmp1 = esh.enter_context(nc.sbuf_tensor(f"tmp{li}", [128, D], F32))
                for tt in range(NT):
                    rows = rows_of(tt)
                    r0 = tt * 128
                    yb = tt % 2
                    blk = tt // 4 if tt < 16 else 4
                    S.dma('sp', lambda e: e.dma_start(out=xr[yb][:rows, :], in_=src[r0:r0 + rows, :]), r=[('dram', id(src), tt)], w=[('xr', yb)])
                    for half in range(2):
                        bank = 4 + 2 * yb + half
                        for fc in range(KF):
                            S.op('pe', lambda e, fc=fc: e.matmul(psum[:rows, bank, :], lhsT=uT[:, fc, r0:r0 + rows],
                                                                 rhs=wd[:, fc, half * 512:(half + 1) * 512],
                                                                 start=(fc == 0), stop=(fc == KF - 1)),
                                 r=[('uT', fc, blk), ('wd', fc)], w=[('ps', bank)], inc=(fc == KF - 1))
                    b0 = 4 + 2 * yb
                    S.op('act', lambda e: e.copy(out=tmp1[:rows, :].rearrange("p (a b) -> p a b", a=2),
                                                 in_=psum[:rows, b0:b0 + 2, :]),
                         r=[('ps', b0), ('ps', b0 + 1)], w=['tmp1'])
                    sumsq(ss[:rows, tt:tt + 1], tmp1[:rows, :], rows, D, ['tmp1'], ('ssc', tt))
                    rstd_act(rstd[:rows, tt:tt + 1], ss[:rows, tt:tt + 1], D, [('ssc', tt)], ('rstd', tt))
                    S.op('dve', lambda e: e.scalar_tensor_tensor(out=tmp1[:rows, :], in0=tmp1[:rows, :],
                                                                 scalar=rstd[:rows, tt:tt + 1], in1=gbc[:rows, g_post, :],
                                                                 op0=ALU.mult, op1=ALU.mult),
                         r=['tmp1', ('rstd', tt), ('gbc', g_post)], w=['tmp1'])
                    S.op('dve', lambda e: e.scalar_tensor_tensor(out=xr[yb][:rows, :], in0=tmp1[:rows, :], scalar=0.5,
                                                                 in1=xr[yb][:rows, :], op0=ALU.mult, op1=ALU.add),
                         r=['tmp1', ('xr', yb)], w=[('xr', yb)])
                    S.dma('sp', lambda e: e.dma_start(out=dst[r0:r0 + rows, :], in_=xr[yb][:rows, :]),
                          r=[('xr', yb)], w=[('dram', id(dst), tt)], is_out=dst_is_out)
            S.freed()

        with ExitStack() as es2:
            ffn(xin, x1_s if stage > 1 else y_out, 0, 0, 1, stage <= 1)
        S.freed()

        def token_mix():
            with ExitStack() as em:
                hnT = em.enter_context(nc.sbuf_tensor("hnT", [128, KD, T], BF16))
                load_gains([2, 3])
                with ExitStack() as est:
                    xt = [est.enter_context(nc.sbuf_tensor(f"xtm_{i}", [128, D], F32)) for i in range(2)]
                    hn = [est.enter_context(nc.sbuf_tensor(f"hnm_{i}", [128, D], BF16)) for i in range(2)]
                    norm_transpose(x1_s, 0, hnT, 'mix', xt, hn)
                S.freed()
                win_d = w_in.rearrange("(kc p) f -> p kc f", p=128)
                wt = [em.enter_context(nc.sbuf_tensor(f"wt_{i}", [128, KD, 512], BF16)) for i in range(2)]
                stg = [em.enter_context(nc.sbuf_tensor(f"stg_{i}", [128, 512], F32)) for i in range(2)]
                hbuf_all = [('hT', tt) for tt in range(NT)]
                for gi_, (col0, dst) in enumerate([(512, k_out), (1024, v_out)]):
                    sl = gi_ % 2
                    S.dma('pool', lambda e: e.dma_start(out=wt[sl][:], in_=win_d[:, :, col0:col0 + 512]),
                          w=[('wt', sl, j) for j in range(4)])
                    for tt in range(NT):
                        rows = rows_of(tt)
                        r0 = tt * 128
                        bank = tt % 2
                        for kc in range(KD):
                            S.op('pe', lambda e, kc=kc: e.matmul(psum[:rows, bank, :], lhsT=hnT[:, kc, r0:r0 + rows],
                                                                 rhs=wt[sl][:, kc, :], start=(kc == 0), stop=(kc == KD - 1)),
                                 r=[('wt', sl, j) for j in range(4)] + [('hT', tt)], w=[('ps', bank)], inc=(kc == KD - 1))
                        S.op('act', lambda e: e.copy(out=stg[bank][:rows, :], in_=psum[:rows, bank, :]),
                             r=[('ps', bank)], w=[('stg', bank)])
                        S.dma('sp', lambda e: e.dma_start(out=dst[r0:r0 + rows, :], in_=stg[bank][:rows, :]),
                              r=[('stg', bank)], w=[('dram', id(dst), tt)], is_out=True)
                if stage > 2:
                    mix_body(em, hnT, win_d, wt, stg)
            S.freed()

        def mix_body(em, hnT, win_d, wt, stg):
            def sbm(name, shape, dt):
                return em.enter_context(nc.sbuf_tensor(name, list(shape), dt))
            mixed = sbm("mixed", [128, NT, D], BF16)
            lsm = sbm("lsm", [128, 8], F32)
            c31 = sbm("c31", [128, 8], F32)
            tz = sbm("tz", [128, 4, 256], F32)
            anb = sbm("anb", [128, 128], F32)
            mnb = sbm("mnb", [128, 128], F32)
            qblk = sbm("qblk", [128, 16, 4, 8], BF16)
            ks_s = sbm("ks_s", [128, 4, 64], BF16)
            vs_s = sbm("vs_s", [64, 4, 128], BF16)
            S.op('dve', lambda e: e.memset(qblk[:], 0.0), w=['qblk'])
            etz = ExitStack()

            def sbt(name, shape, dt):
                return etz.enter_context(nc.sbuf_tensor(name, list(shape), dt))
            lq = sbt("lq", [128, 128], F32)
            lk = sbt("lk", [128, 128], F32)
            S.dma('sp', lambda e: e.dma_start(out=lq[:], in_=lam_q[0:1, :].partition_broadcast(128)), w=['lq'])
            S.dma('sp', lambda e: e.dma_start(out=lk[:], in_=lam_k[0:1, :].partition_broadcast(128)), w=['lk'])
            S.op('dve', lambda e: e.tensor_tensor(out=lq[:], in0=lq[:], in1=lk[:], op=ALU.mult), r=['lq', 'lk'], w=['lq'])
            S.op('dve', lambda e: e.tensor_reduce(out=lsm[:, 0:2], in_=lq[:].rearrange("p (a b) -> p a b", a=2),
                                                  axis=AX.X, op=ALU.add), r=['lq'], w=['lsm'])
            S.op('act', lambda e: e.activation(out=lsm[:, 2:4], in_=lsm[:, 0:2], func=AF.Exp), r=['lsm'], w=['lsm'])
            S.op('dve', lambda e: e.tensor_tensor(out=lsm[:, 4:5], in0=lsm[:, 2:3], in1=lsm[:, 3:4], op=ALU.subtract),
                 r=['lsm'], w=['lsm'])
            S.op('dve', lambda e: e.tensor_scalar(out=lsm[:, 5:6], in0=lsm[:, 4:5], scalar1=LAM_INIT, scalar2=-1.0,
                                                  op0=ALU.add, op1=ALU.mult), r=['lsm'], w=['lsm'])
            nlam = lsm[:, 5:6]
            rbx = sbt("rbx", [33, 4], F32)
            oneh = sbt("oneh", [33, 384], F32)
            vecs = sbt("vecs", [4, 384], F32)
            S.op('dve', lambda e: e.memset(rbx[32:33, :], -30000.0), w=['rbx32'])
            S.dma('sp', lambda e: e.dma_start(out=rbx[0:32, :], in_=rel_bias[:, :]), w=['rbx'])
            S.dma('sp', lambda e: e.dma_start(out=oneh[:], in_=onehot_d[:, :]), w=['oneh'])
            S.dma('sp', lambda e: e.dma_start(out=c31[:, 0:4], in_=rel_bias[31:32, :].partition_broadcast(128)), w=['c31'])
            S.dma('sp', lambda e: e.dma_start(out=anb[:], in_=attn_norm[0:1, :].partition_broadcast(128)), w=['anb'])
            S.dma('sp', lambda e: e.dma_start(out=mnb[:], in_=mlstm_norm[0:1, :].partition_broadcast(128)), w=['mnb'])
            S.op('dve', lambda e: e.tensor_scalar(out=c31[:, 4:8], in0=c31[:, 0:4], scalar1=-1.0, scalar2=None, op0=ALU.mult),
                 r=['c31'], w=['c31n'])
            S.op('pe', lambda e: e.matmul(psum[0:4, 7, 0:384], lhsT=rbx[:, :], rhs=oneh[:, :], start=True, stop=True),
                 r=['rbx', 'rbx32', 'oneh'], w=[('ps', 7)])
            S.op('act', lambda e: e.copy(out=vecs[:], in_=psum[0:4, 7, 0:384]), r=[('ps', 7)], w=['vecs'])
            S.dma('sp', lambda e: e.dma_start(out=vec_s[:, :], in_=vecs[:]), r=['vecs'], w=['vec_s'])
            tzr = etz.enter_context(nc.sbuf_tensor("tzr", [128, 4, 256], F32))
            antiI = etz.enter_context(nc.sbuf_tensor("antiI", [128, 128], F32))
            S.dma('sp', lambda e: e.dma_start(out=antiI[:], in_=tri_d[:, :]), w=['antiI'])
            tz_src = bass.AP(vec_h, 0, [[1, 128], [384, 4], [1, 256]])
            S.dma('sp', lambda e: e.dma_start(out=tzr[:], in_=tz_src), r=['vec_s'], w=['tzr'])
            for half in range(2):
                S.op('pe', lambda e: e.matmul(psum[:, 7, :], lhsT=antiI[:, :],
                                              rhs=tzr[:, 2 * half:2 * half + 2, :].rearrange("p a b -> p (a b)"),
                                              start=True, stop=True), r=['antiI', 'tzr'], w=[('ps', 7)])
                for hh in range(2):
                    h = 2 * half + hh
                    S.op('act', lambda e: e.activation(out=tz[:, h, :], in_=psum[:, 7, hh * 256:(hh + 1) * 256], func=AF.Exp,
                                                       bias=c31[:, 4 + h:5 + h]),
                         r=[('ps', 7), 'c31n'], w=['tz'])
            etz.close()
            S.freed()
            dbg('tz', tz[:, :, :], [128, 4, 256], F32, ['tz'])
            eh = ExitStack()

            def sbm(name, shape, dt):
                return eh.enter_context(nc.sbuf_tensor(name, list(shape), dt))
            dbg('lsm', lsm[:, :], [128, 8], F32, ['lsm'])

            qk = [[sbm(f"qk_{i}_{j}", [128, T], BF16) for j in range(2)] for i in range(2)]
            vt = [sbm(f"vt_{i}", [128, NT, 129], BF16) for i in range(2)]
            pT = [[sbm(f"pT_{i}_{c}", [128, 512], BF16) for c in range(2)] for i in range(2)]
            fin4 = sbm("fin4", [128, 4, 128], F32)
            fin = fin4[:, 0, :]
            osb8 = sbm("osb8", [128, 8, 129], F32)
            osbf = osb8[:, :, :].rearrange("p a b -> p (a b)")
            fsm = sbm("fsm", [128, 8], F32)
            fs4 = sbm("fs4", [128, 16], F32)
            for i in range(2):
                S.op('pool', lambda e: e.memset(vt[i][:], 1.0), w=[('vt', i)])

            PROT = [7, 6, 1, 0]
            prot = [0]

            def pbank():
                prot[0] += 1
                return PROT[prot[0] % 4], ('act' if prot[0] % 2 == 0 else 'dve')

            def evac(eng, out, in_, r, w):
                if eng == 'act':
                    S.op('act', lambda e: e.copy(out=out, in_=in_), r=r, w=w)
                else:
                    S.op('dve', lambda e: e.tensor_copy(out=out, in_=in_), r=r, w=w)

            def proj_fm(dst, dname, wtile, wname, c0):
                for blk in range(5):
                    cc, n = blk_cols(blk)
                    hb = [('hT', tt) for tt in (range(4 * blk, 4 * blk + 4) if blk < 4 else [16])]
                    pbk, eng = pbank()
                    for kc in range(KD):
                        S.op('pe', lambda e, kc=kc: e.matmul(psum[:, pbk, 0:n], lhsT=wtile[:, kc, c0:c0 + 128],
                                                             rhs=hnT[:, kc, cc:cc + n], start=(kc == 0), stop=(kc == KD - 1)),
                             r=[wname] + hb, w=[('ps', pbk)], inc=(kc == KD - 1))
                    evac(eng, dst[:, cc:cc + n], psum[:, pbk, 0:n], [('ps', pbk)], [(dname, blk)])

            def proj_tm(consume, wtile, wname, c0, ncols):
                for tt in range(NT):
                    rows = rows_of(tt)
                    r0 = tt * 128
                    pbk, eng = pbank()
                    for kc in range(KD):
                        S.op('pe', lambda e, kc=kc: e.matmul(psum[:rows, pbk, 0:ncols], lhsT=hnT[:, kc, r0:r0 + rows],
                                                             rhs=wtile[:, kc, c0:c0 + ncols], start=(kc == 0), stop=(kc == KD - 1)),
                             r=[wname, ('hT', tt)], w=[('ps', pbk)], inc=(kc == KD - 1))
                    consume(tt, rows, psum[:rows, pbk, 0:ncols], pbk, eng)

            OPS_BANKS = [0, 1, 6]

            def ops(slot):
                b = OPS_BANKS[slot // 3]
                o = (slot % 3) * 129
                return psum[:, b, o:o + 129], ('ps', b)

            def BC4(ap):
                return ap.rearrange("p (a o) -> p a o", o=1).broadcast_to([128, 4, 128])

            def block_norm(x4, dst, gain, mul3, post, wbufs, extra_r=()):
                j4 = junk[:, 0:512].rearrange("p (a b) -> p a b", a=4)
                S.op('dve', lambda e: e.tensor_tensor(out=j4, in0=x4[:, :, :], in1=x4[:, :, :], op=ALU.mult), r=['fin'], w=['junk'])
                S.op('dve', lambda e: e.tensor_reduce(out=fs4[:, 8:12], in_=j4, axis=AX.X, op=ALU.add), r=['junk'], w=['fs4b'])
                rstd_act(fs4[:, 12:16], fs4[:, 8:12], 128, ['fs4b'], 'fs4c', post=post)
                S.op('dve', lambda e: e.tensor_tensor(out=x4[:, :, :], in0=x4[:, :, :], in1=BC4(fs4[:, 12:16]), op=ALU.mult),
                     r=['fin', 'fs4c'], w=['fin'])
                g3 = gain[:, :].rearrange("p (o d) -> p o d", o=1).broadcast_to([128, 4, 128])
                gname = 'anb' if gain is anb else 'mnb'
                if mul3 is None:
                    S.op('dve', lambda e: e.tensor_tensor(out=dst, in0=x4[:, :, :], in1=g3, op=ALU.mult), r=['fin', gname], w=list(wbufs))
                else:
                    S.op('dve', lambda e: e.tensor_tensor(out=x4[:, :, :], in0=x4[:, :, :], in1=g3, op=ALU.mult), r=['fin', gname], w=['fin'])
                    S.op('dve', lambda e: e.tensor_tensor(out=dst, in0=x4[:, :, :], in1=mul3, op=ALU.mult),
                         r=['fin'] + list(extra_r), w=list(wbufs))

            def attn_head(h, it):
                sl = it % 2
                for j, col0 in enumerate([h * 128, 512 + h * 128, 1024 + h * 128]):
                    S.dma('pool', lambda e: e.dma_start(out=wt[sl][:, :, j * 128:(j + 1) * 128], in_=win_d[:, :, col0:col0 + 128]),
                          w=[('wt', sl, j)])
                qT, kT, V = qk[sl][0], qk[sl][1], vt[sl]
                qn, kn, vn = ('qT', sl), ('kT', sl), ('vt', sl)
                proj_fm(qT, qn, wt[sl], ('wt', sl, 0), 0)
                proj_fm(kT, kn, wt[sl], ('wt', sl, 1), 128)

                def cons_v(tt, rows, ps_ap, pbk, eng):
                    evac(eng, V[:rows, tt, 0:128], ps_ap, [('ps', pbk), vn], [(vn, tt)])
                proj_tm(cons_v, wt[sl], ('wt', sl, 2), 256, 128)
                S.op('dve', lambda e: e.tensor_copy(out=qblk[0:64, :, h, 0:4], in_=qT[0:64, TP:T].rearrange("p (b j) -> p b j", j=4)),
                     r=[(qn, 4), 'qblk'], w=[('qblk', h, 0)])
                S.op('dve', lambda e: e.tensor_copy(out=qblk[64:128, :, h, 4:8], in_=qT[64:128, TP:T].rearrange("p (b j) -> p b j", j=4)),
                     r=[(qn, 4), 'qblk'], w=[('qblk', h, 1)])
                S.op('dve', lambda e: e.tensor_copy(out=ks_s[:, h, :], in_=kT[:, TP:T]), r=[(kn, 4)], w=[('ks_s', h)])
                S.op('dve', lambda e: e.tensor_copy(out=vs_s[:, h, :], in_=V[0:64, 16, 0:128]), r=[(vn, 16)], w=[('vs_s', h)])
                qbufs = lambda q0, n: [(qn, b) for b in range(q0 // 512, (q0 + n - 1) // 512 + 1)]
                iters = [(qb, kt) for qb in range(4) for kt in range(4 * qb + 4)]

                def geom(i):
                    qb, kt = iters[i]
                    qt0 = max(kt, 4 * qb)
                    return qb, kt, qt0, (4 * qb + 4 - qt0) * 128, qt0 * 128, i % 2

                def emit_qk(i):
                    qb, kt, qt0, ncol, q0, pair = geom(i)
                    for c in range(2):
                        bank = 2 + 2 * pair + c
                        S.op('pe', lambda e: e.matmul(psum[:, bank, 0:ncol], lhsT=kT[64 * c:64 * c + 64, kt * 128:(kt + 1) * 128],
                                                      rhs=qT[64 * c:64 * c + 64, q0:q0 + ncol], start=True, stop=True),
                             r=[(kn, kt // 4)] + qbufs(q0, ncol), w=[('ps', bank)])
                    for c in range(2):
                        bank = 2 + 2 * pair + c
                        S.op('act', lambda e: e.activation(out=pT[pair][c][:, 0:ncol], in_=psum[:, bank, 0:ncol], func=AF.Exp,
                                                           bias=c31[:, h:h + 1], scale=0.125),
                             r=[('ps', bank), 'c31'], w=[('pT', pair, c)])
                        if qt0 == kt:
                            S.op('dve', lambda e: e.tensor_tensor(out=pT[pair][c][:, 0:128], in0=pT[pair][c][:, 0:128],
                                                                  in1=tz[:, h, 0:128], op=ALU.mult),
                                 r=[('pT', pair, c), 'tz'], w=[('pT', pair, c)])
                        if qt0 <= kt + 1 <= 4 * qb + 3:
                            off = (kt + 1 - qt0) * 128
                            S.op('dve', lambda e: e.tensor_tensor(out=pT[pair][c][:, off:off + 128], in0=pT[pair][c][:, off:off + 128],
                                                                  in1=tz[:, h, 128:256], op=ALU.mult),
                                 r=[('pT', pair, c), 'tz'], w=[('pT', pair, c)])

                def emit_av(i):
                    qb, kt, qt0, ncol, q0, pair = geom(i)
                    for c in range(2):
                        for qt in range(qt0, 4 * qb + 4):
                            jq = qt - 4 * qb
                            oap, obuf = ops(c * 4 + jq)
                            off = (qt - qt0) * 128
                            S.op('pe', lambda e: e.matmul(oap, lhsT=pT[pair][c][:, off:off + 128], rhs=V[:, kt, :],
                                                          start=(kt == 0 and (c * 4 + jq) % 3 == 0), stop=(kt == qt),
                                                          skip_group_check=True),
                                 r=[('pT', pair, c), (vn, kt)], w=[obuf])
                    if kt == 4 * qb + 3:
                        S.op('dve', lambda e: e.tensor_copy(out=osbf[:, 0:387], in_=psum[:, 0, 0:387]), r=[('ps', 0)], w=['osb8a'])
                        S.op('dve', lambda e: e.tensor_copy(out=osbf[:, 387:774], in_=psum[:, 1, 0:387]), r=[('ps', 1)], w=['osb8b'])
                        S.op('dve', lambda e: e.tensor_copy(out=osbf[:, 774:1032], in_=psum[:, 6, 0:258]), r=[('ps', 6)], w=['osb8c'])
                        ob = ['osb8a', 'osb8b', 'osb8c']
                        t0_ = 4 * qb
                        S.op('dve', lambda e: e.reciprocal(out=fs4[:, 0:8], in_=osb8[:, :, 128]), r=ob, w=['fs4'])
                        S.op('dve', lambda e: e.tensor_scalar(out=fs4[:, 4:8], in0=fs4[:, 4:8], scalar1=nlam, scalar2=None, op0=ALU.mult),
                             r=['fs4', 'lsm'], w=['fs4'])
                        S.op('dve', lambda e: e.tensor_tensor(out=fin4[:, :, :], in0=osb8[:, 0:4, 0:128], in1=BC4(fs4[:, 0:4]), op=ALU.mult),
                             r=ob + ['fs4'], w=['fin'])
                        j3 = junk[:, 512:1024].rearrange("p (a b) -> p a b", a=4)
                        S.op('dve', lambda e: e.tensor_tensor(out=j3, in0=osb8[:, 4:8, 0:128], in1=BC4(fs4[:, 4:8]), op=ALU.mult),
                             r=ob + ['fs4'], w=['junk'])
                        S.op('dve', lambda e: e.tensor_tensor(out=fin4[:, :, :], in0=fin4[:, :, :], in1=j3, op=ALU.add), r=['fin', 'junk'], w=['fin'])
                        block_norm(fin4, mixed[:, t0_:t0_ + 4, h * 128:(h + 1) * 128], anb, None, 1.0 - LAM_INIT,
                                   [('mixed', t0_ + j, h) for j in range(4)])

                for i in range(len(iters) + 1):
                    if i < len(iters):
                        emit_qk(i)
                    if i >= 1:
                        emit_av(i - 1)

            for h in range(4):
                attn_head(h, h)

            SCL = 128.0 ** -0.5
            atm = sbm("atm", [128, NT, 16], F32)
            nAb = sbm("nAb", [128, T], F32)
            selt = sbm("selt", [4, 4, 128], F32)
            trim = sbm("trim_sb", [128, 128], F32)
            identf = sbm("identf_sb", [128, 128], F32)
            mfin = sbm("mfin", [4, 64], F32)
            S.dma('sp', lambda e: e.dma_start(out=selt[:], in_=sel_d[:, :, :]), w=['selt'])
            S.dma('sp', lambda e: e.dma_start(out=trim[:], in_=trim_d[:, :]), w=['trim'])
            S.dma('sp', lambda e: e.dma_start(out=identf[:], in_=identf_d[:, :]), w=['identf'])
            G4 = sbm("G4", [4, T], F32)
            with ExitStack() as eg:
                G1 = eg.enter_context(nc.sbuf_tensor("G1", [4, T], F32))
                G2 = eg.enter_context(nc.sbuf_tensor("G2", [4, T], F32))
                G3 = eg.enter_context(nc.sbuf_tensor("G3", [4, T], F32))
                wgt = eg.enter_context(nc.sbuf_tensor("wgt", [128, KD, 8], BF16))
                bg = eg.enter_context(nc.sbuf_tensor("bg", [4, 4], F32))
                m0T = eg.enter_context(nc.sbuf_tensor("m0T", [4, 16], F32))
                S.dma('pool', lambda e: e.dma_start(out=wgt[:], in_=win_d[:, :, 3584:3592]), w=['wgt'])
                S.dma('sp', lambda e: e.dma_start(out=bg[:, 0:2], in_=b_gates.rearrange("g h -> h g"), allow_slow_non_contiguous=True), w=['bg'])
                S.dma('sp', lambda e: e.dma_start(out=m0T[:], in_=sm_in.rearrange("b h -> h b"), allow_slow_non_contiguous=True), w=['m0T'])
                S.op('dve', lambda e: e.tensor_scalar(out=bg[:, 2:3], in0=bg[:, 1:2], scalar1=-1.0, scalar2=None, op0=ALU.mult),
                     r=['bg'], w=['bgn'])
                for gidx, G in ((0, G1), (1, G2)):
                    for blk in range(5):
                        cc, n = blk_cols(blk)
                        hb = [('hT', tt) for tt in (range(4 * blk, 4 * blk + 4) if blk < 4 else [16])]
                        for kc in range(KD):
                            S.op('pe', lambda e, kc=kc: e.matmul(psum[0:4, 7, 0:n], lhsT=wgt[:, kc, 4 * gidx:4 * gidx + 4],
                                                                 rhs=hnT[:, kc, cc:cc + n], start=(kc == 0), stop=(kc == KD - 1)),
                                 r=['wgt'] + hb, w=[('ps', 7)], inc=(kc == KD - 1))
                        S.op('act', lambda e: e.copy(out=G[:, cc:cc + n], in_=psum[0:4, 7, 0:n]), r=[('ps', 7)], w=[('G', gidx)])
                S.op('act', lambda e: e.activation(out=G2[:, :], in_=G2[:, :], func=AF.Exp, bias=bg[:, 2:3], scale=-1.0),
                     r=[('G', 1), 'bgn'], w=[('G', 1)])
                S.op('act', lambda e: e.activation(out=G2[:, :], in_=G2[:, :], func=AF.Ln, bias=1.0), r=[('G', 1)], w=[('G', 1)])
                S.op('dve', lambda e: e.tensor_tensor_scan(out=G3[:, 0:TP], data0=G2[:, 0:TP], data1=G2[:, 0:TP], initial=0.0,
                                                           op0=ALU.add, op1=ALU.max), r=[('G', 1)], w=[('G', 2)])
                l3 = G2[:, TP:T].rearrange("p (b j) -> p b j", j=4)
                B3 = G3[:, TP:T].rearrange("p (b j) -> p b j", j=4)
                S.op('dve', lambda e: e.tensor_copy(out=B3[:, :, 0], in_=l3[:, :, 0]), r=[('G', 1)], w=[('G', 2)])
                for j in range(1, 4):
                    S.op('dve', lambda e: e.tensor_tensor(out=B3[:, :, j], in0=B3[:, :, j - 1], in1=l3[:, :, j], op=ALU.add),
                         r=[('G', 1), ('G', 2)], w=[('G', 2)])
                S.op('dve', lambda e: e.scalar_tensor_tensor(out=G1[:, :], in0=G1[:, :], scalar=bg[:, 0:1], in1=G3[:, :],
                                                             op0=ALU.add, op1=ALU.add), r=[('G', 0), ('G', 2), 'bg'], w=[('G', 0)])
                S.op('dve', lambda e: e.tensor_tensor_scan(out=G4[:, 0:TP], data0=G1[:, 0:TP], data1=G1[:, 0:TP], initial=0.0,
                                                           op0=ALU.max, op1=ALU.max), r=[('G', 0)], w=[('G', 3)])
                a3 = G1[:, TP:T].rearrange("p (b j) -> p b j", j=4)
                A3 = G4[:, TP:T].rearrange("p (b j) -> p b j", j=4)
                S.op('dve', lambda e: e.tensor_tensor(out=A3[:, :, 0], in0=a3[:, :, 0], in1=m0T[:, :], op=ALU.max),
                     r=[('G', 0), 'm0T'], w=[('G', 3)])
                for j in range(1, 4):
                    S.op('dve', lambda e: e.tensor_tensor(out=A3[:, :, j], in0=A3[:, :, j - 1], in1=a3[:, :, j], op=ALU.max),
                         r=[('G', 0), ('G', 3)], w=[('G', 3)])
                S.op('dve', lambda e: e.tensor_tensor(out=mfin[:, 0:1], in0=G4[:, TP - 1:TP], in1=G3[:, TP - 1:TP], op=ALU.subtract),
                     r=[('G', 2), ('G', 3)], w=['mfin0'])
                S.op('dve', lambda e: e.tensor_tensor(out=mfin[:, 16:32], in0=A3[:, :, 3], in1=B3[:, :, 3], op=ALU.subtract),
                     r=[('G', 2), ('G', 3)], w=['mfin1'])
                S.op('dve', lambda e: e.tensor_tensor(out=mfin[:, 32:48], in0=m0T[:, :], in1=A3[:, :, 3], op=ALU.subtract),
                     r=['m0T', ('G', 3)], w=['mfin2'])
                S.op('act', lambda e: e.activation(out=mfin[:, 32:48], in_=mfin[:, 32:48], func=AF.Exp), r=['mfin2'], w=['mfin2'])
                S.dma('sp', lambda e: e.dma_start(out=mp_out.rearrange("o h -> h o"), in_=mfin[:, 0:1], allow_slow_non_contiguous=True),
                      r=['mfin0'], w=['mp_out'], is_out=True)
                S.dma('sp', lambda e: e.dma_start(out=ms_out.rearrange("b h -> h b"), in_=mfin[:, 16:32], allow_slow_non_contiguous=True),
                      r=['mfin1'], w=['ms_out'], is_out=True)
                S.op('dve', lambda e: e.tensor_tensor(out=G3[:, :], in0=G3[:, :], in1=G4[:, :], op=ALU.subtract),
                     r=[('G', 2), ('G', 3), 'mfin0', 'mfin1'], w=[('G', 2)])
                S.op('act', lambda e: e.activation(out=G3[:, :], in_=G3[:, :], func=AF.Exp), r=[('G', 2)], w=[('G', 2)])
                w3 = G2[:, TP:T].rearrange("p (b j) -> p b j", j=4)
                for j in range(4):
                    S.op('dve', lambda e: e.tensor_tensor(out=w3[:, :, j], in0=m0T[:, :], in1=A3[:, :, j], op=ALU.subtract),
                         r=['m0T', ('G', 3), ('G', 1)], w=[('G', 1)])
                S.op('act', lambda e: e.activation(out=G2[:, TP:T], in_=G2[:, TP:T], func=AF.Exp), r=[('G', 1)], w=[('G', 1)])
                x3 = G2[:, 0:TS].rearrange("p (b j) -> p b j", j=4)
                for j in range(4):
                    S.op('dve', lambda e: e.tensor_tensor(out=x3[:, :, j], in0=a3[:, :, j], in1=A3[:, :, 3], op=ALU.subtract),
                         r=[('G', 0), ('G', 3), ('G', 1)], w=[('G', 1)])
                S.op('act', lambda e: e.activation(out=G2[:, 0:TS], in_=G2[:, 0:TS], func=AF.Exp), r=[('G', 1)], w=[('G', 1)])
                for tt in range(NT):
                    rows = rows_of(tt)
                    r0 = tt * 128
                    srcs = [(G1, 0, ('G', 0), r0), (G3, 4, ('G', 2), r0)]
                    if tt == 16:
                        srcs += [(G2, 8, ('G', 1), r0), (G2, 12, ('G', 1), 0)]
                    for (G, co, gname, c0_) in srcs:
                        S.op('pe', lambda e: e.matmul(psum[:rows, 7, co:co + 4], lhsT=G[0:4, c0_:c0_ + rows], rhs=identf[0:4, 0:4],
                                                      start=True, stop=True), r=[gname, 'identf'], w=[('ps', 7)])
                    ncp = 16 if tt == 16 else 8
                    S.op('act', lambda e: e.copy(out=atm[:rows, tt, 0:ncp], in_=psum[:rows, 7, 0:ncp]), r=[('ps', 7)], w=[('atm', tt)])
                S.op('dve', lambda e: e.tensor_scalar(out=G4[:, :], in0=G4[:, :], scalar1=-1.0, scalar2=None, op0=ALU.mult),
                     r=[('G', 3)], w=[('G', 3)])
                dbg('G1', G1[:, :], [4, T], F32, [('G', 0)])
                dbg('G4', G4[:, :], [4, T], F32, [('G', 3)])
                dbg('G3', G3[:, :], [4, T], F32, [('G', 2)])

            S.freed()
            ktm = [sbm(f"ktm_{i}", [128, NT, 128], BF16) for i in range(2)]
            sig = [sbm(f"sig_{i}", [128, NT, 128], BF16) for i in range(2)]
            dtl = [sbm(f"dt_{i}", [128, 512], F32) for i in range(2)]
            kw = sbm("kw", [128, 128], BF16)
            wsc = sbm("wsc", [128, 16], F32)
            cst = sbm("cst", [128, 129], F32)

            fwb = sbm("fwb", [128, 64], F32)
            for h in range(4):
                S.op('pe', lambda e: e.matmul(psum[:, 7, 0:16], lhsT=selt[0:4, h, :], rhs=mfin[0:4, 32:48], start=True, stop=True),
                     r=['selt', 'mfin2'], w=[('ps', 7)])
                S.op('act', lambda e: e.copy(out=fwb[:, h * 16:(h + 1) * 16], in_=psum[:, 7, 0:16]), r=[('ps', 7)], w=[('fwb', h)])
            C0x = sbm("C0x", [128, 16, 130], BF16)
            snt = sbm("snt", [64, 128], F32)
            n0f = sbm("n0f", [128, 64], F32)
            nnew = sbm("nnew", [128, 64], F32)
            bmask = sbm("bmask_sb", [64, 64], F32)
            rowsel = sbm("rowsel_sb", [64, 16], F32)
            qmask = sbm("qmask", [128, 16, 64], BF16)
            kwm = sbm("kwm", [64, 16, 128], BF16)
            kws = sbm("kws", [64, 128], BF16)
            c0f = [sbm(f"c0f_{i}", [128, 128], F32) for i in range(2)]
            cnew = [sbm(f"cnew_{i}", [128, 129], F32) for i in range(2)]
            dts = sbm("dts", [64, 64], F32)
            pTs = sbm("pTs", [64, 64], BF16)
            ist = sbm("ist", [64, 129], F32)
            sC_v = sC_in.rearrange("bh k v -> k bh v")
            sC_h = sC_in.rearrange("(b h) k v -> k h b v", h=4)
            S.dma('sp', lambda e: e.dma_start(out=snt[:], in_=sn_in[:, :]), w=['snt'])
            S.op('pe', lambda e: e.matmul(psum[:, 7, 0:64], lhsT=snt[:, :], rhs=identf[0:64, 0:64], start=True, stop=True),
                 r=['snt', 'identf'], w=[('ps', 7)])
            S.op('act', lambda e: e.copy(out=n0f[:, :], in_=psum[:, 7, 0:64]), r=[('ps', 7)], w=['n0f'])
            S.dma('sp', lambda e: e.dma_start(out=bmask[:], in_=bmask_d[:, :]), w=['bmask'])
            S.dma('sp', lambda e: e.dma_start(out=rowsel[:], in_=rowsel_d[:, :]), w=['rowsel'])
            S.op('dve', lambda e: e.memset(qmask[:], 0.0), w=['qmask'])

            def mlstm_sample(h, sl):
                qT, kT, V = qk[sl][0], qk[sl][1], vt[sl]
                qn, kn, vn = ('qT', sl), ('kT', sl), ('vt', sl)
                if 'ms_a' not in SKIP:
                    mlstm_sample_a(h, sl)
                mlstm_sample_b(h, sl)

            def mlstm_sample_a(h, sl):
                qT, kT, V = qk[sl][0], qk[sl][1], vt[sl]
                qn, kn, vn = ('qT', sl), ('kT', sl), ('vt', sl)
                S.op('pe', lambda e: e.matmul(psum[0:64, 2, 0:64], lhsT=kT[:, TP:T], rhs=qT[:, TP:T], start=True, stop=True),
                     r=[(kn, 4), (qn, 4)], w=[('ps', 2)])
                S.op('act', lambda e: e.activation(out=dts[:, :], in_=nAb[0:64, TP:T], func=AF.Exp, bias=atm[0:64, 16, h:h + 1]),
                     r=[('nAb', 4), ('atm', 16)], w=['dts'])
                S.op('dve', lambda e: e.tensor_tensor(out=dts[:, :], in0=dts[:, :], in1=bmask[:, :], op=ALU.mult), r=['dts', 'bmask'], w=['dts'])
                S.op('dve', lambda e: e.scalar_tensor_tensor(out=pTs[:, :], in0=psum[0:64, 2, 0:64], scalar=SCL, in1=dts[:, :],
                                                             op0=ALU.mult, op1=ALU.mult), r=[('ps', 2), 'dts'], w=['pTs'])
                S.op('pe', lambda e: e.matmul(psum[0:64, 3, 0:129], lhsT=pTs[:, :], rhs=V[0:64, 16, :], start=True, stop=True),
                     r=['pTs', (vn, 16)], w=[('ps', 3)])
                for b in range(16):
                    S.op('dve', lambda e: e.tensor_copy(out=qmask[:, b, 4 * b:4 * b + 4], in_=qT[:, TP + 4 * b:TP + 4 * b + 4]),
                         r=[(qn, 4), 'qmask'], w=[('qmask', b)])
                for g in range(2):
                    S.dma('pool', lambda e: e.dma_start(out=C0x[:, g * 8:(g + 1) * 8, 0:128], in_=sC_h[:, h, g * 8:(g + 1) * 8, :]),
                          w=[('C0x', g)])
                S.op('act', lambda e: e.copy(out=C0x[:, :, 128], in_=n0f[:, :].rearrange("p (b h) -> p h b", h=4)[:, h, :]),
                     r=['n0f'], w=['C0xn'])
                for b in range(16):
                    S.op('pe', lambda e: e.matmul(psum[0:64, 4, 0:129], lhsT=qmask[:, b, :], rhs=C0x[:, b, 0:129],
                                                  start=(b == 0), stop=(b == 15)),
                         r=[('qmask', b), ('C0x', b // 8), 'C0xn'], w=[('ps', 4)], inc=(b == 15))
                S.op('act', lambda e: e.copy(out=ist[:, :], in_=psum[0:64, 4, 0:129]), r=[('ps', 4)], w=['ist'])
                S.op('dve', lambda e: e.scalar_tensor_tensor(out=cst[0:64, :], in0=ist[:, :], scalar=atm[0:64, 16, 8 + h:9 + h],
                                                             in1=psum[0:64, 3, 0:129], op0=ALU.mult, op1=ALU.add),
                     r=['ist', ('atm', 16), ('ps', 3)], w=['cst'])
                mlstm_finalize(h, sl, 16, 64, cst, ['cst'])

            def mlstm_sample_b(h, sl):
                qT, kT, V = qk[sl][0], qk[sl][1], vt[sl]
                qn, kn, vn = ('qT', sl), ('kT', sl), ('vt', sl)
                if 'ms_b' in SKIP:
                    return
                S.op('dve', lambda e: e.tensor_scalar(out=kws[:, :], in0=ktm[sl][0:64, 16, :], scalar1=atm[0:64, 16, 12 + h:13 + h], scalar2=SCL,
                                                      op0=ALU.mult, op1=ALU.mult), r=[('ktm', sl, 16), ('atm', 16)], w=['kws'])
                for b in range(16):
                    S.op('dve', lambda e: e.tensor_scalar(out=kwm[:, b, :], in0=kws[:, :], scalar1=rowsel[:, b:b + 1], scalar2=None, op0=ALU.mult),
                         r=['kws', 'rowsel'], w=[('kwm', b)])
                for b in range(16):
                    cb = b % 2
                    bank = 5 + cb
                    bh = b * 4 + h
                    S.dma('sp', lambda e: e.dma_start(out=c0f[cb][:], in_=sC_in[bh]), w=[('c0f', cb)])
                    S.op('pe', lambda e: e.matmul(psum[:, bank, 0:129], lhsT=kwm[:, b, :], rhs=V[0:64, 16, :], start=True, stop=True),
                         r=[('kwm', b), (vn, 16)], w=[('ps', bank)])
                    S.op('dve', lambda e: e.scalar_tensor_tensor(out=cnew[cb][:, 0:128], in0=c0f[cb][:, :], scalar=fwb[:, h * 16 + b:h * 16 + b + 1],
                                                                 in1=psum[:, bank, 0:128], op0=ALU.mult, op1=ALU.add),
                         r=[('c0f', cb), ('fwb', h), ('ps', bank)], w=[('cnew', cb)])
                    S.op('dve', lambda e: e.scalar_tensor_tensor(out=nnew[:, bh:bh + 1], in0=n0f[:, bh:bh + 1], scalar=fwb[:, h * 16 + b:h * 16 + b + 1],
                                                                 in1=psum[:, bank, 128:129], op0=ALU.mult, op1=ALU.add),
                         r=['n0f', ('fwb', h), ('ps', bank)], w=[('nnew', bh)])
                    S.dma('sp', lambda e: e.dma_start(out=cs_out[bh], in_=cnew[cb][:, 0:128]), r=[('cnew', cb)], w=[('cs_out', bh)], is_out=True)


            def mlstm_head(h, it):
                sl = it % 2
                for j, col0 in enumerate([1536 + h * 128, 2048 + h * 128, 2560 + h * 128, 3072 + h * 128]):
                    S.dma('pool', lambda e: e.dma_start(out=wt[sl][:, :, j * 128:(j + 1) * 128], in_=win_d[:, :, col0:col0 + 128]),
                          w=[('wt', sl, j)])
                qT, kT, V = qk[sl][0], qk[sl][1], vt[sl]
                qn, kn, vn = ('qT', sl), ('kT', sl), ('vt', sl)
                for blk in range(5):
                    cc, n = blk_cols(blk)
                    S.op('pe', lambda e: e.matmul(psum[:, 7, 0:n], lhsT=selt[0:4, h, :], rhs=G4[0:4, cc:cc + n], start=True, stop=True),
                         r=['selt', ('G', 3)], w=[('ps', 7)])
                    S.op('act', lambda e: e.copy(out=nAb[:, cc:cc + n], in_=psum[:, 7, 0:n]), r=[('ps', 7)], w=[('nAb', blk)])
                proj_fm(qT, qn, wt[sl], ('wt', sl, 0), 0)
                proj_fm(kT, kn, wt[sl], ('wt', sl, 1), 128)

                def cons_k(tt, rows, ps_ap, pbk, eng):
                    evac(eng, ktm[sl][:rows, tt, :], ps_ap, [('ps', pbk)], [('ktm', sl, tt)])
                proj_tm(cons_k, wt[sl], ('wt', sl, 1), 128, 128)

                def cons_v(tt, rows, ps_ap, pbk, eng):
                    evac(eng, V[:rows, tt, 0:128], ps_ap, [('ps', pbk), vn], [(vn, tt)])
                proj_tm(cons_v, wt[sl], ('wt', sl, 2), 256, 128)

                def cons_o(tt, rows, ps_ap, pbk, eng):
                    S.op('act', lambda e: e.activation(out=sig[sl][:rows, tt, :], in_=ps_ap, func=AF.Sigmoid),
                         r=[('ps', pbk)], w=[('sig', sl, tt)])
                proj_tm(cons_o, wt[sl], ('wt', sl, 3), 384, 128)
                qbufs = lambda q0, n: [(qn, b) for b in range(q0 // 512, (q0 + n - 1) // 512 + 1)]
                nbufs = lambda q0, n: [('nAb', b) for b in range(q0 // 512, (q0 + n - 1) // 512 + 1)]
                iters = [(qb, kt) for qb in range(4) for kt in range(4 * qb + 4)]

                def geom(i):
                    qb, kt = iters[i]
                    qt0 = max(kt, 4 * qb)
                    return qb, kt, qt0, (4 * qb + 4 - qt0) * 128, qt0 * 128, 2 + i % 4, i % 2

                def emit_qk(i):
                    qb, kt, qt0, ncol, q0, bank, db = geom(i)
                    S.op('pe', lambda e: e.matmul(psum[:, bank, 0:ncol], lhsT=kT[:, kt * 128:(kt + 1) * 128],
                                                  rhs=qT[:, q0:q0 + ncol], start=True, stop=True),
                         r=[(kn, kt // 4)] + qbufs(q0, ncol), w=[('ps', bank)])
                    S.op('act', lambda e: e.activation(out=dtl[db][:, 0:ncol], in_=nAb[:, q0:q0 + ncol], func=AF.Exp,
                                                       bias=atm[:, kt, h:h + 1]),
                         r=nbufs(q0, ncol) + [('atm', kt)], w=[('dt', db)])
                    if qt0 == kt:
                        S.op('dve', lambda e: e.tensor_tensor(out=dtl[db][:, 0:128], in0=dtl[db][:, 0:128], in1=trim[:, :], op=ALU.mult),
                             r=[('dt', db), 'trim'], w=[('dt', db)])
                    S.op('dve', lambda e: e.scalar_tensor_tensor(out=pT[db][0][:, 0:ncol], in0=psum[:, bank, 0:ncol], scalar=SCL,
                                                                 in1=dtl[db][:, 0:ncol], op0=ALU.mult, op1=ALU.mult),
                         r=[('ps', bank), ('dt', db)], w=[('pT', db, 0)])

                def emit_av(i):
                    qb, kt, qt0, ncol, q0, bank, db = geom(i)
                    for qt in range(qt0, 4 * qb + 4):
                        jq = qt - 4 * qb
                        oap, obuf = ops(jq)
                        off = (qt - qt0) * 128
                        S.op('pe', lambda e: e.matmul(oap, lhsT=pT[db][0][:, off:off + 128], rhs=V[:, kt, :],
                                                      start=(kt == 0 and jq % 3 == 0), stop=(kt == qt), skip_group_check=True),
                             r=[('pT', db, 0), (vn, kt)], w=[obuf])
                    if kt == 4 * qb + 3:
                        S.op('dve', lambda e: e.tensor_copy(out=osbf[:, 0:387], in_=psum[:, 0, 0:387]), r=[('ps', 0)], w=['osb8a'])
                        S.op('dve', lambda e: e.tensor_copy(out=osbf[:, 387:516], in_=psum[:, 1, 0:129]), r=[('ps', 1)], w=['osb8b'])
                        ob = ['osb8a', 'osb8b']
                        t0_ = 4 * qb
                        S.op('dve', lambda e: e.tensor_scalar(out=fs4[:, 0:4], in0=osb8[:, 0:4, 128], scalar1=-1.0, scalar2=None, op0=ALU.mult),
                             r=ob, w=['fs4'])
                        S.op('dve', lambda e: e.tensor_tensor(out=fs4[:, 0:4], in0=fs4[:, 0:4], in1=osb8[:, 0:4, 128], op=ALU.max),
                             r=ob + ['fs4'], w=['fs4'])
                        S.op('dve', lambda e: e.tensor_tensor(out=fs4[:, 0:4], in0=fs4[:, 0:4], in1=atm[:, t0_:t0_ + 4, 4 + h], op=ALU.max),
                             r=['fs4'] + [('atm', t0_ + j) for j in range(4)], w=['fs4'])
                        S.op('dve', lambda e: e.reciprocal(out=fs4[:, 0:4], in_=fs4[:, 0:4]), r=['fs4'], w=['fs4'])
                        S.op('dve', lambda e: e.tensor_tensor(out=fin4[:, :, :], in0=osb8[:, 0:4, 0:128], in1=BC4(fs4[:, 0:4]), op=ALU.mult),
                             r=ob + ['fs4'], w=['fin'])
                        block_norm(fin4, mixed[:, t0_:t0_ + 4, 512 + h * 128:512 + (h + 1) * 128], mnb, sig[sl][:, t0_:t0_ + 4, :], 1.0,
                                   [('mixed', t0_ + j, 4 + h) for j in range(4)], extra_r=[('sig', sl, t0_ + j) for j in range(4)])

                if 'mloop' not in SKIP:
                    for i in range(len(iters) + 1):
                        if i < len(iters):
                            emit_qk(i)
                        if i >= 1:
                            emit_av(i - 1)
                S.op('act', lambda e: e.activation(out=wsc[:, 0:16], in_=atm[:, 0:16, h], func=AF.Exp, bias=nAb[:, TP - 1:TP]),
                     r=[('atm', tt) for tt in range(16)] + [('nAb', 3)], w=['wsc'])
                for kt in range(16 if 'mstate' not in SKIP else 0):
                    S.op('dve', lambda e: e.tensor_scalar(out=kw[:, :], in0=ktm[sl][:, kt, :], scalar1=wsc[:, kt:kt + 1], scalar2=SCL,
                                                          op0=ALU.mult, op1=ALU.mult),
                         r=[('ktm', sl, kt), 'wsc'], w=['kw'])
                    S.op('pe', lambda e: e.matmul(psum[:, 6, 0:129], lhsT=kw[:, :], rhs=V[:, kt, :], start=(kt == 0), stop=(kt == 15)),
                         r=['kw', (vn, kt)], w=[('ps', 6)])
                S.op('act', lambda e: e.copy(out=cst[:, :], in_=psum[:, 6, 0:129]), r=[('ps', 6)], w=['cst'])
                S.dma('sp', lambda e: e.dma_start(out=cp_out[h], in_=cst[:, 0:128]), r=['cst'], w=[('cp_out', h)], is_out=True)
                S.dma('sp', lambda e: e.dma_start(out=np_out[h:h + 1, :].rearrange("o d -> d o"), in_=cst[:, 128:129],
                                                  allow_slow_non_contiguous=True), r=['cst'], w=[('np_out', h)], is_out=True)
                if 'msample' not in SKIP:
                    mlstm_sample(h, sl)

            def mlstm_finalize(h, sl, tt, rows, oap, obufs):
                S.op('dve', lambda e: e.tensor_scalar(out=fsm[:rows, 5:6], in0=oap[:rows, 128:129], scalar1=-1.0, scalar2=None, op0=ALU.mult),
                     r=obufs, w=['fsm5'])
                S.op('dve', lambda e: e.tensor_tensor(out=fsm[:rows, 4:5], in0=oap[:rows, 128:129], in1=fsm[:rows, 5:6], op=ALU.max),
                     r=obufs + ['fsm5'], w=['fsm4'])
                S.op('dve', lambda e: e.tensor_tensor(out=fsm[:rows, 4:5], in0=fsm[:rows, 4:5], in1=atm[:rows, tt, 4 + h:5 + h], op=ALU.max),
                     r=['fsm4', ('atm', tt)], w=['fsm4'])
                S.op('dve', lambda e: e.reciprocal(out=fsm[:rows, 4:5], in_=fsm[:rows, 4:5]), r=['fsm4'], w=['fsm4'])
                S.op('dve', lambda e: e.tensor_scalar(out=fin[:rows, :], in0=oap[:rows, 0:128], scalar1=fsm[:rows, 4:5], scalar2=None, op0=ALU.mult),
                     r=obufs + ['fsm4'], w=['fin'])
                sumsq(fsm[:rows, 2:3], fin[:rows, :], rows, 128, ['fin'], 'fsm2')
                rstd_act(fsm[:rows, 3:4], fsm[:rows, 2:3], 128, ['fsm2'], 'fsm3')
                S.op('dve', lambda e: e.scalar_tensor_tensor(out=fin[:rows, :], in0=fin[:rows, :], scalar=fsm[:rows, 3:4], in1=mnb[:rows, :],
                                                             op0=ALU.mult, op1=ALU.mult), r=['fin', 'fsm3', 'mnb'], w=['fin'])
                S.op('dve', lambda e: e.tensor_tensor(out=mixed[:rows, tt, 512 + h * 128:512 + (h + 1) * 128], in0=fin[:rows, :],
                                                      in1=sig[sl][:rows, tt, :], op=ALU.mult),
                     r=['fin', ('sig', sl, tt)], w=[('mixed', tt, 4 + h)])

            for h in range(4):
                if 'mheads' not in SKIP:
                    mlstm_head(h, h)
            if 'mheads' not in SKIP and 'msample' not in SKIP and 'ms_b' not in SKIP:
                S.op('pe', lambda e: e.matmul(psum[0:64, 7, 0:128], lhsT=nnew[:, :], rhs=identf[:, :], start=True, stop=True),
                     r=[('nnew', bh) for bh in range(64)] + ['identf'], w=[('ps', 7)])
                S.op('act', lambda e: e.copy(out=snt[:, :], in_=psum[0:64, 7, 0:128]), r=[('ps', 7)], w=['snt'])
                S.dma('sp', lambda e: e.dma_start(out=ns_out[:, :], in_=snt[:, :]), r=['snt'], w=['ns_out'], is_out=True)
            eh.close()
            S.freed()
            if stage > 4:
                sample_attn(hnT, qblk, ks_s, vs_s, tz, lsm, anb)
                S.freed()
            dbg('mixed', mixed[:, :, :], [128, NT, D], BF16, [('mixed', tt, h) for tt in range(16) for h in range(8)])
            dbg('hnT', hnT[:, :, :], [128, KD, T], BF16, [('hT', 16)])
            if stage > 3:
                out_proj(em, hnT, mixed, wt)

        def sample_attn(hnT, qblk, ks_s, vs_s, tz, lsm, anb):
            nlam = lsm[:, 5:6]
            with ExitStack() as ea:
                def sba(name, shape, dt):
                    return ea.enter_context(nc.sbuf_tensor(name, list(shape), dt))
                ptb = sba("ptb", [128, 256], I32)
                iot = sba("iot", [128, 256], I32)
                idx = sba("idx", [128, 256], I32)
                identf = sba("identf_sa", [128, 128], F32)
                rmod = sba("rmod_sb", [4, 64], F32)
                rowsel = sba("rowsel_sa", [64, 16], F32)
                tzN = sba("tzN", [4, 32], F32)
                tmpN = sba("tmpN", [64, 32], F32)
                corrN = sba("corrN", [64, 16, 32], F32)
                corr15 = sba("corr15", [128, 32], F32)
                KVb = [sba(f"KVb_{i}", [128, 16, 1024], BF16) for i in range(2)]
                Kb = [KVb[i][:, :, 0:512] for i in range(2)]
                Vb = [KVb[i][:, :, 512:1024] for i in range(2)]
                onesb = sba("onesb", [128, 2], BF16)
                sel01 = sba("sel01_sb", [32, 32], F32)
                dsel = sba("dsel", [32, 16], F32)
                bm16 = sba("bm16_sb", [16, 512], F32)
                anb4 = sba("anb4", [16, 512], F32)
                rs32 = sba("rs32", [32, 2], F32)
                on32 = sba("on32", [32, 512], F32)
                a16 = sba("a16", [16, 512], F32)
                rs16 = sba("rs16", [16, 4], F32)
                kTb = sba("kTb", [128, 4, 2048], BF16)
                pTb = sba("pTb", [128, 512], BF16)
                pTn = sba("pTn", [64, 32], BF16)
                S.dma('sp', lambda e: e.dma_start(out=ptb[:], in_=pt_in[0:1, :].partition_broadcast(128)), w=['ptb'])
                S.dma('sp', lambda e: e.dma_start(out=identf[:], in_=identf_d[:, :]), w=['identf'])
                S.dma('sp', lambda e: e.dma_start(out=rmod[:], in_=rmod_d[:, :]), w=['rmod'])
                S.dma('sp', lambda e: e.dma_start(out=rowsel[:], in_=rowsel_d[:, :]), w=['rowsel'])
                S.op('pool', lambda e: e.iota(iot[:], pattern=[[0, 256]], base=0, channel_multiplier=1), w=['iot'])
                S.op('dve', lambda e: e.scalar_tensor_tensor(out=idx[:], in0=ptb[:], scalar=128, in1=iot[:], op0=ALU.mult, op1=ALU.add),
                     r=['ptb', 'iot'], w=['idx'])
                S.op('dve', lambda e: e.memset(onesb[:], 1.0), w=['onesb'])
                S.dma('sp', lambda e: e.dma_start(out=sel01[:], in_=sel01_d[:, :]), w=['sel01'])
                S.dma('sp', lambda e: e.dma_start(out=bm16[:], in_=bm16_d[:, :]), w=['bm16'])
                for h in range(4):
                    S.dma('sp', lambda e: e.dma_start(out=anb4[:, h * 128:(h + 1) * 128], in_=attn_norm[0:1, :].partition_broadcast(16)),
                          w=['anb4'])
                S.op('dve', lambda e: e.scalar_tensor_tensor(out=dsel[:, :], in0=sel01[:, 16:32], scalar=nlam[0:32, :], in1=sel01[:, 0:16],
                                                             op0=ALU.mult, op1=ALU.add), r=['sel01', 'lsm'], w=['dsel'])
                for h in range(4):
                    for c in range(2):
                        S.op('dve', lambda e: e.tensor_copy(out=corr15[:, h * 8 + c * 4:h * 8 + c * 4 + 4], in_=tz[:, h, 128:132]),
                             r=['tz'], w=[('corr15', h, c)])
                        S.op('dve', lambda e: e.tensor_copy(out=tzN[:, h * 8 + c * 4:h * 8 + c * 4 + 4], in_=tz[0:4, h, 0:4]),
                             r=['tz'], w=[('tzN', h, c)])
                S.op('pe', lambda e: e.matmul(psum[0:64, 7, 0:32], lhsT=rmod[:, :], rhs=tzN[:, :], start=True, stop=True),
                     r=['rmod'] + [('tzN', h, c) for h in range(4) for c in range(2)], w=[('ps', 7)])
                S.op('act', lambda e: e.copy(out=tmpN[:, :], in_=psum[0:64, 7, 0:32]), r=[('ps', 7)], w=['tmpN'])
                for b in range(16):
                    S.op('dve', lambda e: e.tensor_scalar(out=corrN[:, b, :], in0=tmpN[:, :], scalar1=rowsel[:, b:b + 1], scalar2=None, op0=ALU.mult),
                         r=['tmpN', 'rowsel'], w=[('corrN', b)])
                c15 = [('corr15', h, c) for h in range(4) for c in range(2)]

                def gather(b):
                    sl = b % 2
                    for pg in range(16):
                        col = b * 16 + pg
                        S.dma('pool', lambda e: e.indirect_dma_start(out=KVb[sl][:, pg, :], out_offset=None, in_=kv_in[:, :],
                                                                      in_offset=bass.IndirectOffsetOnAxis(ap=idx[:, col:col + 1], axis=0)),
                              r=['idx'], w=[('Kb', sl, pg), ('Vb', sl, pg)])

                gather(0)
                for b in range(16):
                    sl = b % 2
                    if b + 1 < 16:
                        gather(b + 1)
                    tb = 0
                    for h in range(4):
                        for half in range(2):
                            bank = tb % 2
                            tb += 1
                            pv = psb16(bank)
                            for p8 in range(8):
                                pg = half * 8 + p8
                                S.op('pe', lambda e: e.transpose(out=pv[:, p8 * 128:(p8 + 1) * 128], in_=Kb[sl][:, pg, h * 128:(h + 1) * 128],
                                                                 identity=identb[:, :]),
                                     r=[('Kb', sl, pg), 'identb'], w=[('ps', bank)], inc=(p8 == 7))
                            S.op('act', lambda e: e.copy(out=kTb[:, h, half * 1024:(half + 1) * 1024], in_=pv[:, 0:1024]),
                                 r=[('ps', bank)], w=[('kTb', h, half)])
                    for h in range(4):
                        for pg in range(16):
                            S.op('pe', lambda e: e.matmul(psum[:, 2, pg * 32 + h * 8:pg * 32 + h * 8 + 8], lhsT=kTb[:, h, pg * 128:(pg + 1) * 128],
                                                          rhs=qblk[:, b, h, :], start=True, stop=True, skip_group_check=True),
                                 r=[('kTb', h, pg // 8), ('qblk', h, 0), ('qblk', h, 1)], w=[('ps', 2)], inc=(h == 3 and pg == 15))
                    for h in range(4):
                        S.op('pe', lambda e: e.matmul(psum[0:64, 3, h * 8:h * 8 + 8], lhsT=ks_s[:, h, :], rhs=qblk[:, b, h, :],
                                                      start=True, stop=True, skip_group_check=True),
                             r=[('ks_s', h), ('qblk', h, 0), ('qblk', h, 1)], w=[('ps', 3)], inc=(h == 3))
                    S.op('act', lambda e: e.activation(out=pTb[:, :], in_=psum[:, 2, :], func=AF.Exp, scale=0.125), r=[('ps', 2)], w=['pTb'])
                    S.op('dve', lambda e: e.tensor_tensor(out=pTb[:, 480:512], in0=pTb[:, 480:512], in1=corr15[:, :], op=ALU.mult),
                         r=['pTb'] + c15, w=['pTb'])
                    S.op('act', lambda e: e.activation(out=pTn[:, :], in_=psum[0:64, 3, 0:32], func=AF.Exp, scale=0.125), r=[('ps', 3)], w=['pTn'])
                    S.op('dve', lambda e: e.tensor_tensor(out=pTn[:, :], in0=pTn[:, :], in1=corrN[:, b, :], op=ALU.mult),
                         r=['pTn', ('corrN', b)], w=['pTn'])
                    for pg in range(16):
                        S.op('pe', lambda e: e.matmul(psum[0:32, 4, :], lhsT=pTb[:, pg * 32:(pg + 1) * 32], rhs=Vb[sl][:, pg, :],
                                                      start=(pg == 0), stop=False), r=['pTb', ('Vb', sl, pg)], w=[('ps', 4)], inc=False)
                    S.op('pe', lambda e: e.matmul(psum[0:32, 4, :], lhsT=pTn[:, :], rhs=vs_s[:, :, :].rearrange("p h d -> p (h d)"),
                                                  start=False, stop=True), r=['pTn'] + [('vs_s', h) for h in range(4)], w=[('ps', 4)])
                    for pg in range(16):
                        S.op('pe', lambda e: e.matmul(psum[0:32, 5, 0:1], lhsT=pTb[:, pg * 32:(pg + 1) * 32], rhs=onesb[:, 0:1],
                                                      start=(pg == 0), stop=False), r=['pTb', 'onesb'], w=[('ps', 5)], inc=False)
                    S.op('pe', lambda e: e.matmul(psum[0:32, 5, 0:1], lhsT=pTn[:, :], rhs=onesb[0:64, 0:1], start=False, stop=True),
                         r=['pTn', 'onesb'], w=[('ps', 5)])
                    S.op('dve', lambda e: e.reciprocal(out=rs32[:, 0:1], in_=psum[0:32, 5, 0:1]), r=[('ps', 5)], w=['rs32'])
                    S.op('dve', lambda e: e.tensor_scalar(out=on32[:, :], in0=psum[0:32, 4, :], scalar1=rs32[:, 0:1], scalar2=None, op0=ALU.mult),
                         r=[('ps', 4), 'rs32'], w=['on32'])
                    S.op('pe', lambda e: e.matmul(psum[0:16, 6, :], lhsT=dsel[:, :], rhs=on32[:, :], start=True, stop=True),
                         r=['dsel', 'on32'], w=[('ps', 6)])
                    S.op('dve', lambda e: e.tensor_tensor(out=a16[:, :], in0=psum[0:16, 6, :], in1=bm16[:, :], op=ALU.mult),
                         r=[('ps', 6), 'bm16'], w=['a16'])
                    sumsq(rs16[:, 0:1], a16[:, :], 16, 512, ['a16'], 'rs16a')
                    rstd_act(rs16[:, 1:2], rs16[:, 0:1], 128, ['rs16a'], 'rs16b', post=1.0 - LAM_INIT)
                    S.op('dve', lambda e: e.scalar_tensor_tensor(out=a16[:, :], in0=a16[:, :], scalar=rs16[:, 1:2], in1=anb4[:, :],
                                                                 op0=ALU.mult, op1=ALU.mult), r=['a16', 'rs16b', 'anb4'], w=['a16'])
                    for h in range(4):
                        S.op('pe', lambda e: e.matmul(psum[:, 7, h * 16:(h + 1) * 16], lhsT=a16[:, h * 128:(h + 1) * 128], rhs=identf[0:16, 0:16],
                                                      start=True, stop=True, skip_group_check=True),
                             r=['a16', 'identf'], w=[('ps', 7)], inc=(h == 3))
                    S.op('act', lambda e: e.copy(out=hnT[:, 0:4, TP + 4 * b:TP + 4 * b + 4],
                                                 in_=psum[:, 7, 0:80].rearrange("p (h x) -> p h x", x=20)[:, :, 0:4]),
                         r=[('ps', 7)], w=[('hTs', b)])

        def out_proj(em, hnT, mixed, wt):
            xr = [em.enter_context(nc.sbuf_tensor(f"xro_{i}", [128, D], F32)) for i in range(2)]
            tmp1 = em.enter_context(nc.sbuf_tensor("tmpo", [128, D], F32))
            wo_d = w_out.rearrange("(kc p) f -> p kc f", p=128)
            for half in range(2):
                S.dma('pool', lambda e: e.dma_start(out=wt[half][:], in_=wo_d[:, :, half * 512:(half + 1) * 512]),
                      w=[('wo', half)] + [('wt', half, j) for j in range(4)])
            for tt in range(NT):
                rows = rows_of(tt)
                r0 = tt * 128
                bank = tt % 2
                pv = psb16(bank)
                c_lo = 4 if (tt == 16 and stage > 4) else 0
                for c in range(c_lo, KD):
                    S.op('pe', lambda e, c=c: e.transpose(out=pv[:, c * 128:c * 128 + rows], in_=mixed[:rows, tt, c * 128:(c + 1) * 128],
                                                          identity=identb[:rows, :rows]),
                         r=[('mixed', tt, c), 'identb'], w=[('ps', bank)], inc=(c == KD - 1))
                srcv = pv[:, 0:1024].rearrange("p (c r) -> p c r", r=128)[:, c_lo:KD, 0:rows]
                S.op('act', lambda e: e.copy(out=hnT[:, c_lo:KD, r0:r0 + rows], in_=srcv),
                     r=[('ps', bank)] + ([('hTs', b) for b in range(16)] if c_lo else []), w=[('hT', tt)])
            for tt in range(NT):
                rows = rows_of(tt)
                r0 = tt * 128
                yb = tt % 2
                S.dma('sp', lambda e: e.dma_start(out=xr[yb][:rows, :], in_=x1_s[r0:r0 + rows, :]), r=[('dram', id(x1_s), tt)], w=[('xro', yb)])
                for half in range(2):
                    bank = 4 + 2 * yb + half
                    for kc in range(KD):
                        S.op('pe', lambda e, kc=kc: e.matmul(psum[:rows, bank, :], lhsT=hnT[:, kc, r0:r0 + rows], rhs=wt[half][:, kc, :],
                                                             start=(kc == 0), stop=(kc == KD - 1)),
                             r=[('hT', tt), ('wo', half)], w=[('ps', bank)], inc=(kc == KD - 1))
                b0 = 4 + 2 * yb
                S.op('act', lambda e: e.copy(out=tmp1[:rows, :].rearrange("p (a b) -> p a b", a=2), in_=psum[:rows, b0:b0 + 2, :]),
                     r=[('ps', b0), ('ps', b0 + 1)], w=['tmpo'])
                sumsq(ss[:rows, tt:tt + 1], tmp1[:rows, :], rows, D, ['tmpo'], ('ssc', tt))
                rstd_act(rstd[:rows, tt:tt + 1], ss[:rows, tt:tt + 1], D, [('ssc', tt)], ('rstd', tt))
                S.op('dve', lambda e: e.scalar_tensor_tensor(out=tmp1[:rows, :], in0=tmp1[:rows, :], scalar=rstd[:rows, tt:tt + 1],
                                                             in1=gbc[:rows, 1, :], op0=ALU.mult, op1=ALU.mult),
                     r=['tmpo', ('rstd', tt), ('gbc', 1)], w=['tmpo'])
                S.op('dve', lambda e: e.tensor_tensor(out=xr[yb][:rows, :], in0=tmp1[:rows, :], in1=xr[yb][:rows, :], op=ALU.add),
                     r=['tmpo', ('xro', yb)], w=[('xro', yb)])
                S.dma('sp', lambda e: e.dma_start(out=x2_s[r0:r0 + rows, :], in_=xr[yb][:rows, :]),
                      r=[('xro', yb)], w=[('dram', id(x2_s), tt)])

        if stage > 1:
            token_mix()
        if stage > 3:
            with ExitStack() as es2:
                ffn(x2_s, y_out, 1, 4, 5, True)
            S.freed()
        S.finish()
    return nc


_CONST = {}


def consts():
    if not _CONST:
        import ml_dtypes
        _CONST['identb'] = np.eye(128, dtype=np.float32).astype(ml_dtypes.bfloat16)
        _CONST['identf'] = np.eye(128, dtype=np.float32)
        oh = np.zeros((33, 384), np.float32)
        for r in range(384):
            rel = r - 127
            if rel < 0:
                oh[32, r] = 1.0
            else:
                if rel < 16:
                    b = rel
                else:
                    b = 16 + int(np.float32(np.log(np.float32(max(rel, 16)) / np.float32(16))) / np.float32(np.log(8.0)) * np.float32(16))
                    b = min(b, 31)
                oh[b, r] = 1.0
        _CONST['onehot'] = oh
        sel = np.zeros((4, 4, 128), np.float32)
        for h in range(4):
            sel[h, h, :] = 1.0
        _CONST['sel'] = sel
        _CONST['trim'] = np.triu(np.ones((128, 128), np.float32))
        bm = np.zeros((64, 64), np.float32)
        rs = np.zeros((64, 16), np.float32)
        for p in range(64):
            rs[p, p // 4] = 1.0
            for t in range(64):
                if p // 4 == t // 4 and p <= t:
                    bm[p, t] = 1.0
        _CONST['bmask'] = bm
        _CONST['rowsel'] = rs
        rm = np.zeros((4, 64), np.float32)
        for p in range(64):
            rm[p % 4, p] = 1.0
        _CONST['rmod'] = rm
        s01 = np.zeros((32, 32), np.float32)
        b16 = np.zeros((16, 512), np.float32)
        for h in range(4):
            for q in range(4):
                s01[h * 8 + q, h * 4 + q] = 1.0
                s01[h * 8 + 4 + q, 16 + h * 4 + q] = 1.0
                b16[h * 4 + q, h * 128:(h + 1) * 128] = 1.0
        _CONST['sel01'] = s01
        _CONST['bm16'] = b16
        _CONST['tri'] = np.ascontiguousarray(np.eye(128, dtype=np.float32)[::-1])
    return _CONST


def make_in_maps(inputs, cores):
    c = consts()
    maps = []
    if 'cache_k' in inputs and '_cache_kv' not in inputs:
        inputs = dict(inputs)
        inputs['_cache_kv'] = np.concatenate([inputs['cache_k'].reshape(2560 * 128, 512),
                                              inputs['cache_v'].reshape(2560 * 128, 512)], axis=1)
    xp = inputs['x_prompt']
    xs = inputs['x_sample'].reshape(128 * 4, D)
    for k in cores:
        m = {
            'xin': np.ascontiguousarray(np.concatenate([xp[k], xs[k * 64:(k + 1) * 64]], axis=0)),
            'gains': np.ascontiguousarray(inputs['norm_gains'][0]),
            'wgate': np.ascontiguousarray(inputs['ffn_w_gate'][0]),
            'wup': np.ascontiguousarray(inputs['ffn_w_up'][0]),
            'wdown': np.ascontiguousarray(inputs['ffn_w_down'][0]),
            'identb': c['identb'],
            'w_in': np.ascontiguousarray(inputs['w_in'][0]),
            'rel_bias': np.ascontiguousarray(inputs['rel_bias']),
            'b_gates': np.ascontiguousarray(inputs['b_gates'][0]),
            'lam_q': np.ascontiguousarray(inputs['lam_q'][0].reshape(1, 128)),
            'lam_k': np.ascontiguousarray(inputs['lam_k'][0].reshape(1, 128)),
            'attn_norm': np.ascontiguousarray(inputs['attn_norm']),
            'mlstm_norm': np.ascontiguousarray(inputs['mlstm_norm']),
            'w_out': np.ascontiguousarray(inputs['w_out'][0]),
            'onehot': c['onehot'], 'identf': c['identf'], 'tri': c['tri'], 'sel': c['sel'], 'trim': c['trim'],
            'sm': np.ascontiguousarray(inputs['state_m'][0, k * 16:(k + 1) * 16]),
            'sC': np.ascontiguousarray(inputs['state_C'][0, k * 16:(k + 1) * 16].reshape(64, 128, 128)),
            'sn': np.ascontiguousarray(inputs['state_n'][0, k * 16:(k + 1) * 16].reshape(64, 128)),
            'bmask': c['bmask'], 'rowsel': c['rowsel'],
            'rmod': c['rmod'], 'sel01': c['sel01'], 'bm16': c['bm16'],
            **({'cache_kv': inputs['_cache_kv']} if '_cache_kv' in inputs else {}),
            'pt': np.ascontiguousarray(inputs['page_table'][k * 16:(k + 1) * 16].reshape(1, 256)).astype(np.int32),
        }
        maps.append(m)
    return maps


def kernel(**inputs):
    inputs = {k: np.asarray(v) for k, v in inputs.items()}
    nc = build_nc()
    maps = make_in_maps(inputs, list(range(NCORES)))
    res = run_bass_kernel_spmd(nc, maps, core_ids=list(range(NCORES)))
    R = res.results
    f32 = np.float32
    y = np.stack([r['y_out'] for r in R])
    y_prompt = np.ascontiguousarray(y[:, :TP, :]).astype(f32)
    y_sample = np.ascontiguousarray(y[:, TP:, :]).reshape(128, 4, D).astype(f32)
    ko = np.stack([r['k_out'] for r in R]); vo = np.stack([r['v_out'] for r in R])
    k_prompt = ko[:, :TP].reshape(1, 8, TP, 4, 128).astype(f32)
    v_prompt = vo[:, :TP].reshape(1, 8, TP, 4, 128).astype(f32)
    k_sample = ko[:, TP:].reshape(1, 128, 4, 4, 128).astype(f32)
    v_sample = vo[:, TP:].reshape(1, 128, 4, 4, 128).astype(f32)

    def get(name, shape):
        if name in R[0]:
            return np.stack([r[name] for r in R]).reshape(shape).astype(f32)
        return np.zeros(shape, f32)
    C_prompt = get('cp_out', (1, 8, 4, 128, 128))
    n_prompt = get('np_out', (1, 8, 4, 128))
    m_prompt = get('mp_out', (1, 8, 4))
    C_sample = get('cs_out', (1, 128, 4, 128, 128))
    n_sample = get('ns_out', (1, 128, 4, 128))
    m_sample = get('ms_out', (1, 128, 4))
    return (y_prompt, y_sample, k_prompt, v_prompt, C_prompt, n_prompt, m_prompt,
            k_sample, v_sample, C_sample, n_sample, m_sample)
```

```python
import numpy as np
from contextlib import ExitStack
import concourse.bass as bass
import concourse.mybir as mybir
from concourse.bass_utils import run_bass_kernel_spmd

F32 = mybir.dt.float32
BF16 = mybir.dt.bfloat16
I32 = mybir.dt.int32
ALU = mybir.AluOpType
AF = mybir.ActivationFunctionType
AX = mybir.AxisListType

NCORES = 8
TP, TS = 2048, 64
T = TP + TS
NT = 17
D, FF, KD, KF = 1024, 2816, 8, 22
DIN = 3592
EPS = 1e-6
NDS = 40
LAM_INIT = 0.2


def rows_of(tt):
    return 128 if tt < 16 else 64


def blk_cols(blk):
    return (blk * 512, 512) if blk < 4 else (2048, 64)


class Sched:
    def __init__(self, nc, es):
        self.nc = nc
        self.E = {'pe': nc.tensor, 'act': nc.scalar, 'dve': nc.vector, 'pool': nc.gpsimd, 'sp': nc.sync}
        self.sem = {k: es.enter_context(nc.semaphore(f"sem_{k}")) for k in self.E}
        self.cnt = {k: 0 for k in self.E}
        self.pending = {k: False for k in self.E}
        self.dsem = [es.enter_context(nc.semaphore(f"dsem{i}")) for i in range(NDS)]
        self.dcnt = [0] * NDS
        self.dnext = 0
        self.waited = {}
        self.lastw = {}
        self.readers = {}
        self.out_dmas = []
        self.alias_deps = {}
        for h in list(self.sem.values()) + self.dsem:
            nc.gpsimd.sem_clear(h)
        nc.all_engine_barrier()

    def _semof(self, prod):
        return self.sem[prod] if isinstance(prod, str) else self.dsem[prod[1]]

    def _wait(self, e, prod, val):
        key = (e, prod)
        if self.waited.get(key, 0) >= val:
            return
        self.waited[key] = val
        self.E[e].wait_ge(self._semof(prod), val)

    def _deps(self, r, w):
        deps = {}

        def add(p, v):
            if deps.get(p, 0) < v:
                deps[p] = v
        for b in r:
            lw = self.lastw.get(b)
            if lw is not None:
                add(*lw)
            elif b not in self.readers:
                for p, v in self.alias_deps.items():
                    add(p, v)
        for b in w:
            lw = self.lastw.get(b)
            if lw is not None:
                add(*lw)
            else:
                for p, v in self.alias_deps.items():
                    add(p, v)
            for p, v in self.readers.get(b, {}).items():
                add(p, v)
        return deps

    def freed(self):
        d = {}
        for e in self.E:
            v = self.cnt[e] + (1 if self.pending[e] else 0)
            if v > 0:
                d[e] = v
        for j in range(NDS):
            if self.dcnt[j] > 0:
                d[('d', j)] = self.dcnt[j]
        self.alias_deps = d
        self.lastw = {}
        self.readers = {}

    def _record(self, me, r, w):
        for b in w:
            self.lastw[b] = me
            self.readers[b] = {}
        for b in r:
            d = self.readers.setdefault(b, {})
            if d.get(me[0], 0) < me[1]:
                d[me[0]] = me[1]

    def op(self, e, fn, r=(), w=(), inc=True):
        deps = self._deps(r, w)
        for p, v in deps.items():
            if p == e and e == 'pe':
                continue
            self._wait(e, p, v)
        ins = fn(self.E[e])
        if inc:
            self.cnt[e] += 1
            ins.then_inc(self.sem[e], 1)
            me = (e, self.cnt[e])
            self.pending[e] = False
        else:
            me = (e, self.cnt[e] + 1)
            self.pending[e] = True
        self._record(me, r, w)

    def dma(self, q, fn, r=(), w=(), is_out=False):
        deps = self._deps(r, w)
        j = self.dnext
        self.dnext = (self.dnext + 1) % NDS
        if self.dcnt[j] > 0:
            self._wait(q, ('d', j), self.dcnt[j])
        for p, v in deps.items():
            self._wait(q, p, v)
        ins = fn(self.E[q])
        self.dcnt[j] += 16
        ins.then_inc(self.dsem[j], 16)
        me = (('d', j), self.dcnt[j])
        self._record(me, r, w)
        if is_out:
            self.out_dmas.append(me)

    def finish(self):
        for j in range(NDS):
            if self.dcnt[j] > 0:
                self._wait('sp', ('d', j), self.dcnt[j])
        for e in ('pe', 'act', 'dve', 'pool'):
            assert not self.pending[e], e
            if self.cnt[e] > 0:
                self._wait('sp', e, self.cnt[e])
        self.nc.all_engine_barrier()
        for h in list(self.sem.values()) + self.dsem:
            self.nc.gpsimd.sem_clear(h)
        self.nc.all_engine_barrier()


DEBUG = set()
SKIP = set()


def build_nc(stage=99):
    nc = bass.Bass("TRN2", target_bir_lowering=False)

    def din(name, shape, dt=F32):
        return nc.dram_tensor(name, list(shape), dt, kind="ExternalInput").ap()

    def dout(name, shape, dt=F32):
        return nc.dram_tensor(name, list(shape), dt, kind="ExternalOutput").ap()

    def dscr(name, shape, dt=F32):
        return nc.dram_tensor(name, list(shape), dt, kind="Internal").ap()

    xin = din("xin", [T, D])
    gains = din("gains", [6, D])
    wgate = din("wgate", [2, D, FF])
    wup = din("wup", [2, D, FF])
    wdown = din("wdown", [2, FF, D])
    identb_d = din("identb", [128, 128], BF16)

    w_in = din("w_in", [D, DIN])
    rel_bias = din("rel_bias", [32, 4])
    b_gates = din("b_gates", [2, 4])
    lam_q = din("lam_q", [1, 128])
    lam_k = din("lam_k", [1, 128])
    attn_norm = din("attn_norm", [1, 128])
    mlstm_norm = din("mlstm_norm", [1, 128])
    w_out = din("w_out", [D, D])
    onehot_d = din("onehot", [33, 384])
    identf_d = din("identf", [128, 128])
    tri_d = din("tri", [128, 128])
    sel_d = din("sel", [4, 4, 128])
    trim_d = din("trim", [128, 128])
    sm_in = din("sm", [16, 4])
    kv_in = din("cache_kv", [2560 * 128, 1024]) if stage > 4 else None
    pt_in = din("pt", [1, 256], I32)
    rmod_d = din("rmod", [4, 64])
    sel01_d = din("sel01", [32, 32])
    bm16_d = din("bm16", [16, 512])
    sC_in = din("sC", [64, 128, 128])
    sn_in = din("sn", [64, 128])
    bmask_d = din("bmask", [64, 64])
    rowsel_d = din("rowsel", [64, 16])
    cs_out = dout("cs_out", [64, 128, 128])
    ns_out = dout("ns_out", [64, 128])
    ms_out = dout("ms_out", [16, 4])
    cp_out = dout("cp_out", [4, 128, 128])
    np_out = dout("np_out", [4, 128])
    mp_out = dout("mp_out", [1, 4])
    vec_h = nc.dram_tensor("vec_s", [4, 384], F32, kind="Internal")
    vec_s = vec_h.ap()
    y_out = dout("y_out", [T, D])
    k_out = dout("k_out", [T, 512])
    v_out = dout("v_out", [T, 512])
    x1_s = dscr("x1_s", [T, D])
    x2_s = dout("x2_s", [T, D]) if "x2" in DEBUG else dscr("x2_s", [T, D])

    with ExitStack() as es:
        S = Sched(nc, es)

        def sb(name, shape, dt):
            return es.enter_context(nc.sbuf_tensor(name, list(shape), dt))

        psum = es.enter_context(nc.psum_tensor("psum", [128, 8, 512], F32))

        def dbg(name, ap, shape, dt, rbufs):
            if name not in DEBUG:
                return
            o = nc.dram_tensor("dbg_" + name, list(shape), dt, kind="ExternalOutput").ap()
            S.dma('sp', lambda e: e.dma_start(out=o, in_=ap), r=rbufs, w=[('dbg', name)])

        def psb(b, n=512):
            return psum[:, b, 0:n]

        def psb16(b):
            return psum[:, b, :].bitcast(BF16)

        identb = sb("identb_sb", [128, 128], BF16)
        S.dma('sp', lambda e: e.dma_start(out=identb[:], in_=identb_d[:, :]), w=['identb'])
        gbc = sb("gbc", [128, 2, D], F32)

        def load_gains(gis):
            for j, gi in enumerate(gis):
                S.dma('sp', lambda e: e.dma_start(out=gbc[:, j, :], in_=gains[gi:gi + 1, :].partition_broadcast(128)),
                      w=[('gbc', j)])
        ss = sb("ss", [128, 64], F32)
        rstd = sb("rstd", [128, 64], F32)
        junk = sb("junk", [128, D], F32)

        def sumsq(out, in_, rows, width, rbufs, wbuf):
            S.op('dve', lambda e: e.tensor_tensor(out=junk[:rows, 0:width], in0=in_, in1=in_, op=ALU.mult),
                 r=rbufs, w=['junk'])
            S.op('dve', lambda e: e.tensor_reduce(out=out, in_=junk[:rows, 0:width], axis=AX.X, op=ALU.add),
                 r=['junk'], w=[wbuf])

        def rstd_act(out, in_, n, rbufs, wbuf, post=1.0):
            S.op('act', lambda e: e.activation(out=out, in_=in_, func=AF.Ln, bias=EPS, scale=1.0 / n),
                 r=rbufs, w=[wbuf])
            S.op('act', lambda e: e.activation(out=out, in_=out, func=AF.Exp, scale=-0.5),
                 r=[wbuf], w=[wbuf])
            if post != 1.0:
                S.op('act', lambda e: e.mul(out=out, in_=out, mul=post), r=[wbuf], w=[wbuf])

        def norm_transpose(src, gi, hT, tag, xt, hn):
            for tt in range(NT):
                rows = rows_of(tt)
                sl = tt % 2
                r0 = tt * 128
                S.dma('sp', lambda e: e.dma_start(out=xt[sl][:rows, :], in_=src[r0:r0 + rows, :]), r=[('dram', id(src), tt)], w=[('xt', sl)])
                sumsq(ss[:rows, tt:tt + 1], xt[sl][:rows, :], rows, D, [('xt', sl)], ('ssc', tt))
                rstd_act(rstd[:rows, tt:tt + 1], ss[:rows, tt:tt + 1], D, [('ssc', tt)], ('rstd', tt))
                S.op('dve', lambda e: e.scalar_tensor_tensor(out=hn[sl][:rows, :], in0=xt[sl][:rows, :],
                                                             scalar=rstd[:rows, tt:tt + 1], in1=gbc[:rows, gi, :],
                                                             op0=ALU.mult, op1=ALU.mult),
                     r=[('xt', sl), ('rstd', tt), ('gbc', gi)], w=[('hn', sl)])
                bank = sl
                pv = psb16(bank)
                for c in range(KD):
                    S.op('pe', lambda e, c=c: e.transpose(out=pv[:, c * 128:c * 128 + rows],
                                                          in_=hn[sl][:rows, c * 128:(c + 1) * 128],
                                                          identity=identb[:rows, :rows]),
                         r=[('hn', sl), 'identb'], w=[('ps', bank)], inc=(c == KD - 1))
                srcv = pv[:, 0:1024].rearrange("p (c r) -> p c r", r=128)[:, :, 0:rows]
                S.op('act', lambda e: e.copy(out=hT[:, :, r0:r0 + rows], in_=srcv),
                     r=[('ps', bank)], w=[('hT', tt)])

        def ffn(src, dst, li, g_pre, g_post, dst_is_out):
            load_gains([g_pre, g_post])
            g_pre, g_post = 0, 1
            uT = es2.enter_context(nc.sbuf_tensor(f"uT{li}", [128, KF, T], BF16))
            with ExitStack() as esh:
                hT = esh.enter_context(nc.sbuf_tensor(f"hT{li}", [128, KD, T], BF16))
                wd = esh.enter_context(nc.sbuf_tensor(f"wd{li}", [128, KF, D], BF16))
                with ExitStack() as est:
                    xt = [est.enter_context(nc.sbuf_tensor(f"xt{li}_{i}", [128, D], F32)) for i in range(2)]
                    hn = [est.enter_context(nc.sbuf_tensor(f"hn{li}_{i}", [128, D], BF16)) for i in range(2)]
                    norm_transpose(src, g_pre, hT, ('ffn', li), xt, hn)
                S.freed()
                dbg('ss', ss[:, :], [128, 64], F32, [('ssc', tt) for tt in range(NT)])
                dbg('rstd', rstd[:, :], [128, 64], F32, [('rstd', tt) for tt in range(NT)])
                dbg('hT', hT[:, :, :], [128, KD, T], BF16, [('hT', tt) for tt in range(NT)])
                wgs = [esh.enter_context(nc.sbuf_tensor(f"wg{li}_{i}", [128, KD, 128], BF16)) for i in range(2)]
                wus = [esh.enter_context(nc.sbuf_tensor(f"wu{li}_{i}", [128, KD, 128], BF16)) for i in range(2)]
                sg = [esh.enter_context(nc.sbuf_tensor(f"sg{li}_{i}", [128, 512], F32)) for i in range(2)]
                wg_d = wgate[li].rearrange("(kc p) f -> p kc f", p=128)
                wu_d = wup[li].rearrange("(kc p) f -> p kc f", p=128)
                wd_d = wdown[li].rearrange("(fc p) d -> p fc d", p=128)
                pbi = 0
                for g in range(KF):
                    sl = g % 2
                    S.dma('pool', lambda e: e.dma_start(out=wgs[sl][:], in_=wg_d[:, :, g * 128:(g + 1) * 128]),
                          w=[('wg', sl)])
                    S.dma('pool', lambda e: e.dma_start(out=wus[sl][:], in_=wu_d[:, :, g * 128:(g + 1) * 128]),
                          w=[('wu', sl)])
                    S.dma('pool', lambda e: e.dma_start(out=wd[:, g:g + 1, :], in_=wd_d[:, g:g + 1, :]),
                          w=[('wd', g)])
                    for fl in range(1):
                        fc = g
                        for blk in range(5):
                            c0, n = blk_cols(blk)
                            pb = pbi % 2
                            pbi += 1
                            hbufs = [('hT', tt) for tt in (range(4 * blk, 4 * blk + 4) if blk < 4 else [16])]
                            for kc in range(KD):
                                S.op('pe', lambda e, kc=kc: e.matmul(psb(2 * pb, n), lhsT=wgs[sl][:, kc, fl * 128:(fl + 1) * 128],
                                                                     rhs=hT[:, kc, c0:c0 + n], start=(kc == 0), stop=(kc == KD - 1)),
                                     r=[('wg', sl)] + hbufs, w=[('ps', 2 * pb)], inc=(kc == KD - 1))
                            for kc in range(KD):
                                S.op('pe', lambda e, kc=kc: e.matmul(psb(2 * pb + 1, n), lhsT=wus[sl][:, kc, fl * 128:(fl + 1) * 128],
                                                                     rhs=hT[:, kc, c0:c0 + n], start=(kc == 0), stop=(kc == KD - 1)),
                                     r=[('wu', sl)] + hbufs, w=[('ps', 2 * pb + 1)], inc=(kc == KD - 1))
                            S.op('act', lambda e: e.activation(out=sg[pb][:, 0:n], in_=psb(2 * pb, n), func=AF.Silu),
                                 r=[('ps', 2 * pb)], w=[('sg', pb)])
                            S.op('dve', lambda e: e.tensor_tensor(out=uT[:, fc, c0:c0 + n], in0=sg[pb][:, 0:n],
                                                                  in1=psb(2 * pb + 1, n), op=ALU.mult),
                                 r=[('sg', pb), ('ps', 2 * pb + 1)], w=[('uT', fc, blk)])
                dbg('uT', uT[:, :, :], [128, KF, T], BF16, [('uT', fc, blk) for fc in range(KF) for blk in range(5)])
                xr = [esh.enter_context(nc.sbuf_tensor(f"xr{li}_{i}", [128, D], F32)) for i in range(2)]
                tmp1 = esh.enter_context(nc.sbuf_tensor(f"tmp{li}", [128, D], F32))
                for tt in range(NT):
                    rows = rows_of(tt)
                    r0 = tt * 128
                    yb = tt % 2
                    blk = tt // 4 if tt < 16 else 4
                    S.dma('sp', lambda e: e.dma_start(out=xr[yb][:rows, :], in_=src[r0:r0 + rows, :]), r=[('dram', id(src), tt)], w=[('xr', yb)])
                    for half in range(2):
                        bank = 4 + 2 * yb + half
                        for fc in range(KF):
                            S.op('pe', lambda e, fc=fc: e.matmul(psum[:rows, bank, :], lhsT=uT[:, fc, r0:r0 + rows],
                                                                 rhs=wd[:, fc, half * 512:(half + 1) * 512],
                                                                 start=(fc == 0), stop=(fc == KF - 1)),
                                 r=[('uT', fc, blk), ('wd', fc)], w=[('ps', bank)], inc=(fc == KF - 1))
                    b0 = 4 + 2 * yb
                    S.op('act', lambda e: e.copy(out=tmp1[:rows, :].rearrange("p (a b) -> p a b", a=2),
                                                 in_=psum[:rows, b0:b0 + 2, :]),
                         r=[('ps', b0), ('ps', b0 + 1)], w=['tmp1'])
                    sumsq(ss[:rows, tt:tt + 1], tmp1[:rows, :], rows, D, ['tmp1'], ('ssc', tt))
                    rstd_act(rstd[:rows, tt:tt + 1], ss[:rows, tt:tt + 1], D, [('ssc', tt)], ('rstd', tt))
                    S.op('dve', lambda e: e.scalar_tensor_tensor(out=tmp1[:rows, :], in0=tmp1[:rows, :],
                                                                 scalar=rstd[:rows, tt:tt + 1], in1=gbc[:rows, g_post, :],
                                                                 op0=ALU.mult, op1=ALU.mult),
                         r=['tmp1', ('rstd', tt), ('gbc', g_post)], w=['tmp1'])
                    S.op('dve', lambda e: e.scalar_tensor_tensor(out=xr[yb][:rows, :], in0=tmp1[:rows, :], scalar=0.5,
                                                                 in1=xr[yb][:rows, :], op0=ALU.mult, op1=ALU.add),
                         r=['tmp1', ('xr', yb)], w=[('xr', yb)])
                    S.dma('sp', lambda e: e.dma_start(out=dst[r0:r0 + rows, :], in_=xr[yb][:rows, :]),
                          r=[('xr', yb)], w=[('dram', id(dst), tt)], is_out=dst_is_out)
            S.freed()

        with ExitStack() as es2:
            ffn(xin, x1_s if stage > 1 else y_out, 0, 0, 1, stage <= 1)
        S.freed()

        def token_mix():
            with ExitStack() as em:
                hnT = em.enter_context(nc.sbuf_tensor("hnT", [128, KD, T], BF16))
                load_gains([2, 3])
                with ExitStack() as est:
                    xt = [est.enter_context(nc.sbuf_tensor(f"xtm_{i}", [128, D], F32)) for i in range(2)]
                    hn = [est.enter_context(nc.sbuf_tensor(f"hnm_{i}", [128, D], BF16)) for i in range(2)]
                    norm_transpose(x1_s, 0, hnT, 'mix', xt, hn)
                S.freed()
                win_d = w_in.rearrange("(kc p) f -> p kc f", p=128)
                wt = [em.enter_context(nc.sbuf_tensor(f"wt_{i}", [128, KD, 512], BF16)) for i in range(2)]
                stg = [em.enter_context(nc.sbuf_tensor(f"stg_{i}", [128, 512], F32)) for i in range(2)]
                hbuf_all = [('hT', tt) for tt in range(NT)]
                for gi_, (col0, dst) in enumerate([(512, k_out), (1024, v_out)]):
                    sl = gi_ % 2
                    S.dma('pool', lambda e: e.dma_start(out=wt[sl][:], in_=win_d[:, :, col0:col0 + 512]),
                          w=[('wt', sl, j) for j in range(4)])
                    for tt in range(NT):
                        rows = rows_of(tt)
                        r0 = tt * 128
                        bank = tt % 2
                        for kc in range(KD):
                            S.op('pe', lambda e, kc=kc: e.matmul(psum[:rows, bank, :], lhsT=hnT[:, kc, r0:r0 + rows],
                                                                 rhs=wt[sl][:, kc, :], start=(kc == 0), stop=(kc == KD - 1)),
                                 r=[('wt', sl, j) for j in range(4)] + [('hT', tt)], w=[('ps', bank)], inc=(kc == KD - 1))
                        S.op('act', lambda e: e.copy(out=stg[bank][:rows, :], in_=psum[:rows, bank, :]),
                             r=[('ps', bank)], w=[('stg', bank)])
                        S.dma('sp', lambda e: e.dma_start(out=dst[r0:r0 + rows, :], in_=stg[bank][:rows, :]),
                              r=[('stg', bank)], w=[('dram', id(dst), tt)], is_out=True)
                if stage > 2:
                    mix_body(em, hnT, win_d, wt, stg)
            S.freed()

        def mix_body(em, hnT, win_d, wt, stg):
            def sbm(name, shape, dt):
                return em.enter_context(nc.sbuf_tensor(name, list(shape), dt))
            mixed = sbm("mixed", [128, NT, D], BF16)
            lsm = sbm("lsm", [128, 8], F32)
            c31 = sbm("c31", [128, 8], F32)
            tz = sbm("tz", [128, 4, 256], F32)
            anb = sbm("anb", [128, 128], F32)
            mnb = sbm("mnb", [128, 128], F32)
            qblk = sbm("qblk", [128, 16, 4, 8], BF16)
            ks_s = sbm("ks_s", [128, 4, 64], BF16)
            vs_s = sbm("vs_s", [64, 4, 128], BF16)
            S.op('dve', lambda e: e.memset(qblk[:], 0.0), w=['qblk'])
            etz = ExitStack()

            def sbt(name, shape, dt):
                return etz.enter_context(nc.sbuf_tensor(name, list(shape), dt))
            lq = sbt("lq", [128, 128], F32)
            lk = sbt("lk", [128, 128], F32)
            S.dma('sp', lambda e: e.dma_start(out=lq[:], in_=lam_q[0:1, :].partition_broadcast(128)), w=['lq'])
            S.dma('sp', lambda e: e.dma_start(out=lk[:], in_=lam_k[0:1, :].partition_broadcast(128)), w=['lk'])
            S.op('dve', lambda e: e.tensor_tensor(out=lq[:], in0=lq[:], in1=lk[:], op=ALU.mult), r=['lq', 'lk'], w=['lq'])
            S.op('dve', lambda e: e.tensor_reduce(out=lsm[:, 0:2], in_=lq[:].rearrange("p (a b) -> p a b", a=2),
                                                  axis=AX.X, op=ALU.add), r=['lq'], w=['lsm'])
            S.op('act', lambda e: e.activation(out=lsm[:, 2:4], in_=lsm[:, 0:2], func=AF.Exp), r=['lsm'], w=['lsm'])
            S.op('dve', lambda e: e.tensor_tensor(out=lsm[:, 4:5], in0=lsm[:, 2:3], in1=lsm[:, 3:4], op=ALU.subtract),
                 r=['lsm'], w=['lsm'])
            S.op('dve', lambda e: e.tensor_scalar(out=lsm[:, 5:6], in0=lsm[:, 4:5], scalar1=LAM_INIT, scalar2=-1.0,
                                                  op0=ALU.add, op1=ALU.mult), r=['lsm'], w=['lsm'])
            nlam = lsm[:, 5:6]
            rbx = sbt("rbx", [33, 4], F32)
            oneh = sbt("oneh", [33, 384], F32)
            vecs = sbt("vecs", [4, 384], F32)
            S.op('dve', lambda e: e.memset(rbx[32:33, :], -30000.0), w=['rbx32'])
            S.dma('sp', lambda e: e.dma_start(out=rbx[0:32, :], in_=rel_bias[:, :]), w=['rbx'])
            S.dma('sp', lambda e: e.dma_start(out=oneh[:], in_=onehot_d[:, :]), w=['oneh'])
            S.dma('sp', lambda e: e.dma_start(out=c31[:, 0:4], in_=rel_bias[31:32, :].partition_broadcast(128)), w=['c31'])
            S.dma('sp', lambda e: e.dma_start(out=anb[:], in_=attn_norm[0:1, :].partition_broadcast(128)), w=['anb'])
            S.dma('sp', lambda e: e.dma_start(out=mnb[:], in_=mlstm_norm[0:1, :].partition_broadcast(128)), w=['mnb'])
            S.op('dve', lambda e: e.tensor_scalar(out=c31[:, 4:8], in0=c31[:, 0:4], scalar1=-1.0, scalar2=None, op0=ALU.mult),
                 r=['c31'], w=['c31n'])
            S.op('pe', lambda e: e.matmul(psum[0:4, 7, 0:384], lhsT=rbx[:, :], rhs=oneh[:, :], start=True, stop=True),
                 r=['rbx', 'rbx32', 'oneh'], w=[('ps', 7)])
            S.op('act', lambda e: e.copy(out=vecs[:], in_=psum[0:4, 7, 0:384]), r=[('ps', 7)], w=['vecs'])
            S.dma('sp', lambda e: e.dma_start(out=vec_s[:, :], in_=vecs[:]), r=['vecs'], w=['vec_s'])
            tzr = etz.enter_context(nc.sbuf_tensor("tzr", [128, 4, 256], F32))
            antiI = etz.enter_context(nc.sbuf_tensor("antiI", [128, 128], F32))
            S.dma('sp', lambda e: e.dma_start(out=antiI[:], in_=tri_d[:, :]), w=['antiI'])
            tz_src = bass.AP(vec_h, 0, [[1, 128], [384, 4], [1, 256]])
            S.dma('sp', lambda e: e.dma_start(out=tzr[:], in_=tz_src), r=['vec_s'], w=['tzr'])
            for half in range(2):
                S.op('pe', lambda e: e.matmul(psum[:, 7, :], lhsT=antiI[:, :],
                                              rhs=tzr[:, 2 * half:2 * half + 2, :].rearrange("p a b -> p (a b)"),
                                              start=True, stop=True), r=['antiI', 'tzr'], w=[('ps', 7)])
                for hh in range(2):
                    h = 2 * half + hh
                    S.op('act', lambda e: e.activation(out=tz[:, h, :], in_=psum[:, 7, hh * 256:(hh + 1) * 256], func=AF.Exp,
                                                       bias=c31[:, 4 + h:5 + h]),
                         r=[('ps', 7), 'c31n'], w=['tz'])
            etz.close()
            S.freed()
            dbg('tz', tz[:, :, :], [128, 4, 256], F32, ['tz'])
            eh = ExitStack()

            def sbm(name, shape, dt):
                return eh.enter_context(nc.sbuf_tensor(name, list(shape), dt))
            dbg('lsm', lsm[:, :], [128, 8], F32, ['lsm'])

            qk = [[sbm(f"qk_{i}_{j}", [128, T], BF16) for j in range(2)] for i in range(2)]
            vt = [sbm(f"vt_{i}", [128, NT, 129], BF16) for i in range(2)]
            pT = [[sbm(f"pT_{i}_{c}", [128, 512], BF16) for c in range(2)] for i in range(2)]
            fin4 = sbm("fin4", [128, 4, 128], F32)
            fin = fin4[:, 0, :]
            osb8 = sbm("osb8", [128, 8, 129], F32)
            osbf = osb8[:, :, :].rearrange("p a b -> p (a b)")
            fsm = sbm("fsm", [128, 8], F32)
            fs4 = sbm("fs4", [128, 16], F32)
            for i in range(2):
                S.op('pool', lambda e: e.memset(vt[i][:], 1.0), w=[('vt', i)])

            PROT = [7, 6, 1, 0]
            prot = [0]

            def pbank():
                prot[0] += 1
                return PROT[prot[0] % 4], ('act' if prot[0] % 2 == 0 else 'dve')

            def evac(eng, out, in_, r, w):
                if eng == 'act':
                    S.op('act', lambda e: e.copy(out=out, in_=in_), r=r, w=w)
                else:
                    S.op('dve', lambda e: e.tensor_copy(out=out, in_=in_), r=r, w=w)

            def proj_fm(dst, dname, wtile, wname, c0):
                for blk in range(5):
                    cc, n = blk_cols(blk)
                    hb = [('hT', tt) for tt in (range(4 * blk, 4 * blk + 4) if blk < 4 else [16])]
                    pbk, eng = pbank()
                    for kc in range(KD):
                        S.op('pe', lambda e, kc=kc: e.matmul(psum[:, pbk, 0:n], lhsT=wtile[:, kc, c0:c0 + 128],
                                                             rhs=hnT[:, kc, cc:cc + n], start=(kc == 0), stop=(kc == KD - 1)),
                             r=[wname] + hb, w=[('ps', pbk)], inc=(kc == KD - 1))
                    evac(eng, dst[:, cc:cc + n], psum[:, pbk, 0:n], [('ps', pbk)], [(dname, blk)])

            def proj_tm(consume, wtile, wname, c0, ncols):
                for tt in range(NT):
                    rows = rows_of(tt)
                    r0 = tt * 128
                    pbk, eng = pbank()
                    for kc in range(KD):
                        S.op('pe', lambda e, kc=kc: e.matmul(psum[:rows, pbk, 0:ncols], lhsT=hnT[:, kc, r0:r0 + rows],
                                                             rhs=wtile[:, kc, c0:c0 + ncols], start=(kc == 0), stop=(kc == KD - 1)),
                             r=[wname, ('hT', tt)], w=[('ps', pbk)], inc=(kc == KD - 1))
                    consume(tt, rows, psum[:rows, pbk, 0:ncols], pbk, eng)

            OPS_BANKS = [0, 1, 6]

            def ops(slot):
                b = OPS_BANKS[slot // 3]
                o = (slot % 3) * 129
                return psum[:, b, o:o + 129], ('ps', b)

            def BC4(ap):
                return ap.rearrange("p (a o) -> p a o", o=1).broadcast_to([128, 4, 128])

            def block_norm(x4, dst, gain, mul3, post, wbufs, extra_r=()):
                j4 = junk[:, 0:512].rearrange("p (a b) -> p a b", a=4)
                S.op('dve', lambda e: e.tensor_tensor(out=j4, in0=x4[:, :, :], in1=x4[:, :, :], op=ALU.mult), r=['fin'], w=['junk'])
                S.op('dve', lambda e: e.tensor_reduce(out=fs4[:, 8:12], in_=j4, axis=AX.X, op=ALU.add), r=['junk'], w=['fs4b'])
                rstd_act(fs4[:, 12:16], fs4[:, 8:12], 128, ['fs4b'], 'fs4c', post=post)
                S.op('dve', lambda e: e.tensor_tensor(out=x4[:, :, :], in0=x4[:, :, :], in1=BC4(fs4[:, 12:16]), op=ALU.mult),
                     r=['fin', 'fs4c'], w=['fin'])
                g3 = gain[:, :].rearrange("p (o d) -> p o d", o=1).broadcast_to([128, 4, 128])
                gname = 'anb' if gain is anb else 'mnb'
                if mul3 is None:
                    S.op('dve', lambda e: e.tensor_tensor(out=dst, in0=x4[:, :, :], in1=g3, op=ALU.mult), r=['fin', gname], w=list(wbufs))
                else:
                    S.op('dve', lambda e: e.tensor_tensor(out=x4[:, :, :], in0=x4[:, :, :], in1=g3, op=ALU.mult), r=['fin', gname], w=['fin'])
                    S.op('dve', lambda e: e.tensor_tensor(out=dst, in0=x4[:, :, :], in1=mul3, op=ALU.mult),
                         r=['fin'] + list(extra_r), w=list(wbufs))

            def attn_head(h, it):
                sl = it % 2
                for j, col0 in enumerate([h * 128, 512 + h * 128, 1024 + h * 128]):
                    S.dma('pool', lambda e: e.dma_start(out=wt[sl][:, :, j * 128:(j + 1) * 128], in_=win_d[:, :, col0:col0 + 128]),
                          w=[('wt', sl, j)])
                qT, kT, V = qk[sl][0], qk[sl][1], vt[sl]
                qn, kn, vn = ('qT', sl), ('kT', sl), ('vt', sl)
                proj_fm(qT, qn, wt[sl], ('wt', sl, 0), 0)
                proj_fm(kT, kn, wt[sl], ('wt', sl, 1), 128)

                def cons_v(tt, rows, ps_ap, pbk, eng):
                    evac(eng, V[:rows, tt, 0:128], ps_ap, [('ps', pbk), vn], [(vn, tt)])
                proj_tm(cons_v, wt[sl], ('wt', sl, 2), 256, 128)
                S.op('dve', lambda e: e.tensor_copy(out=qblk[0:64, :, h, 0:4], in_=qT[0:64, TP:T].rearrange("p (b j) -> p b j", j=4)),
                     r=[(qn, 4), 'qblk'], w=[('qblk', h, 0)])
                S.op('dve', lambda e: e.tensor_copy(out=qblk[64:128, :, h, 4:8], in_=qT[64:128, TP:T].rearrange("p (b j) -> p b j", j=4)),
                     r=[(qn, 4), 'qblk'], w=[('qblk', h, 1)])
                S.op('dve', lambda e: e.tensor_copy(out=ks_s[:, h, :], in_=kT[:, TP:T]), r=[(kn, 4)], w=[('ks_s', h)])
                S.op('dve', lambda e: e.tensor_copy(out=vs_s[:, h, :], in_=V[0:64, 16, 0:128]), r=[(vn, 16)], w=[('vs_s', h)])
                qbufs = lambda q0, n: [(qn, b) for b in range(q0 // 512, (q0 + n - 1) // 512 + 1)]
                iters = [(qb, kt) for qb in range(4) for kt in range(4 * qb + 4)]

                def geom(i):
                    qb, kt = iters[i]
                    qt0 = max(kt, 4 * qb)
                    return qb, kt, qt0, (4 * qb + 4 - qt0) * 128, qt0 * 128, i % 2

                def emit_qk(i):
                    qb, kt, qt0, ncol, q0, pair = geom(i)
                    for c in range(2):
                        bank = 2 + 2 * pair + c
                        S.op('pe', lambda e: e.matmul(psum[:, bank, 0:ncol], lhsT=kT[64 * c:64 * c + 64, kt * 128:(kt + 1) * 128],
                                                      rhs=qT[64 * c:64 * c + 64, q0:q0 + ncol], start=True, stop=True),
                             r=[(kn, kt // 4)] + qbufs(q0, ncol), w=[('ps', bank)])
                    for c in range(2):
                        bank = 2 + 2 * pair + c
                        S.op('act', lambda e: e.activation(out=pT[pair][c][:, 0:ncol], in_=psum[:, bank, 0:ncol], func=AF.Exp,
                                                           bias=c31[:, h:h + 1], scale=0.125),
                             r=[('ps', bank), 'c31'], w=[('pT', pair, c)])
                        if qt0 == kt:
                            S.op('dve', lambda e: e.tensor_tensor(out=pT[pair][c][:, 0:128], in0=pT[pair][c][:, 0:128],
                                                                  in1=tz[:, h, 0:128], op=ALU.mult),
                                 r=[('pT', pair, c), 'tz'], w=[('pT', pair, c)])
                        if qt0 <= kt + 1 <= 4 * qb + 3:
                            off = (kt + 1 - qt0) * 128
                            S.op('dve', lambda e: e.tensor_tensor(out=pT[pair][c][:, off:off + 128], in0=pT[pair][c][:, off:off + 128],
                                                                  in1=tz[:, h, 128:256], op=ALU.mult),
                                 r=[('pT', pair, c), 'tz'], w=[('pT', pair, c)])

                def emit_av(i):
                    qb, kt, qt0, ncol, q0, pair = geom(i)
                    for c in range(2):
                        for qt in range(qt0, 4 * qb + 4):
                            jq = qt - 4 * qb
                            oap, obuf = ops(c * 4 + jq)
                            off = (qt - qt0) * 128
                            S.op('pe', lambda e: e.matmul(oap, lhsT=pT[pair][c][:, off:off + 128], rhs=V[:, kt, :],
                                                          start=(kt == 0 and (c * 4 + jq) % 3 == 0), stop=(kt == qt),
                                                          skip_group_check=True),
                                 r=[('pT', pair, c), (vn, kt)], w=[obuf])
                    if kt == 4 * qb + 3:
                        S.op('dve', lambda e: e.tensor_copy(out=osbf[:, 0:387], in_=psum[:, 0, 0:387]), r=[('ps', 0)], w=['osb8a'])
                        S.op('dve', lambda e: e.tensor_copy(out=osbf[:, 387:774], in_=psum[:, 1, 0:387]), r=[('ps', 1)], w=['osb8b'])
                        S.op('dve', lambda e: e.tensor_copy(out=osbf[:, 774:1032], in_=psum[:, 6, 0:258]), r=[('ps', 6)], w=['osb8c'])
                        ob = ['osb8a', 'osb8b', 'osb8c']
                        t0_ = 4 * qb
                        S.op('dve', lambda e: e.reciprocal(out=fs4[:, 0:8], in_=osb8[:, :, 128]), r=ob, w=['fs4'])
                        S.op('dve', lambda e: e.tensor_scalar(out=fs4[:, 4:8], in0=fs4[:, 4:8], scalar1=nlam, scalar2=None, op0=ALU.mult),
                             r=['fs4', 'lsm'], w=['fs4'])
                        S.op('dve', lambda e: e.tensor_tensor(out=fin4[:, :, :], in0=osb8[:, 0:4, 0:128], in1=BC4(fs4[:, 0:4]), op=ALU.mult),
                             r=ob + ['fs4'], w=['fin'])
                        j3 = junk[:, 512:1024].rearrange("p (a b) -> p a b", a=4)
                        S.op('dve', lambda e: e.tensor_tensor(out=j3, in0=osb8[:, 4:8, 0:128], in1=BC4(fs4[:, 4:8]), op=ALU.mult),
                             r=ob + ['fs4'], w=['junk'])
                        S.op('dve', lambda e: e.tensor_tensor(out=fin4[:, :, :], in0=fin4[:, :, :], in1=j3, op=ALU.add), r=['fin', 'junk'], w=['fin'])
                        block_norm(fin4, mixed[:, t0_:t0_ + 4, h * 128:(h + 1) * 128], anb, None, 1.0 - LAM_INIT,
                                   [('mixed', t0_ + j, h) for j in range(4)])

                for i in range(len(iters) + 1):
                    if i < len(iters):
                        emit_qk(i)
                    if i >= 1:
                        emit_av(i - 1)

            for h in range(4):
                attn_head(h, h)

            SCL = 128.0 ** -0.5
            atm = sbm("atm", [128, NT, 16], F32)
            nAb = sbm("nAb", [128, T], F32)
            selt = sbm("selt", [4, 4, 128], F32)
            trim = sbm("trim_sb", [128, 128], F32)
            identf = sbm("identf_sb", [128, 128], F32)
            mfin = sbm("mfin", [4, 64], F32)
            S.dma('sp', lambda e: e.dma_start(out=selt[:], in_=sel_d[:, :, :]), w=['selt'])
            S.dma('sp', lambda e: e.dma_start(out=trim[:], in_=trim_d[:, :]), w=['trim'])
            S.dma('sp', lambda e: e.dma_start(out=identf[:], in_=identf_d[:, :]), w=['identf'])
            G4 = sbm("G4", [4, T], F32)
            with ExitStack() as eg:
                G1 = eg.enter_context(nc.sbuf_tensor("G1", [4, T], F32))
                G2 = eg.enter_context(nc.sbuf_tensor("G2", [4, T], F32))
                G3 = eg.enter_context(nc.sbuf_tensor("G3", [4, T], F32))
                wgt = eg.enter_context(nc.sbuf_tensor("wgt", [128, KD, 8], BF16))
                bg = eg.enter_context(nc.sbuf_tensor("bg", [4, 4], F32))
                m0T = eg.enter_context(nc.sbuf_tensor("m0T", [4, 16], F32))
                S.dma('pool', lambda e: e.dma_start(out=wgt[:], in_=win_d[:, :, 3584:3592]), w=['wgt'])
                S.dma('sp', lambda e: e.dma_start(out=bg[:, 0:2], in_=b_gates.rearrange("g h -> h g"), allow_slow_non_contiguous=True), w=['bg'])
                S.dma('sp', lambda e: e.dma_start(out=m0T[:], in_=sm_in.rearrange("b h -> h b"), allow_slow_non_contiguous=True), w=['m0T'])
                S.op('dve', lambda e: e.tensor_scalar(out=bg[:, 2:3], in0=bg[:, 1:2], scalar1=-1.0, scalar2=None, op0=ALU.mult),
                     r=['bg'], w=['bgn'])
                for gidx, G in ((0, G1), (1, G2)):
                    for blk in range(5):
                        cc, n = blk_cols(blk)
                        hb = [('hT', tt) for tt in (range(4 * blk, 4 * blk + 4) if blk < 4 else [16])]
                        for kc in range(KD):
                            S.op('pe', lambda e, kc=kc: e.matmul(psum[0:4, 7, 0:n], lhsT=wgt[:, kc, 4 * gidx:4 * gidx + 4],
                                                                 rhs=hnT[:, kc, cc:cc + n], start=(kc == 0), stop=(kc == KD - 1)),
                                 r=['wgt'] + hb, w=[('ps', 7)], inc=(kc == KD - 1))
                        S.op('act', lambda e: e.copy(out=G[:, cc:cc + n], in_=psum[0:4, 7, 0:n]), r=[('ps', 7)], w=[('G', gidx)])
                S.op('act', lambda e: e.activation(out=G2[:, :], in_=G2[:, :], func=AF.Exp, bias=bg[:, 2:3], scale=-1.0),
                     r=[('G', 1), 'bgn'], w=[('G', 1)])
                S.op('act', lambda e: e.activation(out=G2[:, :], in_=G2[:, :], func=AF.Ln, bias=1.0), r=[('G', 1)], w=[('G', 1)])
                S.op('dve', lambda e: e.tensor_tensor_scan(out=G3[:, 0:TP], data0=G2[:, 0:TP], data1=G2[:, 0:TP], initial=0.0,
                                                           op0=ALU.add, op1=ALU.max), r=[('G', 1)], w=[('G', 2)])
                l3 = G2[:, TP:T].rearrange("p (b j) -> p b j", j=4)
                B3 = G3[:, TP:T].rearrange("p (b j) -> p b j", j=4)
                S.op('dve', lambda e: e.tensor_copy(out=B3[:, :, 0], in_=l3[:, :, 0]), r=[('G', 1)], w=[('G', 2)])
                for j in range(1, 4):
                    S.op('dve', lambda e: e.tensor_tensor(out=B3[:, :, j], in0=B3[:, :, j - 1], in1=l3[:, :, j], op=ALU.add),
                         r=[('G', 1), ('G', 2)], w=[('G', 2)])
                S.op('dve', lambda e: e.scalar_tensor_tensor(out=G1[:, :], in0=G1[:, :], scalar=bg[:, 0:1], in1=G3[:, :],
                                                             op0=ALU.add, op1=ALU.add), r=[('G', 0), ('G', 2), 'bg'], w=[('G', 0)])
                S.op('dve', lambda e: e.tensor_tensor_scan(out=G4[:, 0:TP], data0=G1[:, 0:TP], data1=G1[:, 0:TP], initial=0.0,
                                                           op0=ALU.max, op1=ALU.max), r=[('G', 0)], w=[('G', 3)])
                a3 = G1[:, TP:T].rearrange("p (b j) -> p b j", j=4)
                A3 = G4[:, TP:T].rearrange("p (b j) -> p b j", j=4)
                S.op('dve', lambda e: e.tensor_tensor(out=A3[:, :, 0], in0=a3[:, :, 0], in1=m0T[:, :], op=ALU.max),
                     r=[('G', 0), 'm0T'], w=[('G', 3)])
                for j in range(1, 4):
                    S.op('dve', lambda e: e.tensor_tensor(out=A3[:, :, j], in0=A3[:, :, j - 1], in1=a3[:, :, j], op=ALU.max),
                         r=[('G', 0), ('G', 3)], w=[('G', 3)])
                S.op('dve', lambda e: e.tensor_tensor(out=mfin[:, 0:1], in0=G4[:, TP - 1:TP], in1=G3[:, TP - 1:TP], op=ALU.subtract),
                     r=[('G', 2), ('G', 3)], w=['mfin0'])
                S.op('dve', lambda e: e.tensor_tensor(out=mfin[:, 16:32], in0=A3[:, :, 3], in1=B3[:, :, 3], op=ALU.subtract),
                     r=[('G', 2), ('G', 3)], w=['mfin1'])
                S.op('dve', lambda e: e.tensor_tensor(out=mfin[:, 32:48], in0=m0T[:, :], in1=A3[:, :, 3], op=ALU.subtract),
                     r=['m0T', ('G', 3)], w=['mfin2'])
                S.op('act', lambda e: e.activation(out=mfin[:, 32:48], in_=mfin[:, 32:48], func=AF.Exp), r=['mfin2'], w=['mfin2'])
                S.dma('sp', lambda e: e.dma_start(out=mp_out.rearrange("o h -> h o"), in_=mfin[:, 0:1], allow_slow_non_contiguous=True),
                      r=['mfin0'], w=['mp_out'], is_out=True)
                S.dma('sp', lambda e: e.dma_start(out=ms_out.rearrange("b h -> h b"), in_=mfin[:, 16:32], allow_slow_non_contiguous=True),
                      r=['mfin1'], w=['ms_out'], is_out=True)
                S.op('dve', lambda e: e.tensor_tensor(out=G3[:, :], in0=G3[:, :], in1=G4[:, :], op=ALU.subtract),
                     r=[('G', 2), ('G', 3), 'mfin0', 'mfin1'], w=[('G', 2)])
                S.op('act', lambda e: e.activation(out=G3[:, :], in_=G3[:, :], func=AF.Exp), r=[('G', 2)], w=[('G', 2)])
                w3 = G2[:, TP:T].rearrange("p (b j) -> p b j", j=4)
                for j in range(4):
                    S.op('dve', lambda e: e.tensor_tensor(out=w3[:, :, j], in0=m0T[:, :], in1=A3[:, :, j], op=ALU.subtract),
                         r=['m0T', ('G', 3), ('G', 1)], w=[('G', 1)])
                S.op('act', lambda e: e.activation(out=G2[:, TP:T], in_=G2[:, TP:T], func=AF.Exp), r=[('G', 1)], w=[('G', 1)])
                x3 = G2[:, 0:TS].rearrange("p (b j) -> p b j", j=4)
                for j in range(4):
                    S.op('dve', lambda e: e.tensor_tensor(out=x3[:, :, j], in0=a3[:, :, j], in1=A3[:, :, 3], op=ALU.subtract),
                         r=[('G', 0), ('G', 3), ('G', 1)], w=[('G', 1)])
                S.op('act', lambda e: e.activation(out=G2[:, 0:TS], in_=G2[:, 0:TS], func=AF.Exp), r=[('G', 1)], w=[('G', 1)])
                for tt in range(NT):
                    rows = rows_of(tt)
                    r0 = tt * 128
                    srcs = [(G1, 0, ('G', 0), r0), (G3, 4, ('G', 2), r0)]
                    if tt == 16:
                        srcs += [(G2, 8, ('G', 1), r0), (G2, 12, ('G', 1), 0)]
                    for (G, co, gname, c0_) in srcs:
                        S.op('pe', lambda e: e.matmul(psum[:rows, 7, co:co + 4], lhsT=G[0:4, c0_:c0_ + rows], rhs=identf[0:4, 0:4],
                                                      start=True, stop=True), r=[gname, 'identf'], w=[('ps', 7)])
                    ncp = 16 if tt == 16 else 8
                    S.op('act', lambda e: e.copy(out=atm[:rows, tt, 0:ncp], in_=psum[:rows, 7, 0:ncp]), r=[('ps', 7)], w=[('atm', tt)])
                S.op('dve', lambda e: e.tensor_scalar(out=G4[:, :], in0=G4[:, :], scalar1=-1.0, scalar2=None, op0=ALU.mult),
                     r=[('G', 3)], w=[('G', 3)])
                dbg('G1', G1[:, :], [4, T], F32, [('G', 0)])
                dbg('G4', G4[:, :], [4, T], F32, [('G', 3)])
                dbg('G3', G3[:, :], [4, T], F32, [('G', 2)])

            S.freed()
            ktm = [sbm(f"ktm_{i}", [128, NT, 128], BF16) for i in range(2)]
            sig = [sbm(f"sig_{i}", [128, NT, 128], BF16) for i in range(2)]
            dtl = [sbm(f"dt_{i}", [128, 512], F32) for i in range(2)]
            kw = sbm("kw", [128, 128], BF16)
            wsc = sbm("wsc", [128, 16], F32)
            cst = sbm("cst", [128, 129], F32)

            fwb = sbm("fwb", [128, 64], F32)
            for h in range(4):
                S.op('pe', lambda e: e.matmul(psum[:, 7, 0:16], lhsT=selt[0:4, h, :], rhs=mfin[0:4, 32:48], start=True, stop=True),
                     r=['selt', 'mfin2'], w=[('ps', 7)])
                S.op('act', lambda e: e.copy(out=fwb[:, h * 16:(h + 1) * 16], in_=psum[:, 7, 0:16]), r=[('ps', 7)], w=[('fwb', h)])
            C0x = sbm("C0x", [128, 16, 130], BF16)
            snt = sbm("snt", [64, 128], F32)
            n0f = sbm("n0f", [128, 64], F32)
            nnew = sbm("nnew", [128, 64], F32)
            bmask = sbm("bmask_sb", [64, 64], F32)
            rowsel = sbm("rowsel_sb", [64, 16], F32)
            qmask = sbm("qmask", [128, 16, 64], BF16)
            kwm = sbm("kwm", [64, 16, 128], BF16)
            kws = sbm("kws", [64, 128], BF16)
            c0f = [sbm(f"c0f_{i}", [128, 128], F32) for i in range(2)]
            cnew = [sbm(f"cnew_{i}", [128, 129], F32) for i in range(2)]
            dts = sbm("dts", [64, 64], F32)
            pTs = sbm("pTs", [64, 64], BF16)
            ist = sbm("ist", [64, 129], F32)
            sC_v = sC_in.rearrange("bh k v -> k bh v")
            sC_h = sC_in.rearrange("(b h) k v -> k h b v", h=4)
            S.dma('sp', lambda e: e.dma_start(out=snt[:], in_=sn_in[:, :]), w=['snt'])
            S.op('pe', lambda e: e.matmul(psum[:, 7, 0:64], lhsT=snt[:, :], rhs=identf[0:64, 0:64], start=True, stop=True),
                 r=['snt', 'identf'], w=[('ps', 7)])
            S.op('act', lambda e: e.copy(out=n0f[:, :], in_=psum[:, 7, 0:64]), r=[('ps', 7)], w=['n0f'])
            S.dma('sp', lambda e: e.dma_start(out=bmask[:], in_=bmask_d[:, :]), w=['bmask'])
            S.dma('sp', lambda e: e.dma_start(out=rowsel[:], in_=rowsel_d[:, :]), w=['rowsel'])
            S.op('dve', lambda e: e.memset(qmask[:], 0.0), w=['qmask'])

            def mlstm_sample(h, sl):
                qT, kT, V = qk[sl][0], qk[sl][1], vt[sl]
                qn, kn, vn = ('qT', sl), ('kT', sl), ('vt', sl)
                if 'ms_a' not in SKIP:
                    mlstm_sample_a(h, sl)
                mlstm_sample_b(h, sl)

            def mlstm_sample_a(h, sl):
                qT, kT, V = qk[sl][0], qk[sl][1], vt[sl]
                qn, kn, vn = ('qT', sl), ('kT', sl), ('vt', sl)
                S.op('pe', lambda e: e.matmul(psum[0:64, 2, 0:64], lhsT=kT[:, TP:T], rhs=qT[:, TP:T], start=True, stop=True),
                     r=[(kn, 4), (qn, 4)], w=[('ps', 2)])
                S.op('act', lambda e: e.activation(out=dts[:, :], in_=nAb[0:64, TP:T], func=AF.Exp, bias=atm[0:64, 16, h:h + 1]),
                     r=[('nAb', 4), ('atm', 16)], w=['dts'])
                S.op('dve', lambda e: e.tensor_tensor(out=dts[:, :], in0=dts[:, :], in1=bmask[:, :], op=ALU.mult), r=['dts', 'bmask'], w=['dts'])
                S.op('dve', lambda e: e.scalar_tensor_tensor(out=pTs[:, :], in0=psum[0:64, 2, 0:64], scalar=SCL, in1=dts[:, :],
                                                             op0=ALU.mult, op1=ALU.mult), r=[('ps', 2), 'dts'], w=['pTs'])
                S.op('pe', lambda e: e.matmul(psum[0:64, 3, 0:129], lhsT=pTs[:, :], rhs=V[0:64, 16, :], start=True, stop=True),
                     r=['pTs', (vn, 16)], w=[('ps', 3)])
                for b in range(16):
                    S.op('dve', lambda e: e.tensor_copy(out=qmask[:, b, 4 * b:4 * b + 4], in_=qT[:, TP + 4 * b:TP + 4 * b + 4]),
                         r=[(qn, 4), 'qmask'], w=[('qmask', b)])
                for g in range(2):
                    S.dma('pool', lambda e: e.dma_start(out=C0x[:, g * 8:(g + 1) * 8, 0:128], in_=sC_h[:, h, g * 8:(g + 1) * 8, :]),
                          w=[('C0x', g)])
                S.op('act', lambda e: e.copy(out=C0x[:, :, 128], in_=n0f[:, :].rearrange("p (b h) -> p h b", h=4)[:, h, :]),
                     r=['n0f'], w=['C0xn'])
                for b in range(16):
                    S.op('pe', lambda e: e.matmul(psum[0:64, 4, 0:129], lhsT=qmask[:, b, :], rhs=C0x[:, b, 0:129],
                                                  start=(b == 0), stop=(b == 15)),
                         r=[('qmask', b), ('C0x', b // 8), 'C0xn'], w=[('ps', 4)], inc=(b == 15))
                S.op('act', lambda e: e.copy(out=ist[:, :], in_=psum[0:64, 4, 0:129]), r=[('ps', 4)], w=['ist'])
                S.op('dve', lambda e: e.scalar_tensor_tensor(out=cst[0:64, :], in0=ist[:, :], scalar=atm[0:64, 16, 8 + h:9 + h],
                                                             in1=psum[0:64, 3, 0:129], op0=ALU.mult, op1=ALU.add),
                     r=['ist', ('atm', 16), ('ps', 3)], w=['cst'])
                mlstm_finalize(h, sl, 16, 64, cst, ['cst'])

            def mlstm_sample_b(h, sl):
                qT, kT, V = qk[sl][0], qk[sl][1], vt[sl]
                qn, kn, vn = ('qT', sl), ('kT', sl), ('vt', sl)
                if 'ms_b' in SKIP:
                    return
                S.op('dve', lambda e: e.tensor_scalar(out=kws[:, :], in0=ktm[sl][0:64, 16, :], scalar1=atm[0:64, 16, 12 + h:13 + h], scalar2=SCL,
                                                      op0=ALU.mult, op1=ALU.mult), r=[('ktm', sl, 16), ('atm', 16)], w=['kws'])
                for b in range(16):
                    S.op('dve', lambda e: e.tensor_scalar(out=kwm[:, b, :], in0=kws[:, :], scalar1=rowsel[:, b:b + 1], scalar2=None, op0=ALU.mult),
                         r=['kws', 'rowsel'], w=[('kwm', b)])
                for b in range(16):
                    cb = b % 2
                    bank = 5 + cb
                    bh = b * 4 + h
                    S.dma('sp', lambda e: e.dma_start(out=c0f[cb][:], in_=sC_in[bh]), w=[('c0f', cb)])
                    S.op('pe', lambda e: e.matmul(psum[:, bank, 0:129], lhsT=kwm[:, b, :], rhs=V[0:64, 16, :], start=True, stop=True),
                         r=[('kwm', b), (vn, 16)], w=[('ps', bank)])
                    S.op('dve', lambda e: e.scalar_tensor_tensor(out=cnew[cb][:, 0:128], in0=c0f[cb][:, :], scalar=fwb[:, h * 16 + b:h * 16 + b + 1],
                                                                 in1=psum[:, bank, 0:128], op0=ALU.mult, op1=ALU.add),
                         r=[('c0f', cb), ('fwb', h), ('ps', bank)], w=[('cnew', cb)])
                    S.op('dve', lambda e: e.scalar_tensor_tensor(out=nnew[:, bh:bh + 1], in0=n0f[:, bh:bh + 1], scalar=fwb[:, h * 16 + b:h * 16 + b + 1],
                                                                 in1=psum[:, bank, 128:129], op0=ALU.mult, op1=ALU.add),
                         r=['n0f', ('fwb', h), ('ps', bank)], w=[('nnew', bh)])
                    S.dma('sp', lambda e: e.dma_start(out=cs_out[bh], in_=cnew[cb][:, 0:128]), r=[('cnew', cb)], w=[('cs_out', bh)], is_out=True)


            def mlstm_head(h, it):
                sl = it % 2
                for j, col0 in enumerate([1536 + h * 128, 2048 + h * 128, 2560 + h * 128, 3072 + h * 128]):
                    S.dma('pool', lambda e: e.dma_start(out=wt[sl][:, :, j * 128:(j + 1) * 128], in_=win_d[:, :, col0:col0 + 128]),
                          w=[('wt', sl, j)])
                qT, kT, V = qk[sl][0], qk[sl][1], vt[sl]
                qn, kn, vn = ('qT', sl), ('kT', sl), ('vt', sl)
                for blk in range(5):
                    cc, n = blk_cols(blk)
                    S.op('pe', lambda e: e.matmul(psum[:, 7, 0:n], lhsT=selt[0:4, h, :], rhs=G4[0:4, cc:cc + n], start=True, stop=True),
                         r=['selt', ('G', 3)], w=[('ps', 7)])
                    S.op('act', lambda e: e.copy(out=nAb[:, cc:cc + n], in_=psum[:, 7, 0:n]), r=[('ps', 7)], w=[('nAb', blk)])
                proj_fm(qT, qn, wt[sl], ('wt', sl, 0), 0)
                proj_fm(kT, kn, wt[sl], ('wt', sl, 1), 128)

                def cons_k(tt, rows, ps_ap, pbk, eng):
                    evac(eng, ktm[sl][:rows, tt, :], ps_ap, [('ps', pbk)], [('ktm', sl, tt)])
                proj_tm(cons_k, wt[sl], ('wt', sl, 1), 128, 128)

                def cons_v(tt, rows, ps_ap, pbk, eng):
                    evac(eng, V[:rows, tt, 0:128], ps_ap, [('ps', pbk), vn], [(vn, tt)])
                proj_tm(cons_v, wt[sl], ('wt', sl, 2), 256, 128)

                def cons_o(tt, rows, ps_ap, pbk, eng):
                    S.op('act', lambda e: e.activation(out=sig[sl][:rows, tt, :], in_=ps_ap, func=AF.Sigmoid),
                         r=[('ps', pbk)], w=[('sig', sl, tt)])
                proj_tm(cons_o, wt[sl], ('wt', sl, 3), 384, 128)
                qbufs = lambda q0, n: [(qn, b) for b in range(q0 // 512, (q0 + n - 1) // 512 + 1)]
                nbufs = lambda q0, n: [('nAb', b) for b in range(q0 // 512, (q0 + n - 1) // 512 + 1)]
                iters = [(qb, kt) for qb in range(4) for kt in range(4 * qb + 4)]

                def geom(i):
                    qb, kt = iters[i]
                    qt0 = max(kt, 4 * qb)
                    return qb, kt, qt0, (4 * qb + 4 - qt0) * 128, qt0 * 128, 2 + i % 4, i % 2

                def emit_qk(i):
                    qb, kt, qt0, ncol, q0, bank, db = geom(i)
                    S.op('pe', lambda e: e.matmul(psum[:, bank, 0:ncol], lhsT=kT[:, kt * 128:(kt + 1) * 128],
                                                  rhs=qT[:, q0:q0 + ncol], start=True, stop=True),
                         r=[(kn, kt // 4)] + qbufs(q0, ncol), w=[('ps', bank)])
                    S.op('act', lambda e: e.activation(out=dtl[db][:, 0:ncol], in_=nAb[:, q0:q0 + ncol], func=AF.Exp,
                                                       bias=atm[:, kt, h:h + 1]),
                         r=nbufs(q0, ncol) + [('atm', kt)], w=[('dt', db)])
                    if qt0 == kt:
                        S.op('dve', lambda e: e.tensor_tensor(out=dtl[db][:, 0:128], in0=dtl[db][:, 0:128], in1=trim[:, :], op=ALU.mult),
                             r=[('dt', db), 'trim'], w=[('dt', db)])
                    S.op('dve', lambda e: e.scalar_tensor_tensor(out=pT[db][0][:, 0:ncol], in0=psum[:, bank, 0:ncol], scalar=SCL,
                                                                 in1=dtl[db][:, 0:ncol], op0=ALU.mult, op1=ALU.mult),
                         r=[('ps', bank), ('dt', db)], w=[('pT', db, 0)])

                def emit_av(i):
                    qb, kt, qt0, ncol, q0, bank, db = geom(i)
                    for qt in range(qt0, 4 * qb + 4):
                        jq = qt - 4 * qb
                        oap, obuf = ops(jq)
                        off = (qt - qt0) * 128
                        S.op('pe', lambda e: e.matmul(oap, lhsT=pT[db][0][:, off:off + 128], rhs=V[:, kt, :],
                                                      start=(kt == 0 and jq % 3 == 0), stop=(kt == qt), skip_group_check=True),
                             r=[('pT', db, 0), (vn, kt)], w=[obuf])
                    if kt == 4 * qb + 3:
                        S.op('dve', lambda e: e.tensor_copy(out=osbf[:, 0:387], in_=psum[:, 0, 0:387]), r=[('ps', 0)], w=['osb8a'])
                        S.op('dve', lambda e: e.tensor_copy(out=osbf[:, 387:516], in_=psum[:, 1, 0:129]), r=[('ps', 1)], w=['osb8b'])
                        ob = ['osb8a', 'osb8b']
                        t0_ = 4 * qb
                        S.op('dve', lambda e: e.tensor_scalar(out=fs4[:, 0:4], in0=osb8[:, 0:4, 128], scalar1=-1.0, scalar2=None, op0=ALU.mult),
                             r=ob, w=['fs4'])
                        S.op('dve', lambda e: e.tensor_tensor(out=fs4[:, 0:4], in0=fs4[:, 0:4], in1=osb8[:, 0:4, 128], op=ALU.max),
                             r=ob + ['fs4'], w=['fs4'])
                        S.op('dve', lambda e: e.tensor_tensor(out=fs4[:, 0:4], in0=fs4[:, 0:4], in1=atm[:, t0_:t0_ + 4, 4 + h], op=ALU.max),
                             r=['fs4'] + [('atm', t0_ + j) for j in range(4)], w=['fs4'])
                        S.op('dve', lambda e: e.reciprocal(out=fs4[:, 0:4], in_=fs4[:, 0:4]), r=['fs4'], w=['fs4'])
                        S.op('dve', lambda e: e.tensor_tensor(out=fin4[:, :, :], in0=osb8[:, 0:4, 0:128], in1=BC4(fs4[:, 0:4]), op=ALU.mult),
                             r=ob + ['fs4'], w=['fin'])
                        block_norm(fin4, mixed[:, t0_:t0_ + 4, 512 + h * 128:512 + (h + 1) * 128], mnb, sig[sl][:, t0_:t0_ + 4, :], 1.0,
                                   [('mixed', t0_ + j, 4 + h) for j in range(4)], extra_r=[('sig', sl, t0_ + j) for j in range(4)])

                if 'mloop' not in SKIP:
                    for i in range(len(iters) + 1):
                        if i < len(iters):
                            emit_qk(i)
                        if i >= 1:
                            emit_av(i - 1)
                S.op('act', lambda e: e.activation(out=wsc[:, 0:16], in_=atm[:, 0:16, h], func=AF.Exp, bias=nAb[:, TP - 1:TP]),
                     r=[('atm', tt) for tt in range(16)] + [('nAb', 3)], w=['wsc'])
                for kt in range(16 if 'mstate' not in SKIP else 0):
                    S.op('dve', lambda e: e.tensor_scalar(out=kw[:, :], in0=ktm[sl][:, kt, :], scalar1=wsc[:, kt:kt + 1], scalar2=SCL,
                                                          op0=ALU.mult, op1=ALU.mult),
                         r=[('ktm', sl, kt), 'wsc'], w=['kw'])
                    S.op('pe', lambda e: e.matmul(psum[:, 6, 0:129], lhsT=kw[:, :], rhs=V[:, kt, :], start=(kt == 0), stop=(kt == 15)),
                         r=['kw', (vn, kt)], w=[('ps', 6)])
                S.op('act', lambda e: e.copy(out=cst[:, :], in_=psum[:, 6, 0:129]), r=[('ps', 6)], w=['cst'])
                S.dma('sp', lambda e: e.dma_start(out=cp_out[h], in_=cst[:, 0:128]), r=['cst'], w=[('cp_out', h)], is_out=True)
                S.dma('sp', lambda e: e.dma_start(out=np_out[h:h + 1, :].rearrange("o d -> d o"), in_=cst[:, 128:129],
                                                  allow_slow_non_contiguous=True), r=['cst'], w=[('np_out', h)], is_out=True)
                if 'msample' not in SKIP:
                    mlstm_sample(h, sl)

            def mlstm_finalize(h, sl, tt, rows, oap, obufs):
                S.op('dve', lambda e: e.tensor_scalar(out=fsm[:rows, 5:6], in0=oap[:rows, 128:129], scalar1=-1.0, scalar2=None, op0=ALU.mult),
                     r=obufs, w=['fsm5'])
                S.op('dve', lambda e: e.tensor_tensor(out=fsm[:rows, 4:5], in0=oap[:rows, 128:129], in1=fsm[:rows, 5:6], op=ALU.max),
                     r=obufs + ['fsm5'], w=['fsm4'])
                S.op('dve', lambda e: e.tensor_tensor(out=fsm[:rows, 4:5], in0=fsm[:rows, 4:5], in1=atm[:rows, tt, 4 + h:5 + h], op=ALU.max),
                     r=['fsm4', ('atm', tt)], w=['fsm4'])
                S.op('dve', lambda e: e.reciprocal(out=fsm[:rows, 4:5], in_=fsm[:rows, 4:5]), r=['fsm4'], w=['fsm4'])
                S.op('dve', lambda e: e.tensor_scalar(out=fin[:rows, :], in0=oap[:rows, 0:128], scalar1=fsm[:rows, 4:5], scalar2=None, op0=ALU.mult),
                     r=obufs + ['fsm4'], w=['fin'])
                sumsq(fsm[:rows, 2:3], fin[:rows, :], rows, 128, ['fin'], 'fsm2')
                rstd_act(fsm[:rows, 3:4], fsm[:rows, 2:3], 128, ['fsm2'], 'fsm3')
                S.op('dve', lambda e: e.scalar_tensor_tensor(out=fin[:rows, :], in0=fin[:rows, :], scalar=fsm[:rows, 3:4], in1=mnb[:rows, :],
                                                             op0=ALU.mult, op1=ALU.mult), r=['fin', 'fsm3', 'mnb'], w=['fin'])
                S.op('dve', lambda e: e.tensor_tensor(out=mixed[:rows, tt, 512 + h * 128:512 + (h + 1) * 128], in0=fin[:rows, :],
                                                      in1=sig[sl][:rows, tt, :], op=ALU.mult),
                     r=['fin', ('sig', sl, tt)], w=[('mixed', tt, 4 + h)])

            for h in range(4):
                if 'mheads' not in SKIP:
                    mlstm_head(h, h)
            if 'mheads' not in SKIP and 'msample' not in SKIP and 'ms_b' not in SKIP:
                S.op('pe', lambda e: e.matmul(psum[0:64, 7, 0:128], lhsT=nnew[:, :], rhs=identf[:, :], start=True, stop=True),
                     r=[('nnew', bh) for bh in range(64)] + ['identf'], w=[('ps', 7)])
                S.op('act', lambda e: e.copy(out=snt[:, :], in_=psum[0:64, 7, 0:128]), r=[('ps', 7)], w=['snt'])
                S.dma('sp', lambda e: e.dma_start(out=ns_out[:, :], in_=snt[:, :]), r=['snt'], w=['ns_out'], is_out=True)
            eh.close()
            S.freed()
            if stage > 4:
                sample_attn(hnT, qblk, ks_s, vs_s, tz, lsm, anb)
                S.freed()
            dbg('mixed', mixed[:, :, :], [128, NT, D], BF16, [('mixed', tt, h) for tt in range(16) for h in range(8)])
            dbg('hnT', hnT[:, :, :], [128, KD, T], BF16, [('hT', 16)])
            if stage > 3:
                out_proj(em, hnT, mixed, wt)

        def sample_attn(hnT, qblk, ks_s, vs_s, tz, lsm, anb):
            nlam = lsm[:, 5:6]
            with ExitStack() as ea:
                def sba(name, shape, dt):
                    return ea.enter_context(nc.sbuf_tensor(name, list(shape), dt))
                ptb = sba("ptb", [128, 256], I32)
                iot = sba("iot", [128, 256], I32)
                idx = sba("idx", [128, 256], I32)
                identf = sba("identf_sa", [128, 128], F32)
                rmod = sba("rmod_sb", [4, 64], F32)
                rowsel = sba("rowsel_sa", [64, 16], F32)
                tzN = sba("tzN", [4, 32], F32)
                tmpN = sba("tmpN", [64, 32], F32)
                corrN = sba("corrN", [64, 16, 32], F32)
                corr15 = sba("corr15", [128, 32], F32)
                KVb = [sba(f"KVb_{i}", [128, 16, 1024], BF16) for i in range(2)]
                Kb = [KVb[i][:, :, 0:512] for i in range(2)]
                Vb = [KVb[i][:, :, 512:1024] for i in range(2)]
                onesb = sba("onesb", [128, 2], BF16)
                sel01 = sba("sel01_sb", [32, 32], F32)
                dsel = sba("dsel", [32, 16], F32)
                bm16 = sba("bm16_sb", [16, 512], F32)
                anb4 = sba("anb4", [16, 512], F32)
                rs32 = sba("rs32", [32, 2], F32)
                on32 = sba("on32", [32, 512], F32)
                a16 = sba("a16", [16, 512], F32)
                rs16 = sba("rs16", [16, 4], F32)
                kTb = sba("kTb", [128, 4, 2048], BF16)
                pTb = sba("pTb", [128, 512], BF16)
                pTn = sba("pTn", [64, 32], BF16)
                S.dma('sp', lambda e: e.dma_start(out=ptb[:], in_=pt_in[0:1, :].partition_broadcast(128)), w=['ptb'])
                S.dma('sp', lambda e: e.dma_start(out=identf[:], in_=identf_d[:, :]), w=['identf'])
                S.dma('sp', lambda e: e.dma_start(out=rmod[:], in_=rmod_d[:, :]), w=['rmod'])
                S.dma('sp', lambda e: e.dma_start(out=rowsel[:], in_=rowsel_d[:, :]), w=['rowsel'])
                S.op('pool', lambda e: e.iota(iot[:], pattern=[[0, 256]], base=0, channel_multiplier=1), w=['iot'])
                S.op('dve', lambda e: e.scalar_tensor_tensor(out=idx[:], in0=ptb[:], scalar=128, in1=iot[:], op0=ALU.mult, op1=ALU.add),
                     r=['ptb', 'iot'], w=['idx'])
                S.op('dve', lambda e: e.memset(onesb[:], 1.0), w=['onesb'])
                S.dma('sp', lambda e: e.dma_start(out=sel01[:], in_=sel01_d[:, :]), w=['sel01'])
                S.dma('sp', lambda e: e.dma_start(out=bm16[:], in_=bm16_d[:, :]), w=['bm16'])
                for h in range(4):
                    S.dma('sp', lambda e: e.dma_start(out=anb4[:, h * 128:(h + 1) * 128], in_=attn_norm[0:1, :].partition_broadcast(16)),
                          w=['anb4'])
                S.op('dve', lambda e: e.scalar_tensor_tensor(out=dsel[:, :], in0=sel01[:, 16:32], scalar=nlam[0:32, :], in1=sel01[:, 0:16],
                                                             op0=ALU.mult, op1=ALU.add), r=['sel01', 'lsm'], w=['dsel'])
                for h in range(4):
                    for c in range(2):
                        S.op('dve', lambda e: e.tensor_copy(out=corr15[:, h * 8 + c * 4:h * 8 + c * 4 + 4], in_=tz[:, h, 128:132]),
                             r=['tz'], w=[('corr15', h, c)])
                        S.op('dve', lambda e: e.tensor_copy(out=tzN[:, h * 8 + c * 4:h * 8 + c * 4 + 4], in_=tz[0:4, h, 0:4]),
                             r=['tz'], w=[('tzN', h, c)])
                S.op('pe', lambda e: e.matmul(psum[0:64, 7, 0:32], lhsT=rmod[:, :], rhs=tzN[:, :], start=True, stop=True),
                     r=['rmod'] + [('tzN', h, c) for h in range(4) for c in range(2)], w=[('ps', 7)])
                S.op('act', lambda e: e.copy(out=tmpN[:, :], in_=psum[0:64, 7, 0:32]), r=[('ps', 7)], w=['tmpN'])
                for b in range(16):
                    S.op('dve', lambda e: e.tensor_scalar(out=corrN[:, b, :], in0=tmpN[:, :], scalar1=rowsel[:, b:b + 1], scalar2=None, op0=ALU.mult),
                         r=['tmpN', 'rowsel'], w=[('corrN', b)])
                c15 = [('corr15', h, c) for h in range(4) for c in range(2)]

                def gather(b):
                    sl = b % 2
                    for pg in range(16):
                        col = b * 16 + pg
                        S.dma('pool', lambda e: e.indirect_dma_start(out=KVb[sl][:, pg, :], out_offset=None, in_=kv_in[:, :],
                                                                      in_offset=bass.IndirectOffsetOnAxis(ap=idx[:, col:col + 1], axis=0)),
                              r=['idx'], w=[('Kb', sl, pg), ('Vb', sl, pg)])

                def stage_a(b):
                    sl = b % 2
                    tb = 0
                    for h in range(4):
                        for half in range(2):
                            bank = tb % 2
                            tb += 1
                            pv = psb16(bank)
                            for p8 in range(8):
                                pg = half * 8 + p8
                                S.op('pe', lambda e: e.transpose(out=pv[:, p8 * 128:(p8 + 1) * 128], in_=Kb[sl][:, pg, h * 128:(h + 1) * 128],
                                                                 identity=identb[:, :]),
                                     r=[('Kb', sl, pg), 'identb'], w=[('ps', bank)], inc=(p8 == 7))
                            S.op('act', lambda e: e.copy(out=kTb[:, h, half * 1024:(half + 1) * 1024], in_=pv[:, 0:1024]),
                                 r=[('ps', bank)], w=[('kTb', h, half)])
                    for h in range(4):
                        for pg in range(16):
                            S.op('pe', lambda e: e.matmul(psum[:, 2, pg * 32 + h * 8:pg * 32 + h * 8 + 8], lhsT=kTb[:, h, pg * 128:(pg + 1) * 128],
                                                          rhs=qblk[:, b, h, :], start=True, stop=True, skip_group_check=True),
                                 r=[('kTb', h, pg // 8), ('qblk', h, 0), ('qblk', h, 1)], w=[('ps', 2)], inc=(h == 3 and pg == 15))
                    for h in range(4):
                        S.op('pe', lambda e: e.matmul(psum[0:64, 3, h * 8:h * 8 + 8], lhsT=ks_s[:, h, :], rhs=qblk[:, b, h, :],
                                                      start=True, stop=True, skip_group_check=True),
                             r=[('ks_s', h), ('qblk', h, 0), ('qblk', h, 1)], w=[('ps', 3)], inc=(h == 3))
                    S.op('act', lambda e: e.activation(out=pTb[:, :], in_=psum[:, 2, :], func=AF.Exp, scale=0.125), r=[('ps', 2)], w=['pTb'])
                    S.op('dve', lambda e: e.tensor_tensor(out=pTb[:, 480:512], in0=pTb[:, 480:512], in1=corr15[:, :], op=ALU.mult),
                         r=['pTb'] + c15, w=['pTb'])
                    S.op('act', lambda e: e.activation(out=pTn[:, :], in_=psum[0:64, 3, 0:32], func=AF.Exp, scale=0.125), r=[('ps', 3)], w=['pTn'])
                    S.op('dve', lambda e: e.tensor_tensor(out=pTn[:, :], in0=pTn[:, :], in1=corrN[:, b, :], op=ALU.mult),
                         r=['pTn', ('corrN', b)], w=['pTn'])
                    avb = 4 + b % 2
                    sc = 64 + b % 2
                    for pg in range(16):
                        S.op('pe', lambda e: e.matmul(psum[0:32, avb, :], lhsT=pTb[:, pg * 32:(pg + 1) * 32], rhs=Vb[sl][:, pg, :],
                                                      start=(pg == 0), stop=False), r=['pTb', ('Vb', sl, pg)], w=[('ps', avb)], inc=False)
                    S.op('pe', lambda e: e.matmul(psum[0:32, avb, :], lhsT=pTn[:, :], rhs=vs_s[:, :, :].rearrange("p h d -> p (h d)"),
                                                  start=False, stop=True), r=['pTn'] + [('vs_s', h) for h in range(4)], w=[('ps', avb)])
                    for pg in range(16):
                        S.op('pe', lambda e: e.matmul(psum[0:32, 3, sc:sc + 1], lhsT=pTb[:, pg * 32:(pg + 1) * 32], rhs=onesb[:, 0:1],
                                                      start=(pg == 0), stop=False, skip_group_check=True),
                             r=['pTb', 'onesb', 'pTn'], w=[('sums', b % 2)], inc=False)
                    S.op('pe', lambda e: e.matmul(psum[0:32, 3, sc:sc + 1], lhsT=pTn[:, :], rhs=onesb[0:64, 0:1], start=False, stop=True,
                                                  skip_group_check=True),
                         r=['pTn', 'onesb'], w=[('sums', b % 2)])

                def stage_b(b):
                    avb = 4 + b % 2
                    sc = 64 + b % 2
                    S.op('dve', lambda e: e.reciprocal(out=rs32[:, 0:1], in_=psum[0:32, 3, sc:sc + 1]), r=[('sums', b % 2)], w=['rs32'])
                    S.op('dve', lambda e: e.tensor_scalar(out=on32[:, :], in0=psum[0:32, avb, :], scalar1=rs32[:, 0:1], scalar2=None, op0=ALU.mult),
                         r=[('ps', avb), 'rs32'], w=['on32'])
                    S.op('pe', lambda e: e.matmul(psum[0:16, 6, :], lhsT=dsel[:, :], rhs=on32[:, :], start=True, stop=True),
                         r=['dsel', 'on32'], w=[('ps', 6)])
                    S.op('dve', lambda e: e.tensor_tensor(out=a16[:, :], in0=psum[0:16, 6, :], in1=bm16[:, :], op=ALU.mult),
                         r=[('ps', 6), 'bm16'], w=['a16'])
                    sumsq(rs16[:, 0:1], a16[:, :], 16, 512, ['a16'], 'rs16a')
                    rstd_act(rs16[:, 1:2], rs16[:, 0:1], 128, ['rs16a'], 'rs16b', post=1.0 - LAM_INIT)
                    S.op('dve', lambda e: e.scalar_tensor_tensor(out=a16[:, :], in0=a16[:, :], scalar=rs16[:, 1:2], in1=anb4[:, :],
                                                                 op0=ALU.mult, op1=ALU.mult), r=['a16', 'rs16b', 'anb4'], w=['a16'])
                    for h in range(4):
                        S.op('pe', lambda e: e.matmul(psum[:, 7, h * 16:(h + 1) * 16], lhsT=a16[:, h * 128:(h + 1) * 128], rhs=identf[0:16, 0:16],
                                                      start=True, stop=True, skip_group_check=True),
                             r=['a16', 'identf'], w=[('ps', 7)], inc=(h == 3))
                    S.op('act', lambda e: e.copy(out=hnT[:, 0:4, TP + 4 * b:TP + 4 * b + 4],
                                                 in_=psum[:, 7, 0:80].rearrange("p (h x) -> p h x", x=20)[:, :, 0:4]),
                         r=[('ps', 7)], w=[('hTs', b)])

                gather(0)
                gather(1)
                stage_a(0)
                for b in range(16):
                    if b + 1 < 16:
                        if b + 2 < 16:
                            gather(b + 2)
                        stage_a(b + 1)
                    stage_b(b)

        def out_proj(em, hnT, mixed, wt):
            xr = [em.enter_context(nc.sbuf_tensor(f"xro_{i}", [128, D], F32)) for i in range(2)]
            tmp1 = em.enter_context(nc.sbuf_tensor("tmpo", [128, D], F32))
            wo_d = w_out.rearrange("(kc p) f -> p kc f", p=128)
            for half in range(2):
                S.dma('pool', lambda e: e.dma_start(out=wt[half][:], in_=wo_d[:, :, half * 512:(half + 1) * 512]),
                      w=[('wo', half)] + [('wt', half, j) for j in range(4)])
            for tt in range(NT):
                rows = rows_of(tt)
                r0 = tt * 128
                bank = tt % 2
                pv = psb16(bank)
                c_lo = 4 if (tt == 16 and stage > 4) else 0
                for c in range(c_lo, KD):
                    S.op('pe', lambda e, c=c: e.transpose(out=pv[:, c * 128:c * 128 + rows], in_=mixed[:rows, tt, c * 128:(c + 1) * 128],
                                                          identity=identb[:rows, :rows]),
                         r=[('mixed', tt, c), 'identb'], w=[('ps', bank)], inc=(c == KD - 1))
                srcv = pv[:, 0:1024].rearrange("p (c r) -> p c r", r=128)[:, c_lo:KD, 0:rows]
                S.op('act', lambda e: e.copy(out=hnT[:, c_lo:KD, r0:r0 + rows], in_=srcv),
                     r=[('ps', bank)] + ([('hTs', b) for b in range(16)] if c_lo else []), w=[('hT', tt)])
            for tt in range(NT):
                rows = rows_of(tt)
                r0 = tt * 128
                yb = tt % 2
                S.dma('sp', lambda e: e.dma_start(out=xr[yb][:rows, :], in_=x1_s[r0:r0 + rows, :]), r=[('dram', id(x1_s), tt)], w=[('xro', yb)])
                for half in range(2):
                    bank = 4 + 2 * yb + half
                    for kc in range(KD):
                        S.op('pe', lambda e, kc=kc: e.matmul(psum[:rows, bank, :], lhsT=hnT[:, kc, r0:r0 + rows], rhs=wt[half][:, kc, :],
                                                             start=(kc == 0), stop=(kc == KD - 1)),
                             r=[('hT', tt), ('wo', half)], w=[('ps', bank)], inc=(kc == KD - 1))
                b0 = 4 + 2 * yb
                S.op('act', lambda e: e.copy(out=tmp1[:rows, :].rearrange("p (a b) -> p a b", a=2), in_=psum[:rows, b0:b0 + 2, :]),
                     r=[('ps', b0), ('ps', b0 + 1)], w=['tmpo'])
                sumsq(ss[:rows, tt:tt + 1], tmp1[:rows, :], rows, D, ['tmpo'], ('ssc', tt))
                rstd_act(rstd[:rows, tt:tt + 1], ss[:rows, tt:tt + 1], D, [('ssc', tt)], ('rstd', tt))
                S.op('dve', lambda e: e.scalar_tensor_tensor(out=tmp1[:rows, :], in0=tmp1[:rows, :], scalar=rstd[:rows, tt:tt + 1],
                                                             in1=gbc[:rows, 1, :], op0=ALU.mult, op1=ALU.mult),
                     r=['tmpo', ('rstd', tt), ('gbc', 1)], w=['tmpo'])
                S.op('dve', lambda e: e.tensor_tensor(out=xr[yb][:rows, :], in0=tmp1[:rows, :], in1=xr[yb][:rows, :], op=ALU.add),
                     r=['tmpo', ('xro', yb)], w=[('xro', yb)])
                S.dma('sp', lambda e: e.dma_start(out=x2_s[r0:r0 + rows, :], in_=xr[yb][:rows, :]),
                      r=[('xro', yb)], w=[('dram', id(x2_s), tt)])

        if stage > 1:
            token_mix()
        if stage > 3:
            with ExitStack() as es2:
                ffn(x2_s, y_out, 1, 4, 5, True)
            S.freed()
        S.finish()
    return nc


_CONST = {}


def consts():
    if not _CONST:
        import ml_dtypes
        _CONST['identb'] = np.eye(128, dtype=np.float32).astype(ml_dtypes.bfloat16)
        _CONST['identf'] = np.eye(128, dtype=np.float32)
        oh = np.zeros((33, 384), np.float32)
        for r in range(384):
            rel = r - 127
            if rel < 0:
                oh[32, r] = 1.0
            else:
                if rel < 16:
                    b = rel
                else:
                    b = 16 + int(np.float32(np.log(np.float32(max(rel, 16)) / np.float32(16))) / np.float32(np.log(8.0)) * np.float32(16))
                    b = min(b, 31)
                oh[b, r] = 1.0
        _CONST['onehot'] = oh
        sel = np.zeros((4, 4, 128), np.float32)
        for h in range(4):
            sel[h, h, :] = 1.0
        _CONST['sel'] = sel
        _CONST['trim'] = np.triu(np.ones((128, 128), np.float32))
        bm = np.zeros((64, 64), np.float32)
        rs = np.zeros((64, 16), np.float32)
        for p in range(64):
            rs[p, p // 4] = 1.0
            for t in range(64):
                if p // 4 == t // 4 and p <= t:
                    bm[p, t] = 1.0
        _CONST['bmask'] = bm
        _CONST['rowsel'] = rs
        rm = np.zeros((4, 64), np.float32)
        for p in range(64):
            rm[p % 4, p] = 1.0
        _CONST['rmod'] = rm
        s01 = np.zeros((32, 32), np.float32)
        b16 = np.zeros((16, 512), np.float32)
        for h in range(4):
            for q in range(4):
                s01[h * 8 + q, h * 4 + q] = 1.0
                s01[h * 8 + 4 + q, 16 + h * 4 + q] = 1.0
                b16[h * 4 + q, h * 128:(h + 1) * 128] = 1.0
        _CONST['sel01'] = s01
        _CONST['bm16'] = b16
        _CONST['tri'] = np.ascontiguousarray(np.eye(128, dtype=np.float32)[::-1])
    return _CONST


def make_in_maps(inputs, cores):
    c = consts()
    maps = []
    if 'cache_k' in inputs and '_cache_kv' not in inputs:
        inputs = dict(inputs)
        inputs['_cache_kv'] = np.concatenate([inputs['cache_k'].reshape(2560 * 128, 512),
                                              inputs['cache_v'].reshape(2560 * 128, 512)], axis=1)
    xp = inputs['x_prompt']
    xs = inputs['x_sample'].reshape(128 * 4, D)
    for k in cores:
        m = {
            'xin': np.ascontiguousarray(np.concatenate([xp[k], xs[k * 64:(k + 1) * 64]], axis=0)),
            'gains': np.ascontiguousarray(inputs['norm_gains'][0]),
            'wgate': np.ascontiguousarray(inputs['ffn_w_gate'][0]),
            'wup': np.ascontiguousarray(inputs['ffn_w_up'][0]),
            'wdown': np.ascontiguousarray(inputs['ffn_w_down'][0]),
            'identb': c['identb'],
            'w_in': np.ascontiguousarray(inputs['w_in'][0]),
            'rel_bias': np.ascontiguousarray(inputs['rel_bias']),
            'b_gates': np.ascontiguousarray(inputs['b_gates'][0]),
            'lam_q': np.ascontiguousarray(inputs['lam_q'][0].reshape(1, 128)),
            'lam_k': np.ascontiguousarray(inputs['lam_k'][0].reshape(1, 128)),
            'attn_norm': np.ascontiguousarray(inputs['attn_norm']),
            'mlstm_norm': np.ascontiguousarray(inputs['mlstm_norm']),
            'w_out': np.ascontiguousarray(inputs['w_out'][0]),
            'onehot': c['onehot'], 'identf': c['identf'], 'tri': c['tri'], 'sel': c['sel'], 'trim': c['trim'],
            'sm': np.ascontiguousarray(inputs['state_m'][0, k * 16:(k + 1) * 16]),
            'sC': np.ascontiguousarray(inputs['state_C'][0, k * 16:(k + 1) * 16].reshape(64, 128, 128)),
            'sn': np.ascontiguousarray(inputs['state_n'][0, k * 16:(k + 1) * 16].reshape(64, 128)),
            'bmask': c['bmask'], 'rowsel': c['rowsel'],
            'rmod': c['rmod'], 'sel01': c['sel01'], 'bm16': c['bm16'],
            **({'cache_kv': inputs['_cache_kv']} if '_cache_kv' in inputs else {}),
            'pt': np.ascontiguousarray(inputs['page_table'][k * 16:(k + 1) * 16].reshape(1, 256)).astype(np.int32),
        }
        maps.append(m)
    return maps


def kernel(**inputs):
    inputs = {k: np.asarray(v) for k, v in inputs.items()}
    nc = build_nc()
    maps = make_in_maps(inputs, list(range(NCORES)))
    res = run_bass_kernel_spmd(nc, maps, core_ids=list(range(NCORES)))
    R = res.results
    f32 = np.float32
    y = np.stack([r['y_out'] for r in R])
    y_prompt = np.ascontiguousarray(y[:, :TP, :]).astype(f32)
    y_sample = np.ascontiguousarray(y[:, TP:, :]).reshape(128, 4, D).astype(f32)
    ko = np.stack([r['k_out'] for r in R]); vo = np.stack([r['v_out'] for r in R])
    k_prompt = ko[:, :TP].reshape(1, 8, TP, 4, 128).astype(f32)
    v_prompt = vo[:, :TP].reshape(1, 8, TP, 4, 128).astype(f32)
    k_sample = ko[:, TP:].reshape(1, 128, 4, 4, 128).astype(f32)
    v_sample = vo[:, TP:].reshape(1, 128, 4, 4, 128).astype(f32)

    def get(name, shape):
        if name in R[0]:
            return np.stack([r[name] for r in R]).reshape(shape).astype(f32)
        return np.zeros(shape, f32)
    C_prompt = get('cp_out', (1, 8, 4, 128, 128))
    n_prompt = get('np_out', (1, 8, 4, 128))
    m_prompt = get('mp_out', (1, 8, 4))
    C_sample = get('cs_out', (1, 128, 4, 128, 128))
    n_sample = get('ns_out', (1, 128, 4, 128))
    m_sample = get('ms_out', (1, 128, 4))
    return (y_prompt, y_sample, k_prompt, v_prompt, C_prompt, n_prompt, m_prompt,
            k_sample, v_sample, C_sample, n_sample, m_sample)
```
